# Optimizing a Trainium2 kernel written in Bass

```python
import math
import jax
import jax.numpy as jnp
from jax import lax
import numpy as np

D_MODEL = 1024
BATCH = 4
SEQ = 4096
DEPTH = 2
DEC_BATCH = 32
DEC_SEQ = 1
PAST_LEN = 8192
PAGE_SIZE = 128

HEAD_DIM = 64
NSA_HEADS = 8
NSA_KV = 2
NSA_REP = NSA_HEADS // NSA_KV
NSA_WIDTH = NSA_HEADS * HEAD_DIM
NSA_KV_WIDTH = NSA_KV * HEAD_DIM
CMP_LEN = 32
CMP_STRIDE = 16
SLC_BLOCK = 64
SLC_TOPK = 16
WINDOW = 512
MOBA_HEADS = 8
MOBA_WIDTH = MOBA_HEADS * HEAD_DIM
MOBA_BLOCK = 256
MOBA_TOPK = 3
N_BUCKETS = 32
MAX_EXACT = N_BUCKETS // 2
MAX_DISTANCE = 128
NSA_QBLOCK = 32
MOBA_QBLOCK = 16
LN_EPS = 1e-5
DEEPNORM_ALPHA = (2 * DEPTH) ** 0.25
DEEPNORM_BETA = (8 * DEPTH) ** -0.25
IN_WIDTHS = (NSA_WIDTH, NSA_KV_WIDTH, NSA_KV_WIDTH, NSA_KV_WIDTH, NSA_KV_WIDTH, NSA_KV_WIDTH, NSA_KV_WIDTH,
             3 * NSA_HEADS, NSA_WIDTH, MOBA_WIDTH, MOBA_WIDTH, MOBA_WIDTH, MOBA_WIDTH, D_MODEL, D_MODEL)

kernel_name = 'nsa_moba_gated_hybrid_step'


def layer_norm(x, g, b):
    xf = x.astype(jnp.float32)
    mu = jnp.mean(xf, axis=-1, keepdims=True)
    var = jnp.mean(jnp.square(xf - mu), axis=-1, keepdims=True)
    y = (xf - mu) * lax.rsqrt(var + LN_EPS)
    return (y * g.astype(jnp.float32) + b.astype(jnp.float32)).astype(x.dtype)


def masked_softmax(s, valid):
    s = jnp.where(valid, s.astype(jnp.float32), -jnp.inf)
    m = jnp.max(s, axis=-1, keepdims=True)
    m = jnp.where(jnp.isfinite(m), m, 0.0)
    e = jnp.where(valid, jnp.exp(s - m), 0.0)
    return e / jnp.maximum(jnp.sum(e, axis=-1, keepdims=True), 1e-30)


def t5_bucket(dist):
    n = jnp.maximum(dist, 0)
    nf = jnp.maximum(n, 1).astype(jnp.float32)
    large = MAX_EXACT + (jnp.log(nf / MAX_EXACT) / math.log(MAX_DISTANCE / MAX_EXACT)
                         * (N_BUCKETS - MAX_EXACT)).astype(jnp.int32)
    return jnp.where(n < MAX_EXACT, n, jnp.minimum(large, N_BUCKETS - 1))


def map_query_blocks(fn, qblock, q_pos, *xs):
    t = q_pos.shape[0]
    qb = min(qblock, t)
    n_blk = t // qb
    chunks = tuple(jnp.moveaxis(x.reshape((x.shape[0], n_blk, qb) + x.shape[2:]), 1, 0) for x in xs)
    out = lax.map(fn, chunks + (q_pos.reshape(n_blk, qb),))
    out = jnp.moveaxis(out, 0, 1)
    return out.reshape((out.shape[0], t) + out.shape[3:])


def compress(kseq, pos_emb, w1, b1, w2, b2):
    bsz, tk, g, dh = kseq.shape
    n = (tk - CMP_LEN) // CMP_STRIDE + 1
    idx = np.arange(n)[:, None] * CMP_STRIDE + np.arange(CMP_LEN)[None, :]
    blk = kseq[:, idx] + pos_emb[None, None, :, None, :]
    blk = blk.transpose(0, 1, 3, 2, 4).reshape(bsz, n, g, CMP_LEN * dh)
    return jax.nn.gelu(blk @ w1 + b1) @ w2 + b2


def overlap_matrix(n_cmp, n_slc):
    i = np.arange(n_cmp)[:, None]
    j = np.arange(n_slc)[None, :]
    units = SLC_BLOCK // CMP_STRIDE
    m = sum(((i + u) // units == j).astype(np.float32) for u in range(CMP_LEN // CMP_STRIDE))
    return jnp.asarray(m, dtype=jnp.float32)


def nsa_block(qc, gc, posc, ck, cv, sk_b, sv_b, wk, wv, win_pos0, overlap, tab_g, k_sel):
    bsz, qb, g, r, dh = qc.shape
    scale = dh ** -0.5
    n_cmp = ck.shape[1]
    n_slc = sk_b.shape[2]
    cend = jnp.arange(n_cmp, dtype=jnp.int32) * CMP_STRIDE + (CMP_LEN - 1)
    dist_c = posc[:, None] - cend[None, :]
    valid_c = (dist_c >= 0)[None, :, None, None, :]
    bias_c = tab_g[:, t5_bucket(dist_c), :].transpose(1, 0, 3, 2)
    s_c = jnp.einsum('bqgrd,bngd->bqgrn', qc, ck) * scale + bias_c
    p_c = masked_softmax(s_c, valid_c)
    o_c = jnp.einsum('bqgrn,bngd->bqgrd', p_c.astype(cv.dtype), cv)
    imp = jnp.einsum('bqgrn,nj->bqgj', p_c, overlap)
    j = jnp.arange(n_slc, dtype=jnp.int32)[None, :]
    cur = (posc // SLC_BLOCK)[:, None]
    forced = (j == 0) | (j == cur) | (j == cur - 1)
    imp = jnp.where(forced[None, :, None, :], jnp.inf, imp)
    imp = jnp.where((j <= cur)[None, :, None, :], imp, -jnp.inf)
    top_val, top_idx = lax.top_k(imp, k_sel)
    sel_idx = top_idx.transpose(0, 2, 1, 3)
    sel_ok = (top_val > -jnp.inf).transpose(0, 2, 1, 3)
    flat = sel_idx.reshape(bsz, g, qb * k_sel, 1)
    kg = jnp.take_along_axis(sk_b, flat, axis=2).reshape(bsz, g, qb, k_sel, SLC_BLOCK, dh)
    vg = jnp.take_along_axis(sv_b, flat, axis=2).reshape(bsz, g, qb, k_sel * SLC_BLOCK, dh)
    kpos = sel_idx[..., None] * SLC_BLOCK + jnp.arange(SLC_BLOCK, dtype=jnp.int32)
    dist_s = posc[None, None, :, None, None] - kpos
    valid_s = ((dist_s >= 0) & sel_ok[..., None]).reshape(bsz, g, qb, 1, k_sel * SLC_BLOCK)
    g_ix = jnp.arange(g)[None, :, None, None, None]
    bias_s = tab_g[g_ix, t5_bucket(dist_s)].transpose(0, 1, 2, 5, 3, 4)
    s_s = jnp.einsum('bqgrd,bgqksd->bgqrks', qc, kg) * scale + bias_s
    p_s = masked_softmax(s_s.reshape(bsz, g, qb, r, k_sel * SLC_BLOCK), valid_s)
    o_s = jnp.einsum('bgqrn,bgqnd->bqgrd', p_s.astype(vg.dtype), vg)
    lw = qb + WINDOW - 1
    first = posc[0] - (WINDOW - 1)
    kw = lax.dynamic_slice_in_dim(wk, first - win_pos0, lw, axis=1)
    vw = lax.dynamic_slice_in_dim(wv, first - win_pos0, lw, axis=1)
    wpos = first + jnp.arange(lw, dtype=jnp.int32)
    dist_w = posc[:, None] - wpos[None, :]
    valid_w = ((dist_w >= 0) & (dist_w < WINDOW) & (wpos[None, :] >= 0))[None, :, None, None, :]
    bias_w = tab_g[:, t5_bucket(dist_w), :].transpose(1, 0, 3, 2)
    s_w = jnp.einsum('bqgrd,bkgd->bqgrk', qc, kw) * scale + bias_w
    p_w = masked_softmax(s_w, valid_w)
    o_w = jnp.einsum('bqgrk,bkgd->bqgrd', p_w.astype(vw.dtype), vw)
    return gc[..., 0:1] * o_c + gc[..., 1:2] * o_s + gc[..., 2:3] * o_w


def nsa_attention(q, gates, q_pos, cmp_seq, slc_seq, win_pad, win_pos0, phi_pos, phi_w1, phi_b1, phi_w2, phi_b2, tab):
    bsz = q.shape[0]
    tk = cmp_seq.shape[1]
    ck = compress(cmp_seq[:, :, 0], phi_pos[0], phi_w1[0], phi_b1[0], phi_w2[0], phi_b2[0])
    cv = compress(cmp_seq[:, :, 1], phi_pos[1], phi_w1[1], phi_b1[1], phi_w2[1], phi_b2[1])
    n_slc = -(-tk // SLC_BLOCK)
    slc_pad = jnp.pad(slc_seq, ((0, 0), (0, n_slc * SLC_BLOCK - tk), (0, 0), (0, 0), (0, 0)))
    slc_blk = slc_pad.reshape(bsz, n_slc, SLC_BLOCK, 2, NSA_KV, HEAD_DIM).transpose(3, 0, 4, 1, 2, 5)
    slc_blk = slc_blk.reshape(2, bsz, NSA_KV, n_slc, SLC_BLOCK * HEAD_DIM)
    sk_b, sv_b = slc_blk[0], slc_blk[1]
    wk, wv = win_pad[:, :, 0], win_pad[:, :, 1]
    overlap = overlap_matrix(ck.shape[1], n_slc)
    tab_g = tab.reshape(N_BUCKETS, NSA_KV, NSA_REP).transpose(1, 0, 2)
    k_sel = min(SLC_TOPK, n_slc)

    def block_fn(args):
        qc, gc, posc = args
        return nsa_block(qc, gc, posc, ck, cv, sk_b, sv_b, wk, wv, win_pos0, overlap, tab_g, k_sel)

    return map_query_blocks(block_fn, NSA_QBLOCK, q_pos, q, gates)


def moba_block(qc, posc, kmean, kb, vb, tab_h, k_m):
    bsz, qb, h, dh = qc.shape
    scale = dh ** -0.5
    blk = posc // MOBA_BLOCK
    own = blk[0]
    kown = lax.dynamic_index_in_dim(kb, own, axis=2, keepdims=False)
    vown = lax.dynamic_index_in_dim(vb, own, axis=2, keepdims=False)
    opos = own * MOBA_BLOCK + jnp.arange(MOBA_BLOCK, dtype=jnp.int32)
    dist_o = posc[:, None] - opos[None, :]
    s_own = jnp.einsum('bqhd,bhld->bhql', qc, kown) * scale + tab_h[:, t5_bucket(dist_o)]
    valid_own = jnp.broadcast_to((dist_o >= 0)[None, None], s_own.shape)
    if k_m == 0:
        p = masked_softmax(s_own, valid_own).astype(vb.dtype)
        return jnp.einsum('bhql,bhld->bqhd', p, vown)
    nblk = kmean.shape[1]
    gs = jnp.einsum('bqhd,bnhd->bqhn', qc.astype(jnp.float32), kmean)
    past_ok = (jnp.arange(nblk, dtype=jnp.int32)[None, :] < blk[:, None])[None, :, None, :]
    gs = jnp.where(past_ok, gs, -jnp.inf)
    top_val, top_idx = lax.top_k(gs, k_m)
    sel_idx = top_idx.transpose(0, 2, 1, 3)
    sel_ok = (top_val > -jnp.inf).transpose(0, 2, 1, 3)
    flat = sel_idx.reshape(bsz, h, qb * k_m, 1)
    nb = kb.shape[2]
    ksel = jnp.take_along_axis(kb.reshape(bsz, h, nb, MOBA_BLOCK * dh), flat, axis=2)
    ksel = ksel.reshape(bsz, h, qb, k_m, MOBA_BLOCK, dh)
    vsel = jnp.take_along_axis(vb.reshape(bsz, h, nb, MOBA_BLOCK * dh), flat, axis=2)
    vsel = vsel.reshape(bsz, h, qb, k_m * MOBA_BLOCK, dh)
    kpos = sel_idx[..., None] * MOBA_BLOCK + jnp.arange(MOBA_BLOCK, dtype=jnp.int32)
    dist = posc[None, None, :, None, None] - kpos
    h_ix = jnp.arange(h)[None, :, None, None, None]
    s_sel = jnp.einsum('bqhd,bhqkld->bhqkl', qc, ksel) * scale + tab_h[h_ix, t5_bucket(dist)]
    valid_sel = jnp.broadcast_to(sel_ok[..., None], s_sel.shape)
    s = jnp.concatenate([s_sel.reshape(bsz, h, qb, k_m * MOBA_BLOCK), s_own], axis=-1)
    valid = jnp.concatenate([valid_sel.reshape(bsz, h, qb, k_m * MOBA_BLOCK), valid_own], axis=-1)
    p = masked_softmax(s, valid).astype(vb.dtype)
    n_sel = k_m * MOBA_BLOCK
    return (jnp.einsum('bhqn,bhqnd->bqhd', p[..., :n_sel], vsel)
            + jnp.einsum('bhql,bhld->bqhd', p[..., n_sel:], vown))


def moba_attention(q, q_pos, kv_seq, tab):
    bsz, _, h, dh = q.shape
    tk = kv_seq.shape[1]
    nblk = -(-tk // MOBA_BLOCK)
    kv_pad = jnp.pad(kv_seq, ((0, 0), (0, nblk * MOBA_BLOCK - tk), (0, 0), (0, 0), (0, 0)))
    kv_blk = kv_pad.reshape(bsz, nblk, MOBA_BLOCK, 2, h, dh)
    kmean = jnp.mean(kv_blk[:, :, :, 0].astype(jnp.float32), axis=2)
    kb = kv_blk[:, :, :, 0].transpose(0, 3, 1, 2, 4)
    vb = kv_blk[:, :, :, 1].transpose(0, 3, 1, 2, 4)
    tab_h = tab.T
    k_m = min(MOBA_TOPK, nblk - 1)

    def block_fn(args):
        qc, posc = args
        return moba_block(qc, posc, kmean, kb, vb, tab_h, k_m)

    return map_query_blocks(block_fn, MOBA_QBLOCK, q_pos, q)


def trunk_layer(x, c, offset, past, rel_bias, w_ada, b_ada, w_in, phi_pos, phi_w1, phi_b1, phi_w2, phi_b2,
                w_up_a, w_up_b, w_out, ln_g, ln_b):
    bsz, t, _ = x.shape
    shift, scale, gate = jnp.split(jax.nn.silu(c) @ w_ada + b_ada, 3, axis=-1)
    h = x * (1.0 + scale[:, None, :]) + shift[:, None, :]
    parts = jnp.split(h @ w_in, np.cumsum(IN_WIDTHS)[:-1], axis=-1)
    (a_q, a_kc, a_vc, a_ks, a_vs, a_kw, a_vw, a_g, a_z, b_q, b_k, b_v, b_z, m_a, m_b) = parts

    def kv_rows(k, v, heads):
        return jnp.stack([k.reshape(bsz, t, heads, HEAD_DIM), v.reshape(bsz, t, heads, HEAD_DIM)], axis=2)

    cmp_new = kv_rows(a_kc, a_vc, NSA_KV)
    slc_new = kv_rows(a_ks, a_vs, NSA_KV)
    win_new = kv_rows(a_kw, a_vw, NSA_KV)
    moba_new = kv_rows(b_k, b_v, MOBA_HEADS)
    if past is None:
        cmp_seq, slc_seq, moba_seq, win_seq = cmp_new, slc_new, moba_new, win_new
        win_past = 0
    else:
        cmp_rows, slc_rows, moba_rows, win_rows = past
        cmp_seq = jnp.concatenate([cmp_rows, cmp_new], axis=1)
        slc_seq = jnp.concatenate([slc_rows, slc_new], axis=1)
        moba_seq = jnp.concatenate([moba_rows, moba_new], axis=1)
        win_seq = jnp.concatenate([win_rows, win_new], axis=1)
        win_past = win_rows.shape[1]
    win_pad = jnp.concatenate([jnp.zeros((bsz, WINDOW) + win_seq.shape[2:], win_seq.dtype), win_seq], axis=1)
    q_pos = offset + jnp.arange(t, dtype=jnp.int32)

    o_a = nsa_attention(a_q.reshape(bsz, t, NSA_KV, NSA_REP, HEAD_DIM),
                        jax.nn.sigmoid(a_g.reshape(bsz, t, NSA_KV, NSA_REP, 3)),
                        q_pos, cmp_seq, slc_seq, win_pad, offset - win_past - WINDOW,
                        phi_pos, phi_w1, phi_b1, phi_w2, phi_b2, rel_bias[:, :NSA_HEADS])
    o_b = moba_attention(b_q.reshape(bsz, t, MOBA_HEADS, HEAD_DIM), q_pos, moba_seq, rel_bias[:, NSA_HEADS:])
    y_a = (o_a.reshape(bsz, t, NSA_WIDTH) * jax.nn.silu(a_z)) @ w_up_a
    y_b = (o_b.reshape(bsz, t, MOBA_WIDTH) * jax.nn.silu(b_z)) @ w_up_b
    mixed = (jax.nn.sigmoid(m_a) * y_a + jax.nn.sigmoid(m_b) * y_b) @ w_out
    x_new = layer_norm(DEEPNORM_ALPHA * x + gate[:, None, :] * mixed, ln_g, ln_b)
    win_out = win_new[:, t - min(WINDOW, t):] if past is None else win_new
    return x_new, cmp_new, slc_new, moba_new, win_out


def gather_rows(pool, page_table, layer):
    rows = pool[page_table, layer]
    return rows.reshape((rows.shape[0], rows.shape[1] * rows.shape[2]) + rows.shape[3:])


def setup_inputs(seed: int = 0) -> dict:
    key = jax.random.key(seed)
    ks = jax.random.split(key, 26)
    f32 = jnp.float32
    n_pages = PAST_LEN // PAGE_SIZE
    n_used = DEC_BATCH * n_pages
    n_phys = n_used + (n_used + 3) // 4
    win_rows = min(WINDOW, PAST_LEN)
    n_in = sum(IN_WIDTHS)

    def nrm(k, shape, s):
        return jax.random.normal(k, shape, f32) * s

    page_table = jax.random.permutation(ks[0], n_phys)[:n_used].reshape(DEC_BATCH, n_pages).astype(jnp.int32)
    return {
        'x_prompt': nrm(ks[1], (BATCH, SEQ, D_MODEL), 1.0),
        'x_sample': nrm(ks[2], (DEC_BATCH, DEC_SEQ, D_MODEL), 1.0),
        'cache_nsa_cmp': nrm(ks[3], (n_phys, DEPTH, PAGE_SIZE, 2, NSA_KV, HEAD_DIM), 1.0),
        'cache_nsa_slc': nrm(ks[4], (n_phys, DEPTH, PAGE_SIZE, 2, NSA_KV, HEAD_DIM), 1.0),
        'cache_moba': nrm(ks[5], (n_phys, DEPTH, PAGE_SIZE, 2, MOBA_HEADS, HEAD_DIM), 1.0),
        'state_nsa_win': nrm(ks[6], (DEPTH, DEC_BATCH, win_rows, 2, NSA_KV, HEAD_DIM), 1.0),
        'page_table': page_table,
        'c_prompt': nrm(ks[7], (BATCH, D_MODEL), 1.0),
        'c_sample': nrm(ks[8], (DEC_BATCH, D_MODEL), 1.0),
        'rel_bias': nrm(ks[9], (N_BUCKETS, NSA_HEADS + MOBA_HEADS), 0.5),
        'w_ada': nrm(ks[10], (DEPTH, D_MODEL, 3 * D_MODEL), 0.5 * D_MODEL ** -0.5),
        'b_ada': nrm(ks[11], (DEPTH, 3 * D_MODEL), 0.02),
        'w_in': nrm(ks[12], (DEPTH, D_MODEL, n_in), D_MODEL ** -0.5),
        'phi_pos': nrm(ks[13], (DEPTH, 2, CMP_LEN, HEAD_DIM), 0.02),
        'phi_w1': nrm(ks[14], (DEPTH, 2, CMP_LEN * HEAD_DIM, HEAD_DIM), (CMP_LEN * HEAD_DIM) ** -0.5),
        'phi_b1': nrm(ks[15], (DEPTH, 2, HEAD_DIM), 0.02),
        'phi_w2': nrm(ks[16], (DEPTH, 2, HEAD_DIM, HEAD_DIM), HEAD_DIM ** -0.5),
        'phi_b2': nrm(ks[17], (DEPTH, 2, HEAD_DIM), 0.02),
        'w_up_a': nrm(ks[18], (DEPTH, NSA_WIDTH, D_MODEL), NSA_WIDTH ** -0.5 * DEEPNORM_BETA),
        'w_up_b': nrm(ks[19], (DEPTH, MOBA_WIDTH, D_MODEL), MOBA_WIDTH ** -0.5 * DEEPNORM_BETA),
        'w_out': nrm(ks[20], (DEPTH, D_MODEL, D_MODEL), D_MODEL ** -0.5 * DEEPNORM_BETA),
        'ln_g': 1.0 + nrm(ks[21], (DEPTH, D_MODEL), 0.02),
        'ln_b': nrm(ks[22], (DEPTH, D_MODEL), 0.02),
    }


def reference(x_prompt, x_sample, cache_nsa_cmp, cache_nsa_slc, cache_moba, state_nsa_win, page_table,
              c_prompt, c_sample, rel_bias, w_ada, b_ada, w_in, phi_pos, phi_w1, phi_b1, phi_w2, phi_b2,
              w_up_a, w_up_b, w_out, ln_g, ln_b):
    past_len = page_table.shape[1] * PAGE_SIZE
    yp, ys = x_prompt, x_sample
    p_cmp, s_cmp, p_slc, s_slc, p_moba, s_moba, p_win, s_win = [], [], [], [], [], [], [], []
    for l in range(DEPTH):
        wl = (w_ada[l], b_ada[l], w_in[l], phi_pos[l], phi_w1[l], phi_b1[l], phi_w2[l], phi_b2[l],
              w_up_a[l], w_up_b[l], w_out[l], ln_g[l], ln_b[l])
        yp, pc, ps, pm, pw = trunk_layer(yp, c_prompt, 0, None, rel_bias, *wl)
        past = (gather_rows(cache_nsa_cmp, page_table, l), gather_rows(cache_nsa_slc, page_table, l),
                gather_rows(cache_moba, page_table, l), state_nsa_win[l])
        ys, sc, ss, sm, sw = trunk_layer(ys, c_sample, past_len, past, rel_bias, *wl)
        p_cmp.append(pc); s_cmp.append(sc); p_slc.append(ps); s_slc.append(ss)
        p_moba.append(pm); s_moba.append(sm); p_win.append(pw); s_win.append(sw)
    return (yp, ys,
            jnp.stack(p_cmp, axis=1), jnp.stack(s_cmp, axis=1),
            jnp.stack(p_slc, axis=1), jnp.stack(s_slc, axis=1),
            jnp.stack(p_moba, axis=1), jnp.stack(s_moba, axis=1),
            jnp.stack(p_win, axis=0), jnp.stack(s_win, axis=0))
```

```python
import math
from contextlib import ExitStack

import numpy as np
import ml_dtypes

import concourse.bass as bass
import concourse.mybir as mybir
from concourse.bass_types import AP
from concourse.bass_utils import run_bass_kernel_spmd

F32 = mybir.dt.float32
BF16 = mybir.dt.bfloat16
I32 = mybir.dt.int32
AF = mybir.ActivationFunctionType
ALU = mybir.AluOpType
AX = mybir.AxisListType

D = 1024
NIN = 5912
C_AQ, C_KC, C_VC, C_KS, C_VS, C_KW, C_VW, C_AG, C_AZ, C_BQ, C_BK, C_BV, C_BZ, C_MA, C_MB = (
    0, 512, 640, 768, 896, 1024, 1152, 1280, 1304, 1816, 2328, 2840, 3352, 3864, 4888)
NEG = -30000.0
LN_EPS = 1e-5
VL = 1024
VOFF = 400

import os
NO_PE_SKIP = not bool(os.environ.get("PE_SKIP"))
DEFAULT_CFG = dict(SEQ=4096, PAST=8192, NS=4, NPHYS=2560, DEPTH=2, BATCH=4, ALPHA=(2 * 2) ** 0.25, STOP=None)


class KB:
    ENGS = ["tensor", "vector", "scalar", "gpsimd", "sync"]

    def __init__(self, nc, es):
        self.nc = nc
        self.es = es
        self.sem = {e: es.enter_context(nc.semaphore("s_" + e)) for e in self.ENGS}
        self.cnt = {e: 0 for e in self.ENGS}
        self.waited = {}
        self.lastw = {}
        self.reads = {}
        self.dsem = {}
        self.ninstr = 0
        self.excl = set()

    def _eng(self, eng):
        return getattr(self.nc, eng)

    def _wait(self, eng, s, c):
        if self.waited.get((eng, s), 0) >= c:
            return
        self.waited[(eng, s)] = c
        semh = s[1] if isinstance(s, tuple) else self.sem[s]
        self._eng(eng).wait_ge(semh, c)

    def _deps(self, eng, reads, writes):
        need = {}
        for k in reads:
            if k in self.lastw:
                s, c = self.lastw[k]
                need[s] = max(need.get(s, 0), c)
        for k in writes:
            if k in self.lastw:
                s, c = self.lastw[k]
                need[s] = max(need.get(s, 0), c)
            for (s, c) in self.reads.get(k, ()):
                need[s] = max(need.get(s, 0), c)
        for s, c in need.items():
            if eng == "tensor" and s == "tensor" and not NO_PE_SKIP:
                continue
            self._wait(eng, s, c)

    def _mark(self, tok, reads, writes):
        for k in writes:
            self.lastw[k] = tok
            self.reads[k] = []
        for k in reads:
            lst = self.reads.setdefault(k, [])
            lst[:] = [x for x in lst if x[0] != tok[0]]
            lst.append(tok)

    def _split(self, reads, writes):
        if not self.excl:
            return reads, writes
        r2 = [k for k in reads if k not in self.excl]
        w2 = list(writes) + [k for k in reads if k in self.excl and k not in writes]
        return r2, w2

    def op(self, eng, fn, reads=(), writes=()):
        reads, writes = self._split(reads, writes)
        self._deps(eng, reads, writes)
        self.cnt[eng] += 1
        fn(self._eng(eng)).then_inc(self.sem[eng], 1)
        self._mark((eng, self.cnt[eng]), reads, writes)
        self.ninstr += 1

    def dma(self, eng, fn, slot, reads=(), writes=()):
        if slot not in self.dsem:
            self.dsem[slot] = [self.es.enter_context(self.nc.semaphore("d%d" % len(self.dsem))), 0]
        reads, writes = self._split(reads, writes)
        self._deps(eng, reads, writes)
        d = self.dsem[slot]
        d[1] += 16
        fn(self._eng(eng)).then_inc(d[0], 16)
        self._mark((("d", d[0]), d[1]), reads, writes)
        self.ninstr += 1

    def barrier(self):
        for e in self.ENGS:
            for slot, (semh, c) in self.dsem.items():
                if c:
                    self._wait(e, ("d", semh), c)
            for e2 in self.ENGS:
                if e2 != e and self.cnt[e2]:
                    self._wait(e, e2, self.cnt[e2])
        self.lastw = {}
        self.reads = {}


class Builder:
    def __init__(self, cfg):
        self.cfg = cfg
        self.SEQ = cfg["SEQ"]
        self.PAST = cfg["PAST"]
        self.NS = cfg["NS"]
        self.NPHYS = cfg["NPHYS"]
        self.DEPTH = cfg["DEPTH"]
        self.NT = self.SEQ // 128
        self.NPG = self.PAST // 128
        self.NTOK = self.SEQ + self.NS
        self.NCP = self.SEQ // 16 - 1
        self.NCS = self.PAST // 16 - 1
        self.nc = bass.Bass("TRN2", target_bir_lowering=False)
        self.uid = 0

    def dram(self, name, shape, dt, kind):
        return self.nc.dram_tensor(name, list(shape), dt, kind=kind).ap()

    def sb(self, es, name, shape, dt):
        self.uid += 1
        return es.enter_context(self.nc.sbuf_tensor("%s_%d" % (name, self.uid), list(shape), dt))

    def ps(self, es, name, shape, dt=F32):
        self.uid += 1
        nbytes = int(np.prod(shape[1:])) * (4 if dt == F32 else 2)
        assert nbytes % 2048 == 0, (name, shape)
        t = es.enter_context(self.nc.psum_tensor("%s_%d" % (name, self.uid), list(shape), dt))
        self.kb.excl.add(t.name)
        return t

    def mm(self, out, lhsT, rhs, start, stop, reads, writes):
        self.kb.op("tensor", lambda e: e.matmul(out, lhsT=lhsT, rhs=rhs, start=start, stop=stop),
                   reads=reads, writes=writes)

    def tr(self, out, in_, ident, reads, writes):
        self.kb.op("tensor", lambda e: e.transpose(out=out, in_=in_, identity=ident), reads=reads, writes=writes)

    def act(self, out, in_, func, reads, writes, bias=None, scale=None):
        kw = {}
        if bias is not None:
            kw["bias"] = bias
        if scale is not None:
            kw["scale"] = scale
        self.kb.op("scalar", lambda e: e.activation(out=out, in_=in_, func=func, **kw), reads=reads, writes=writes)

    def copy(self, eng, out, in_, reads, writes):
        if eng == "scalar":
            self.kb.op("scalar", lambda e: e.copy(out=out, in_=in_), reads=reads, writes=writes)
        else:
            self.kb.op(eng, lambda e: e.tensor_copy(out=out, in_=in_), reads=reads, writes=writes)

    def ts(self, eng, out, in0, s1, s2, op0, op1, reads, writes):
        if op1 is None:
            self.kb.op(eng, lambda e: e.tensor_scalar(out=out, in0=in0, scalar1=s1, scalar2=None, op0=op0),
                       reads=reads, writes=writes)
        else:
            self.kb.op(eng, lambda e: e.tensor_scalar(out=out, in0=in0, scalar1=s1, scalar2=s2, op0=op0, op1=op1),
                       reads=reads, writes=writes)

    def tt(self, eng, out, in0, in1, op, reads, writes):
        self.kb.op(eng, lambda e: e.tensor_tensor(out=out, in0=in0, in1=in1, op=op), reads=reads, writes=writes)

    def ld(self, out, in_, slot, reads, writes, eng="sync", slow=False):
        self.kb.dma(eng, lambda e: e.dma_start(out=out, in_=in_, allow_slow_non_contiguous=True), slot,
                    reads=reads, writes=writes)

    def dump(self, idx, ap, key, w):
        self.ld(self.d["dbg"][idx, 0:ap.shape[0], 0:w], ap, "dbg", [key], ["dbg"])

    def declare(self):
        c = self
        I, O, N = "ExternalInput", "ExternalOutput", "Internal"
        SEQ, NS, NPHYS, DEPTH, NPG, NTOK = c.SEQ, c.NS, c.NPHYS, c.DEPTH, c.NPG, c.NTOK
        d = {}
        d["xp"] = c.dram("xp", [SEQ, D], F32, I)
        d["xs"] = c.dram("xs", [NS, D], F32, I)
        d["pcmp"] = c.dram("pcmp", [NPHYS * DEPTH * 128, 256], F32, I)
        d["pslc"] = c.dram("pslc", [NPHYS * DEPTH * 128, 256], F32, I)
        d["pmoba"] = c.dram("pmoba", [NPHYS * DEPTH * 128 * 2, 512], F32, I)
        d["wst"] = c.dram("wst", [DEPTH, NS, 512, 256], F32, I)
        d["pt"] = c.dram("pt", [NS, NPG], I32, I)
        d["c5"] = c.dram("c5", [1 + NS, D], F32, I)
        d["relb"] = c.dram("relb", [32, 16], F32, I)
        d["w_ada"] = c.dram("w_ada", [DEPTH, D, 3 * D], F32, I)
        d["b_ada"] = c.dram("b_ada", [DEPTH, 3 * D], F32, I)
        d["w_in"] = c.dram("w_in", [DEPTH, D, NIN], F32, I)
        d["phi_pos"] = c.dram("phi_pos", [DEPTH, 2, 32, 64], F32, I)
        d["phi_w1"] = c.dram("phi_w1", [DEPTH, 2, 2048, 64], F32, I)
        d["phi_b1"] = c.dram("phi_b1", [DEPTH, 2, 64], F32, I)
        d["phi_w2"] = c.dram("phi_w2", [DEPTH, 2, 64, 64], F32, I)
        d["phi_b2"] = c.dram("phi_b2", [DEPTH, 2, 64], F32, I)
        d["w_up_a"] = c.dram("w_up_a", [DEPTH, 512, D], F32, I)
        d["w_up_b"] = c.dram("w_up_b", [DEPTH, 512, D], F32, I)
        d["w_out"] = c.dram("w_out", [DEPTH, D, D], F32, I)
        d["ln_g"] = c.dram("ln_g", [DEPTH, D], F32, I)
        d["ln_b"] = c.dram("ln_b", [DEPTH, D], F32, I)
        d["bkt"] = c.dram("bkt", [33, VL], F32, I)
        d["ovl_p"] = c.dram("ovl_p", [256, 64], F32, I)
        d["ovl_s"] = c.dram("ovl_s", [c.NCS + 1, 132], F32, I)
        d["yp"] = c.dram("yp", [SEQ, D], F32, O)
        d["ys"] = c.dram("ys", [NS, D], F32, O)
        d["ncp"] = c.dram("ncp", [DEPTH, SEQ, 256], F32, O)
        d["ncs"] = c.dram("ncs", [NS, DEPTH, 256], F32, O)
        d["nsp"] = c.dram("nsp", [DEPTH, SEQ, 256], F32, O)
        d["nss"] = c.dram("nss", [NS, DEPTH, 256], F32, O)
        d["nmp"] = c.dram("nmp", [DEPTH, SEQ, 1024], F32, O)
        d["nms"] = c.dram("nms", [NS, DEPTH, 1024], F32, O)
        d["nwp"] = c.dram("nwp", [DEPTH, 512, 256], F32, O)
        d["nws"] = c.dram("nws", [DEPTH, NS, 256], F32, O)
        if c.cfg.get("DBG_T") is not None:
            d["dbg"] = c.dram("dbg", [16, 128, 1040], F32, O)
        d["x1p"] = c.dram("x1p", [SEQ, D], F32, N)
        d["x1s"] = c.dram("x1s", [NS, D], F32, N)
        d["qs"] = c.dram("qs", [NTOK, NIN], F32, N)
        for nm in ("ktc", "vtc", "kts", "ktw"):
            d[nm] = c.dram(nm, [128, NTOK], BF16, N)
        d["ktm"] = c.dram("ktm", [128, 4, NTOK], BF16, N)
        d["vs"] = c.dram("vs", [NTOK, 128], BF16, N)
        d["vw"] = c.dram("vw", [NTOK, 128], BF16, N)
        d["vm"] = c.dram("vm", [NTOK, 512], BF16, N)
        d["gsr"] = c.dram("gsr", [NS, D], F32, N)
        d["vvec"] = c.dram("vvec", [16, VL], F32, N)
        d["sk1"] = c.dram("sk1", [16, 128 * (VL + 1)], F32, N)
        d["sk16"] = c.dram("sk16", [16, 128 * (VL + 16)], F32, N)
        self.d = d

    def constants(self, es):
        c, kb, d = self, self.kb, self.d
        self.ident_f = c.sb(es, "identf", [128, 128], F32)
        self.ident_b = c.sb(es, "identb", [128, 128], BF16)
        self.ones_b = c.sb(es, "onesb", [128, 128], BF16)
        self.ones_f = c.sb(es, "onesf", [128, 128], F32)
        kb.op("gpsimd", lambda e: e.memset(self.ident_f[:], 0.0), writes=["identf"])
        kb.op("gpsimd", lambda e: e.affine_select(out=self.ident_f[:], in_=self.ident_f[:], pattern=[[-1, 128]],
                                                    compare_op=ALU.not_equal, fill=1.0, base=0, channel_multiplier=1),
              reads=["identf"], writes=["identf"])
        c.copy("vector", self.ident_b[:], self.ident_f[:], ["identf"], ["identb"])
        kb.op("gpsimd", lambda e: e.memset(self.ones_f[:], 1.0), writes=["onesf"])
        kb.op("gpsimd", lambda e: e.memset(self.ones_b[:], 1.0), writes=["onesb"])
        with ExitStack() as ph:
            tab = c.sb(ph, "tab", [33, 16], F32)
            t31 = c.sb(ph, "t31", [33, 16], F32)
            bk = c.sb(ph, "bk", [33, VL], F32)
            vv = c.sb(ph, "vv", [16, VL], F32)
            vrep = c.sb(ph, "vrep", [128, VL], F32)
            pv = c.ps(ph, "pv", [128, 512], F32)
            kb.op("vector", lambda e: e.memset(tab[:], NEG), writes=["tab"])
            kb.op("vector", lambda e: e.memset(t31[:], 0.0), writes=["t31"])
            c.ld(tab[0:32, :], d["relb"][:, :], "c_tab", [], ["tab"])
            c.ld(t31[0:32, :], d["relb"][31:32, :].partition_broadcast(32), "c_t31", [], ["t31"])
            c.ld(bk[:], d["bkt"][:, :], "c_bk", [], ["bk"])
            c.tt("vector", tab[:], tab[:], t31[:], ALU.subtract, ["tab", "t31"], ["tab"])
            for j in range(VL // 512):
                c.mm(pv[0:16, :], tab[:], bk[:, j * 512:(j + 1) * 512], True, True, ["tab", "bk"], ["pv"])
                c.copy("vector", vv[:, j * 512:(j + 1) * 512], pv[0:16, :], ["pv"], ["vv"])
            c.ld(d["vvec"][:, :], vv[:], "c_vv", ["vv"], ["vvec"])
            for h in range(16):
                c.ld(vrep[:], d["vvec"][h:h + 1, :].partition_broadcast(128), "c_vrep", ["vvec"], ["vrep"])
                for nm, pitch in (("sk1", VL + 1), ("sk16", VL + 16)):
                    dst = AP(tensor=d[nm].tensor, offset=h * 128 * pitch, ap=[[pitch, 128], [1, VL]])
                    c.ld(dst, vrep[:], "c_" + nm, ["vrep"], [nm])
        self.diag = c.sb(es, "diag", [128, 16, 128], F32)
        self.off = c.sb(es, "off", [128, 16, 128], F32)
        p1 = VL + 1
        src = AP(tensor=d["sk1"].tensor, offset=VOFF, ap=[[p1 - 1, 128], [128 * p1, 16], [1, 128]])
        c.ld(self.diag[:], src, "c_diag", ["sk1"], ["diag"])
        src = AP(tensor=d["sk1"].tensor, offset=VOFF + 128, ap=[[p1 - 1, 128], [128 * p1, 16], [1, 128]])
        c.ld(self.off[:], src, "c_off", ["sk1"], ["off"])

    def phase_S(self, es, l):
        c, kb, d, NS = self, self.kb, self.d, self.NS
        NB = 1 + NS
        s1T = c.sb(es, "s1T", [128, 8, NB], F32)
        shT = c.sb(es, "shT", [128, 8, NB], F32)
        gate_bc = c.sb(es, "gatebc", [128, D], F32)
        lng = c.sb(es, "lng", [128, D], F32)
        lnb = c.sb(es, "lnb", [128, D], F32)
        c.ld(lng[:], d["ln_g"][l:l + 1, :].partition_broadcast(128), "s_lng", [], ["lng"])
        c.ld(lnb[:], d["ln_b"][l:l + 1, :].partition_broadcast(128), "s_lnb", [], ["lnb"])
        with ExitStack() as ph:
            wada = c.sb(ph, "wada", [128, 8, 3 * D], BF16)
            cT = c.sb(ph, "cT", [128, 8, NB], F32)
            tmp = c.sb(ph, "ctmp", [128, 8, NB], F32)
            scT = c.sb(ph, "scT", [128, 8, NB], BF16)
            screp = c.sb(ph, "screp", [128, 8, 128], BF16)
            badaT = c.sb(ph, "badaT", [128, 24], F32)
            adaT = c.sb(ph, "adaT", [128, 24, NB], F32)
            bg = c.sb(ph, "bg", [128, D], F32)
            grow = c.sb(ph, "grow", [1, D], F32)
            pa = c.ps(ph, "pa", [128, 512], F32)
            pg = c.ps(ph, "pg", [128, 1024], F32)
            for k in range(8):
                c.ld(wada[:, k, :], d["w_ada"][l, k * 128:(k + 1) * 128, :], "s_wada", [], ["wada"], eng="gpsimd")
            for j in range(NB):
                srcc = AP(tensor=d["c5"].tensor, offset=j * D, ap=[[1, 128], [128, 8]])
                c.ld(cT[:, :, j], srcc, "s_cT", [], ["cT"], slow=True)
            srcb = AP(tensor=d["b_ada"].tensor, offset=l * 3 * D, ap=[[1, 128], [128, 24]])
            c.ld(badaT[:], srcb, "s_bada", [], ["badaT"], slow=True)
            c.ld(bg[:], d["b_ada"][l:l + 1, 2 * D:3 * D].partition_broadcast(128), "s_bg", [], ["bg"])
            c.act(tmp[:], cT[:], AF.Exp, ["cT"], ["ctmp"], scale=-1.0)
            c.ts("vector", tmp[:], tmp[:], 1.0, None, ALU.add, None, ["ctmp"], ["ctmp"])
            kb.op("vector", lambda e: e.reciprocal(out=tmp[:], in_=tmp[:]), reads=["ctmp"], writes=["ctmp"])
            c.tt("vector", scT[:], cT[:], tmp[:], ALU.mult, ["cT", "ctmp"], ["scT"])
            for k in range(8):
                c.copy("vector", screp[:, k, :], scT[:, k, 0:1].broadcast_to([128, 128]), ["scT"], ["screp"])
            for j in range(24):
                for k in range(8):
                    c.mm(pa[:, j * NB:(j + 1) * NB], wada[:, k, j * 128:(j + 1) * 128], scT[:, k, :],
                         k == 0, k == 7, ["wada", "scT"], ["pa"])
            c.tt("vector", adaT[:], pa[:, 0:24 * NB].rearrange("p (j b) -> p j b", b=NB),
                 badaT[:].unsqueeze(2).broadcast_to([128, 24, NB]), ALU.add, ["pa", "badaT"], ["adaT"])
            c.copy("vector", shT[:], adaT[:, 0:8, :], ["adaT"], ["shT"])
            c.ts("vector", s1T[:], adaT[:, 8:16, :], 1.0, None, ALU.add, None, ["adaT"], ["s1T"])
            for hf in range(2):
                for k in range(8):
                    c.mm(pg[:, hf * 512:(hf + 1) * 512], screp[:, k, :], wada[:, k, 2 * D + hf * 512:2 * D + (hf + 1) * 512],
                         k == 0, k == 7, ["screp", "wada"], ["pg"])
            c.tt("vector", gate_bc[:], pg[:], bg[:], ALU.add, ["pg", "bg"], ["gatebc"])
            for s in range(NS):
                for hf in range(2):
                    for k in range(8):
                        c.mm(pg[0:1, hf * 512:(hf + 1) * 512], scT[:, k, 1 + s:2 + s],
                             wada[:, k, 2 * D + hf * 512:2 * D + (hf + 1) * 512], k == 0, k == 7, ["scT", "wada"], ["pg"])
                c.tt("vector", grow[:], pg[0:1, :], bg[0:1, :], ALU.add, ["pg", "bg"], ["grow"])
                c.ld(d["gsr"][s:s + 1, :], grow[:], "s_gsr", ["grow"], ["gsr"])
        return dict(s1T=s1T, shT=shT, gate_bc=gate_bc, lng=lng, lnb=lnb)

    def phase_P(self, l, S):
        c, kb, d, NS, SEQ, NT = self, self.kb, self.d, self.NS, self.SEQ, self.NT
        xin_p = d["xp"] if l == 0 else d["x1p"]
        xin_s = d["xs"] if l == 0 else d["x1s"]
        with ExitStack() as ph:
            win = c.sb(ph, "win", [128, 8, NIN], BF16)
            xt = [c.sb(ph, "xt", [128, D], F32) for _ in range(2)]
            hT = [c.sb(ph, "hT", [128, 8, 128], BF16) for _ in range(2)]
            p32 = [c.sb(ph, "p32", [128, NIN], F32) for _ in range(2)]
            kT = [c.sb(ph, "kT", [128, 8, 128], BF16) for _ in range(2)]
            pT = c.ps(ph, "pT", [128, 1024], F32)
            pacc = [c.ps(ph, "pacc", [128, 512], F32) for _ in range(3)]
            pk = c.ps(ph, "pk", [128, 1024], F32)
            for k in range(8):
                for hf in range(2):
                    c0 = hf * (NIN // 2)
                    c.ld(win[:, k, c0:c0 + NIN // 2], d["w_in"][l, k * 128:(k + 1) * 128, c0:c0 + NIN // 2],
                         "p_win", [], ["win"], eng="gpsimd")
            groups = [(g0, min(512, NIN - g0)) for g0 in range(0, NIN, 512)]
            gi = 0
            for t in range(NT + 1):
                i = t % 2
                samp = t == NT
                nt = NS if samp else 128
                r0 = SEQ if samp else t * 128
                kx, kh, kp, kk = "xt%d" % i, "hT%d" % i, "p32%d" % i, "kT%d" % i
                if samp:
                    c.ld(xt[i][0:nt, :], xin_s[:, :], "p_x%d" % i, ["x1s"], [kx])
                else:
                    c.ld(xt[i][:, :], xin_p[r0:r0 + 128, :], "p_x%d" % i, ["x1p"], [kx])
                for k in range(8):
                    c.tr(pT[:, k * 128:k * 128 + nt], xt[i][0:nt, k * 128:(k + 1) * 128], self.ident_f[0:nt, 0:nt],
                         [kx, "identf"], ["pT"])
                if samp:
                    pv = pT[:, :].rearrange("p (k t) -> p k t", t=128)[:, :, 0:nt]
                    c.tt("vector", pv, pv, S["s1T"][:, :, 1:1 + NS], ALU.mult, ["pT", "s1T"], ["pT"])
                    c.tt("vector", hT[i][:, :, 0:nt], pv, S["shT"][:, :, 1:1 + NS], ALU.add, ["pT", "shT"], [kh])
                else:
                    for k in range(8):
                        if k % 2 == 0:
                            c.act(hT[i][:, k, :], pT[:, k * 128:(k + 1) * 128], AF.Identity, ["pT", "s1T", "shT"], [kh],
                                  bias=S["shT"][:, k, 0:1], scale=S["s1T"][:, k, 0:1])
                        else:
                            c.ts("vector", hT[i][:, k, :], pT[:, k * 128:(k + 1) * 128], S["s1T"][:, k, 0:1],
                                 S["shT"][:, k, 0:1], ALU.mult, ALU.add, ["pT", "s1T", "shT"], [kh])
                for (g0, w) in groups:
                    pa = pacc[gi % 3]
                    pkey = "pacc%d" % (gi % 3)
                    for k in range(8):
                        c.mm(pa[0:nt, 0:w], hT[i][:, k, 0:nt], win[:, k, g0:g0 + w], k == 0, k == 7, [kh, "win"], [pkey])
                    c.copy("scalar" if gi % 2 == 0 else "vector", p32[i][0:nt, g0:g0 + w], pa[0:nt, 0:w], [pkey], [kp])
                    gi += 1
                blocks = [C_KC, C_VC, C_KS, C_KW, C_BK, C_BK + 128, C_BK + 256, C_BK + 384]
                for bi, c0 in enumerate(blocks):
                    c.tr(pk[:, bi * 128:bi * 128 + nt], p32[i][0:nt, c0:c0 + 128], self.ident_f[0:nt, 0:nt],
                         [kp, "identf"], ["pk"])
                pkv = pk[:, :].rearrange("p (k t) -> p k t", t=128)
                c.copy("vector", kT[i][:, 0:4, 0:nt], pkv[:, 0:4, 0:nt], ["pk"], [kk])
                c.copy("scalar", kT[i][:, 4:8, 0:nt], pkv[:, 4:8, 0:nt], ["pk"], [kk])
                for bi, nm in enumerate(("ktc", "vtc", "kts", "ktw")):
                    c.ld(d[nm][:, r0:r0 + nt], kT[i][:, bi, 0:nt], "p_kt%d" % i, [kk], [nm])
                c.ld(d["ktm"][:, :, r0:r0 + nt], kT[i][:, 4:8, 0:nt], "p_kt%d" % i, [kk], ["ktm"])
                c.ld(d["vs"][r0:r0 + nt, :], p32[i][0:nt, C_VS:C_VS + 128], "p_v%d" % i, [kp], ["vs"], eng="gpsimd")
                c.ld(d["vw"][r0:r0 + nt, :], p32[i][0:nt, C_VW:C_VW + 128], "p_v%d" % i, [kp], ["vw"], eng="gpsimd")
                c.ld(d["vm"][r0:r0 + nt, :], p32[i][0:nt, C_BV:C_BV + 512], "p_v%d" % i, [kp], ["vm"], eng="gpsimd")
                if samp:
                    c.ld(d["ncs"][:, l, :], p32[i][0:nt, C_KC:C_KC + 256], "p_o%d" % i, [kp], ["ncs"])
                    c.ld(d["nss"][:, l, :], p32[i][0:nt, C_KS:C_KS + 256], "p_o%d" % i, [kp], ["nss"])
                    c.ld(d["nws"][l, :, :], p32[i][0:nt, C_KW:C_KW + 256], "p_o%d" % i, [kp], ["nws"])
                    c.ld(d["nms"][:, l, :], p32[i][0:nt, C_BK:C_BK + 1024], "p_o%d" % i, [kp], ["nms"])
                else:
                    c.ld(d["ncp"][l, r0:r0 + 128, :], p32[i][:, C_KC:C_KC + 256], "p_o%d" % i, [kp], ["ncp"])
                    c.ld(d["nsp"][l, r0:r0 + 128, :], p32[i][:, C_KS:C_KS + 256], "p_o%d" % i, [kp], ["nsp"])
                    c.ld(d["nmp"][l, r0:r0 + 128, :], p32[i][:, C_BK:C_BK + 1024], "p_o%d" % i, [kp], ["nmp"])
                    if r0 >= SEQ - 512:
                        w0 = r0 - (SEQ - 512)
                        c.ld(d["nwp"][l, w0:w0 + 128, :], p32[i][:, C_KW:C_KW + 256], "p_o%d" % i, [kp], ["nwp"])
                c.ld(d["qs"][r0:r0 + nt, :], p32[i][0:nt, :], "p_o%d" % i, [kp], ["qs"])


    def gelu_tanh(self, u, x, n, h1):
        c, kb = self, self.kb
        kx, ku = x.name, u.name
        X, U = x[:, 0:n], u[:, 0:n]
        c.tt("vector", U, X, X, ALU.mult, [kx], [ku])
        c.ts("vector", U, U, 0.044715, 1.0, ALU.mult, ALU.add, [ku], [ku])
        c.tt("vector", U, U, X, ALU.mult, [ku, kx], [ku])
        c.act(U, U, AF.Exp, [ku], [ku], scale=-2.0 * math.sqrt(2.0 / math.pi))
        c.ts("vector", U, U, 1.0, None, ALU.add, None, [ku], [ku])
        kb.op("vector", lambda e: e.reciprocal(out=U, in_=U), reads=[ku], writes=[ku])
        c.tt("vector", h1[:, 0:n], X, U, ALU.mult, [kx, ku], [h1.name])

    def load_phi(self, ph, l):
        c, kb, d = self, self.kb, self.d
        W = {}
        pb = c.ps(ph, "pb1", [128, 512], F32)
        for kv in range(2):
            w1 = c.sb(ph, "w1bd", [128, 32, 128], BF16)
            w2 = c.sb(ph, "w2bd", [128, 128], BF16)
            pos = c.sb(ph, "posbd", [128, 32], BF16)
            b1 = c.sb(ph, "b1T", [128, 1], F32)
            b2 = c.sb(ph, "b2T", [128, 1], F32)
            b1e = c.sb(ph, "b1e", [128, 1], F32)
            kb.op("gpsimd", lambda e, w1=w1: e.memset(w1[:], 0.0), writes=[w1.name])
            kb.op("gpsimd", lambda e, w2=w2: e.memset(w2[:], 0.0), writes=[w2.name])
            for g in range(2):
                src = AP(tensor=d["phi_w1"].tensor, offset=(l * 2 + kv) * 2048 * 64, ap=[[64, 64], [4096, 32], [1, 64]])
                c.ld(w1[64 * g:64 * g + 64, :, 64 * g:64 * g + 64], src, "c_w1", [], [w1.name], eng="gpsimd")
                c.ld(w2[64 * g:64 * g + 64, 64 * g:64 * g + 64], d["phi_w2"][l, kv, :, :], "c_w2", [], [w2.name], eng="gpsimd")
                srcp = AP(tensor=d["phi_pos"].tensor, offset=(l * 2 + kv) * 2048, ap=[[1, 64], [64, 32]])
                c.ld(pos[64 * g:64 * g + 64, :], srcp, "c_pos", [], [pos.name], eng="gpsimd")
                srcb = AP(tensor=d["phi_b1"].tensor, offset=(l * 2 + kv) * 64, ap=[[1, 64], [1, 1]])
                c.ld(b1[64 * g:64 * g + 64, :], srcb, "c_b1", [], [b1.name])
                srcb = AP(tensor=d["phi_b2"].tensor, offset=(l * 2 + kv) * 64, ap=[[1, 64], [1, 1]])
                c.ld(b2[64 * g:64 * g + 64, :], srcb, "c_b2", [], [b2.name])
            for lp in range(32):
                c.mm(pb[:, kv:kv + 1], w1[:, lp, :], pos[:, lp:lp + 1], lp == 0, lp == 31, [w1.name, pos.name], ["pb1"])
            c.tt("vector", b1e[:], pb[:, kv:kv + 1], b1[:], ALU.add, ["pb1", b1.name], [b1e.name])
            W[kv] = dict(w1=w1, w2=w2, b1e=b1e, b2=b2)
        b2bc = c.sb(ph, "b2bc", [128, 128], F32)
        for g in range(2):
            c.ld(b2bc[:, 64 * g:64 * g + 64], d["phi_b2"][l, 1:2, :].partition_broadcast(128), "c_b2bc", [], [b2bc.name])
        W["b2bc"] = b2bc
        return W

    def compress(self, ph, W, ktc, vtc, n, ckT, cv_write, tag):
        c, kb = self, self.kb
        if "cp1" not in W or W.get("cp_scope") is not ph:
            W["cp1"] = c.ps(ph, "cps1", [128, 512], F32)
            W["cp2"] = c.ps(ph, "cps2", [128, 512], F32)
            W["cx"] = c.sb(ph, "cx", [128, 512], F32)
            W["ch1"] = c.sb(ph, "ch1", [128, 512], BF16)
            W["cu"] = c.sb(ph, "cu", [128, 512], F32)
            W["cp_scope"] = ph
        p1, p2 = W["cp1"], W["cp2"]
        for kv, src in ((0, ktc), (1, vtc)):
            w = W[kv]
            x = W["cx"]
            h1 = W["ch1"]
            for lp in range(32):
                c.mm(p1[:, 0:n], w["w1"][:, lp, :], src[:, lp:lp + 16 * (n - 1) + 1:16], lp == 0, lp == 31,
                     [w["w1"].name, src.name], [p1.name])
            c.act(x[:, 0:n], p1[:, 0:n], AF.Identity, [p1.name, w["b1e"].name], [x.name], bias=w["b1e"][:, 0:1])
            self.gelu_tanh(W["cu"], x, n, h1)
            if kv == 0:
                c.mm(p2[:, 0:n], w["w2"][:], h1[:, 0:n], True, True, [w["w2"].name, h1.name], [p2.name])
                c.act(ckT[:, 0:n], p2[:, 0:n], AF.Identity, [p2.name, w["b2"].name], [ckT.name], bias=w["b2"][:, 0:1])
            else:
                for nt_ in range((n + 127) // 128):
                    rows = min(128, n - nt_ * 128)
                    c.mm(p2[0:rows, 0:128], h1[:, nt_ * 128:nt_ * 128 + rows], w["w2"][:], True, True,
                         [w["w2"].name, h1.name], [p2.name])
                    cv_write(nt_, rows, p2)

    def phase_C_prompt(self, les, ph, W, l):
        c, kb, d, SEQ = self, self.kb, self.d, self.SEQ
        n = self.NCP
        NTL = (n + 127) // 128
        ckT, cvU = les
        kb.op("gpsimd", lambda e: e.memset(ckT[:], 0.0), writes=[ckT.name])
        kb.op("gpsimd", lambda e: e.memset(cvU[:], 0.0), writes=[cvU.name])
        kb.op("gpsimd", lambda e: e.memset(cvU[:, :, :, 64:65], 1.0), writes=[cvU.name])
        for g in range(2):
            c.ld(cvU[:, :, g, 65:129], d["ovl_p"][0:NTL * 128, :].rearrange("(t p) j -> p t j", p=128), "c_ovl", [], [cvU.name],
                 eng="gpsimd")
        ktc = c.sb(ph, "ktc_sb", [128, SEQ], BF16)
        vtc = c.sb(ph, "vtc_sb", [128, SEQ], BF16)
        c.ld(ktc[:], d["ktc"][:, 0:SEQ], "c_ktc", ["ktc"], [ktc.name])
        c.ld(vtc[:], d["vtc"][:, 0:SEQ], "c_vtc", ["vtc"], [vtc.name])

        def cv_write(nt_, rows, p2):
            c.tt("vector", cvU[0:rows, nt_, :, 0:64], p2[0:rows, 0:128].rearrange("p (g d) -> p g d", g=2),
                 W["b2bc"][0:rows, :].rearrange("p (g d) -> p g d", g=2), ALU.add, [p2.name, W["b2bc"].name], [cvU.name])
        self.compress(ph, W, ktc, vtc, n, ckT, cv_write, "p")
        return ckT, cvU

    def phaseA_setup(self, ph, l):
        c, kb, d, SEQ = self, self.kb, self.d, self.SEQ
        A = {}
        NSLC = SEQ // 64
        A["wupa"] = c.sb(ph, "wupa", [128, 4, D], BF16)
        A["wupb"] = c.sb(ph, "wupb", [128, 4, D], BF16)
        A["wout"] = c.sb(ph, "wout", [128, 8, D], BF16)
        for k in range(4):
            c.ld(A["wupa"][:, k, :], d["w_up_a"][l, k * 128:(k + 1) * 128, :], "a_w", [], ["wupa"], eng="gpsimd")
            c.ld(A["wupb"][:, k, :], d["w_up_b"][l, k * 128:(k + 1) * 128, :], "a_w", [], ["wupb"], eng="gpsimd")
        for k in range(8):
            c.ld(A["wout"][:, k, :], d["w_out"][l, k * 128:(k + 1) * 128, :], "a_w", [], ["wout"], eng="gpsimd")
        ew = c.sb(ph, "ewide", [64, SEQ], BF16)
        kb.op("gpsimd", lambda e: e.memset(ew[:], 1.0), writes=["ewide"])
        kb.op("gpsimd", lambda e: e.affine_select(out=ew[:], in_=ew[:], pattern=[[1, SEQ]], compare_op=ALU.is_ge, fill=0.0,
                                                    base=0, channel_multiplier=-64), reads=["ewide"], writes=["ewide"])
        kb.op("gpsimd", lambda e: e.affine_select(out=ew[:], in_=ew[:], pattern=[[-1, SEQ]], compare_op=ALU.is_ge, fill=0.0,
                                                    base=63, channel_multiplier=64), reads=["ewide"], writes=["ewide"])
        A["ewide"] = ew
        rs = c.sb(ph, "rsm", [16, 16, 128], BF16)
        kb.op("gpsimd", lambda e: e.memset(rs[:], 1.0), writes=["rsm"])
        kb.op("gpsimd", lambda e: e.affine_select(out=rs[:], in_=rs[:], pattern=[[-1, 16], [0, 128]], compare_op=ALU.is_equal,
                                                    fill=0.0, base=0, channel_multiplier=1), reads=["rsm"], writes=["rsm"])
        A["rsm"] = rs
        zc = c.sb(ph, "zc", [16, 640], F32)
        kb.op("gpsimd", lambda e: e.memset(zc[:], 0.0), writes=["zc"])
        kb.op("gpsimd", lambda e: e.affine_select(out=zc[:], in_=zc[:], pattern=[[1, 640]], compare_op=ALU.not_equal, fill=1.0,
                                                    base=-256, channel_multiplier=-1), reads=["zc"], writes=["zc"])
        A["zc"] = zc
        band = c.sb(ph, "band", [16, 16, 128], F32)
        p16 = VL + 16
        src = AP(tensor=d["sk16"].tensor, offset=VOFF + 97, ap=[[VL, 16], [128 * p16, 16], [1, 128]])
        c.ld(band[:], src, "a_band", [], ["band"])
        A["band"] = band
        anti = c.sb(ph, "anti", [128, 4, 128], BF16)
        kb.op("gpsimd", lambda e: e.memset(anti[:], 0.0), writes=["anti"])
        kb.op("gpsimd", lambda e: e.affine_select(out=anti[:], in_=anti[:], pattern=[[0, 4], [-1, 128]], compare_op=ALU.is_gt,
                                                    fill=NEG, base=0, channel_multiplier=1), reads=["anti"], writes=["anti"])
        A["anti"] = anti
        GW = NSLC + 64
        G = c.sb(ph, "G", [128, GW], F32)
        rel = c.sb(ph, "Grel", [128, GW], F32)
        t1 = c.sb(ph, "Gt1", [128, GW], F32)
        pcol = c.sb(ph, "Gp", [128, 1], F32)
        kb.op("gpsimd", lambda e: e.iota(rel[:], pattern=[[1, GW]], base=-NSLC, channel_multiplier=0,
                                         allow_small_or_imprecise_dtypes=True), writes=["Grel"])
        kb.op("gpsimd", lambda e: e.iota(pcol[:], pattern=[[0, 1]], base=0, channel_multiplier=1,
                                         allow_small_or_imprecise_dtypes=True), writes=["Gp"])
        c.ts("vector", pcol[:], pcol[:], 64.0, None, ALU.is_ge, None, ["Gp"], ["Gp"])
        c.ts("vector", rel[:], rel[:], pcol[:, 0:1], None, ALU.subtract, None, ["Grel", "Gp"], ["Grel"])
        c.ts("vector", G[:], rel[:], 0.0, 1e4, ALU.is_equal, ALU.mult, ["Grel"], ["G"])
        c.ts("vector", t1[:], rel[:], -1.0, 1e4, ALU.is_equal, ALU.mult, ["Grel"], ["Gt1"])
        c.tt("vector", G[:], G[:], t1[:], ALU.add, ["G", "Gt1"], ["G"])
        c.ts("vector", t1[:], rel[:], 0.0, -1e9, ALU.is_gt, ALU.mult, ["Grel"], ["Gt1"])
        c.tt("vector", G[:], G[:], t1[:], ALU.add, ["G", "Gt1"], ["G"])
        A["G"] = G
        zrow = c.sb(ph, "zrow", [1, 512], BF16)
        kb.op("gpsimd", lambda e: e.memset(zrow[:], 0.0), writes=["zrow"])
        A["zrow"] = zrow
        return A

    @staticmethod
    def run_stages(items):
        if not items:
            return
        k = len(items[0])
        n = len(items)
        for step in range(n + k - 1):
            for st in range(k):
                i = step - st
                if 0 <= i < n:
                    items[i][st]()

    @staticmethod
    def run_pipe(items, depth=1):
        n = len(items)
        for i in range(min(depth, n)):
            items[i][0]()
        for i in range(n):
            if i + depth < n:
                items[i + depth][0]()
            items[i][1]()

    def zero_bank(self, A, bank_ap, key):
        self.mm(bank_ap, A["zrow"][0:1, 0:128], A["zrow"][0:1, 0:512], True, False, ["zrow"], [key])

    def epilogue(self, A, S, l, nt, B, dst_rows):
        c, kb = self, self.kb
        L_AG, L_AZ, L_BZ, L_MA, L_MB = 512, 536, 1560, 2072, 3096
        qsb, xres = B["qsb"], B["xres"]
        P = slice(0, nt)
        W1, W2, W3 = B["w1"], B["w2"], B["w3"]
        k1, k2, k3 = W1.name, W2.name, W3.name
        kq = qsb.name
        sm = B["small"]
        ks = sm.name
        c.act(sm[P, 0:24], qsb[P, L_AG:L_AG + 24], AF.Exp, [kq], [ks], scale=-1.0)
        c.ts("vector", sm[P, 0:24], sm[P, 0:24], 1.0, None, ALU.add, None, [ks], [ks])
        kb.op("vector", lambda e: e.reciprocal(out=sm[P, 0:24], in_=sm[P, 0:24]), reads=[ks], writes=[ks])
        for bi, (nm, w) in enumerate((("Ocmp", 129), ("Oslc", 65), ("Owin", 65), ("Omob", 65))):
            c.ts("vector", sm[P, 24 + 8 * bi:32 + 8 * bi], B[nm][P, :, 64], 1e-30, None, ALU.max, None, [B[nm].name], [ks])
            kb.op("vector", lambda e, bi=bi: e.reciprocal(out=sm[P, 24 + 8 * bi:32 + 8 * bi], in_=sm[P, 24 + 8 * bi:32 + 8 * bi]),
                  reads=[ks], writes=[ks])
        sgv = sm[P, 0:24].rearrange("p (h b) -> p b h", b=3)
        c.tt("vector", sm[P, 24:48].rearrange("p (b h) -> p b h", b=3), sm[P, 24:48].rearrange("p (b h) -> p b h", b=3), sgv,
             ALU.mult, [ks], [ks])
        oa = W1[P, 0:512].rearrange("p (h d) -> p h d", h=8)
        tmp = W1[P, 512:1024].rearrange("p (h d) -> p h d", h=8)
        for bi, nm in enumerate(("Ocmp", "Oslc", "Owin")):
            cf = sm[P, 24 + 8 * bi:32 + 8 * bi].unsqueeze(2).broadcast_to([nt, 8, 64])
            if bi == 0:
                c.tt("vector", oa, B[nm][P, :, 0:64], cf, ALU.mult, [B[nm].name, ks], [k1])
            else:
                c.tt("vector", tmp, B[nm][P, :, 0:64], cf, ALU.mult, [B[nm].name, ks], [k1])
                c.tt("vector", oa, oa, tmp, ALU.add, [k1], [k1])
        for (lo, off) in ((L_AZ, 0), (L_BZ, 512)):
            c.act(W2[P, off:off + 512], qsb[P, lo:lo + 512], AF.Exp, [kq], [k2], scale=-1.0)
            c.ts("gpsimd", W2[P, off:off + 512], W2[P, off:off + 512], 1.0, None, ALU.add, None, [k2], [k2])
            kb.op("vector", lambda e, off=off: e.reciprocal(out=W2[P, off:off + 512], in_=W2[P, off:off + 512]), reads=[k2], writes=[k2])
            c.tt("gpsimd", W2[P, off:off + 512], W2[P, off:off + 512], qsb[P, lo:lo + 512], ALU.mult, [k2, kq], [k2])
        oz = B["oz"]
        c.tt("vector", oz[P, 0:512], W1[P, 0:512], W2[P, 0:512], ALU.mult, [k1, k2], [oz.name])
        ob = W1[P, 512:1024].rearrange("p (h d) -> p h d", h=8)
        c.tt("vector", ob, B["Omob"][P, :, 0:64], sm[P, 48:56].unsqueeze(2).broadcast_to([nt, 8, 64]), ALU.mult,
             [B["Omob"].name, ks, k1], [k1])
        c.tt("vector", oz[P, 512:1024], W1[P, 512:1024], W2[P, 512:1024], ALU.mult, [k1, k2], [oz.name])
        pq, ozT = B["pq"], B["ozT"]
        for k in range(8):
            c.tr(pq[:, k, 0:nt], oz[P, k * 128:(k + 1) * 128], self.ident_b[0:nt, 0:nt], [oz.name, "identb"], [pq.name])
        c.copy("scalar", ozT[:, :, 0:nt], pq[:, :, 0:nt], [pq.name], [ozT.name])
        pY = B["pY"]
        for br, wt in ((0, A["wupa"]), (1, A["wupb"])):
            for hf in range(2):
                o = (br * 2 + hf) * 512
                for k in range(4):
                    c.mm(pY[P, o:o + 512], ozT[:, br * 4 + k, 0:nt], wt[:, k, hf * 512:(hf + 1) * 512], k == 0, k == 3,
                         [ozT.name, wt.name.split("_")[0]], [pY.name])
        for (lo, Wt, kk) in ((L_MA, W1, k1), (L_MB, W2, k2)):
            c.act(Wt[P, :], qsb[P, lo:lo + 1024], AF.Exp, [kq], [kk], scale=-1.0)
            c.ts("gpsimd", Wt[P, :], Wt[P, :], 1.0, None, ALU.add, None, [kk], [kk])
            kb.op("vector", lambda e, Wt=Wt: e.reciprocal(out=Wt[P, :], in_=Wt[P, :]), reads=[kk], writes=[kk])
        c.tt("vector", W1[P, :], W1[P, :], pY[P, 0:1024], ALU.mult, [k1, pY.name], [k1])
        c.tt("vector", W2[P, :], W2[P, :], pY[P, 1024:2048], ALU.mult, [k2, pY.name], [k2])
        c.tt("gpsimd", oz[P, :], W1[P, :], W2[P, :], ALU.add, [k1, k2], [oz.name])
        for k in range(8):
            c.tr(pq[:, k, 0:nt], oz[P, k * 128:(k + 1) * 128], self.ident_b[0:nt, 0:nt], [oz.name, "identb"], [pq.name])
        c.copy("scalar", ozT[:, :, 0:nt], pq[:, :, 0:nt], [pq.name], [ozT.name])
        for hf in range(2):
            for k in range(8):
                c.mm(pY[P, hf * 512:(hf + 1) * 512], ozT[:, k, 0:nt], A["wout"][:, k, hf * 512:(hf + 1) * 512], k == 0, k == 7,
                     [ozT.name, "wout"], [pY.name])
        gate = B["gate"]
        c.tt("vector", W1[P, :], pY[P, 0:1024], gate[P, :], ALU.mult, [pY.name, gate.name], [k1])
        kb.op("vector", lambda e: e.scalar_tensor_tensor(out=W1[P, :], in0=xres[P, :], scalar=float(self.cfg["ALPHA"]), in1=W1[P, :],
                                                          op0=ALU.mult, op1=ALU.add), reads=[xres.name, k1], writes=[k1])
        st = B["stats"]
        for hf in range(2):
            kb.op("vector", lambda e, hf=hf: e.bn_stats(out=st[P, hf, :], in_=W1[P, hf * 512:(hf + 1) * 512]), reads=[k1], writes=[st.name])
        kb.op("vector", lambda e: e.bn_aggr(out=sm[P, 56:58], in_=st[P, :, :]), reads=[st.name], writes=[ks])
        c.act(sm[P, 58:59], sm[P, 57:58], AF.Ln, [ks, "epsc"], [ks], bias=self.epsc[P, 0:1])
        c.act(sm[P, 58:59], sm[P, 58:59], AF.Exp, [ks], [ks], scale=-0.5)
        c.ts("vector", W1[P, :], W1[P, :], sm[P, 56:57], sm[P, 58:59], ALU.subtract, ALU.mult, [k1, ks], [k1])
        c.tt("gpsimd", W1[P, :], W1[P, :], S["lng"][P, :], ALU.mult, [k1, "lng"], [k1])
        c.tt("gpsimd", W1[P, :], W1[P, :], S["lnb"][P, :], ALU.add, [k1, "lnb"], [k1])
        c.ld(dst_rows, W1[P, :], "a_y", [k1], [dst_rows.tensor.name])

    def phase_A_prompt(self, ph, A, S, l, ckT, cvU, ydst):
        c, kb, d, SEQ, NT = self, self.kb, self.d, self.SEQ, self.NT
        NSLC = SEQ // 64
        NCP = self.NCP
        B = {}
        B["qsb"] = c.sb(ph, "qsb", [128, 4120], F32)
        B["xres"] = c.sb(ph, "xres", [128, D], F32)
        B["w1"] = c.sb(ph, "wk1", [128, 1024], F32)
        B["w2"] = c.sb(ph, "wk2", [128, 1024], F32)
        B["w3"] = c.sb(ph, "wk3", [128, 1024], F32)
        B["small"] = c.sb(ph, "small", [128, 64], F32)
        B["oz"] = c.sb(ph, "oz", [128, 1024], BF16)
        B["ozT"] = c.sb(ph, "ozT", [128, 8, 128], BF16)
        B["stats"] = c.sb(ph, "stats", [128, 2, 6], F32)
        B["Ocmp"] = c.sb(ph, "Ocmp", [128, 8, 129], F32)
        B["Oslc"] = c.sb(ph, "Oslc", [128, 8, 65], F32)
        B["Owin"] = c.sb(ph, "Owin", [128, 8, 65], F32)
        B["Omob"] = c.sb(ph, "Omob", [128, 8, 65], F32)
        B["gate"] = S["gate_bc"]
        qT = c.sb(ph, "qT", [128, 8, 128], BF16)
        aqp = c.sb(ph, "aqp", [128, 4, 2, 64], BF16)
        bqb = c.sb(ph, "bqb", [128, 512], BF16)
        ET = [c.sb(ph, "ET", [128, 1024], BF16) for _ in range(2)]
        mb = c.sb(ph, "mb", [128, 2, 64], BF16)
        mbT = c.sb(ph, "mbT", [64, 2, 4, 128], BF16)
        impa = c.sb(ph, "impa", [128, 2, 64], F32)
        impr = c.sb(ph, "impr", [128, 64], F32)
        m8 = c.sb(ph, "m8", [128, 8, 8], F32)
        thr = c.sb(ph, "thr", [128, 8], F32)
        gsb = c.sb(ph, "gsb", [128, 8, 16], F32)
        mbm = c.sb(ph, "mbm", [128, 8, 16], BF16)
        mbmT = c.sb(ph, "mbmT", [16, 8, 128], BF16)
        kmeanT = c.sb(ph, "kmeanT", [128, 4, 16], BF16)
        kmsum = c.sb(ph, "kmsum", [128, 4], F32)
        kblk = c.sb(ph, "kblk", [128, 4, 256], BF16)
        NR = 3
        ksT = [c.sb(ph, "ksT", [128, 128], BF16) for _ in range(NR)]
        vsb = [c.sb(ph, "vsb", [128, 2, 65], BF16) for _ in range(NR)]
        kmT = [c.sb(ph, "kmT", [128, 4, 128], BF16) for _ in range(NR)]
        vmb = [c.sb(ph, "vmb", [128, 8, 65], BF16) for _ in range(NR)]
        for i in range(NR):
            kb.op("gpsimd", lambda e, i=i: e.memset(vsb[i][:, :, 64:65], 1.0), writes=[vsb[i].name])
            kb.op("gpsimd", lambda e, i=i: e.memset(vmb[i][:, :, 64:65], 1.0), writes=[vmb[i].name])
        pq = c.ps(ph, "pq", [128, 8, 128], BF16)
        pS = [c.ps(ph, "pS", [128, 512], F32) for _ in range(2)]
        pO = c.ps(ph, "pO", [128, 2048], F32)
        pm = c.ps(ph, "pm", [128, 1024], BF16)
        B["pq"], B["pY"] = pq, pO
        idb, idf = self.ident_b, self.ident_f
        ring = [0]
        eti = [0]
        kvr = [0]

        def evac_O(dst, nh_per_bank, w):
            for bnk in range((8 + nh_per_bank - 1) // nh_per_bank):
                h0 = bnk * nh_per_bank
                nh = min(nh_per_bank, 8 - h0)
                c.copy("vector", dst[:, h0:h0 + nh, :], pO[:, bnk * 512:bnk * 512 + nh * w].rearrange("p (h w) -> p h w", w=w),
                       [pO.name], [dst.name])

        def oslot(h, nh_per_bank, w):
            return pO[:, (h // nh_per_bank) * 512 + (h % nh_per_bank) * w:(h // nh_per_bank) * 512 + (h % nh_per_bank) * w + w]

        for t in range(NT):
            r0 = t * 128
            kq = B["qsb"].name
            c.ld(B["qsb"][:, 0:512], d["qs"][r0:r0 + 128, 0:512], "a_qs", ["qs"], [kq])
            c.ld(B["qsb"][:, 512:1560], d["qs"][r0:r0 + 128, C_AG:C_BQ + 512], "a_qs", ["qs"], [kq])
            c.ld(B["qsb"][:, 1560:4120], d["qs"][r0:r0 + 128, C_BZ:NIN], "a_qs", ["qs"], [kq])
            xsrc = d["xp"] if l == 0 else d["x1p"]
            c.ld(B["xres"][:, :], xsrc[r0:r0 + 128, :], "a_x", ["x1p"], [B["xres"].name])
            c.ts("vector", aqp[:].rearrange("p r g d -> p g r d"), B["qsb"][:, 0:512].rearrange("p (g r d) -> p g r d", g=2, r=4),
                 0.125, None, ALU.mult, None, [kq], [aqp.name])
            c.ts("vector", bqb[:], B["qsb"][:, 1048:1560], 0.125, None, ALU.mult, None, [kq], [bqb.name])
            for r in range(4):
                c.tr(pq[:, r, :], aqp[:, r, :, :].rearrange("p g d -> p (g d)"), idb[:], [aqp.name, "identb"], [pq.name])
                c.tr(pq[:, 4 + r, :], bqb[:, r * 128:(r + 1) * 128], idb[:], [bqb.name, "identb"], [pq.name])
            c.copy("scalar", qT[:], pq[:], [pq.name], [qT.name])
            Mt = min(8 * t + 7, NCP)
            ntiles = [(0, min(Mt, 128))] + ([(128, Mt - 128)] if Mt > 128 else [])
            for bnk in range(3):
                self.zero_bank(A, pO[:, bnk * 512:(bnk + 1) * 512], pO.name)
            items = []
            for g in range(2):
                for (n0, M) in ntiles:
                    ps_ = pS[ring[0] % 2]
                    ring[0] += 1
                    et = ET[eti[0] % 2]
                    eti[0] += 1

                    def stA(g=g, n0=n0, M=M, ps_=ps_, et=et):
                        lo_band = 8 * t - 8
                        has_band = (lo_band < n0 + M) and (8 * t + 6 >= n0)
                        c.mm(ps_[0:M, :], ckT[64 * g:64 * g + 64, n0:n0 + M], qT[64 * g:64 * g + 64, 0:4, :].rearrange("p r q -> p (r q)"),
                             True, not has_band, [ckT.name, qT.name], [ps_.name])
                        if has_band:
                            s0 = 256 - (lo_band - n0)
                            c.mm(ps_[0:M, :], A["zc"][:, s0:s0 + M], A["band"][:, 4 * g:4 * g + 4, :].rearrange("p r q -> p (r q)"),
                                 False, True, ["zc", "band"], [ps_.name])
                        c.act(et[0:M, 0:512], ps_[0:M, :], AF.Exp, [ps_.name], [et.name])

                    def stB(g=g, n0=n0, M=M, et=et):
                        for r in range(4):
                            h = 4 * g + r
                            c.mm(oslot(h, 3, 129)[:, :], et[0:M, r * 128:(r + 1) * 128], cvU[0:M, n0 // 128, g, :], False, False,
                                 [et.name, cvU.name], [pO.name])
                    items.append((stA, stB))
            self.run_pipe(items)
            evac_O(B["Ocmp"], 3, 129)
            ko = B["Ocmp"].name
            c.ts("vector", thr[:, :], B["Ocmp"][:, :, 64], 1e-30, None, ALU.max, None, [ko], [thr.name])
            kb.op("vector", lambda e: e.reciprocal(out=thr[:, :], in_=thr[:, :]), reads=[thr.name], writes=[thr.name])
            U = B["w3"][:, 0:512].rearrange("p (h j) -> p h j", h=8)
            c.tt("vector", U, B["Ocmp"][:, :, 65:129], thr[:, :].unsqueeze(2).broadcast_to([128, 8, 64]), ALU.mult,
                 [ko, thr.name], [B["w3"].name])
            for g in range(2):
                kb.op("vector", lambda e, g=g: e.tensor_reduce(out=impa[:, g, :], in_=B["w3"][:, g * 256:(g + 1) * 256].rearrange("p (r j) -> p j r", r=4),
                                                                   axis=AX.X, op=ALU.add), reads=[B["w3"].name], writes=[impa.name])
            c.tt("vector", impa[:], impa[:], A["G"][:, NSLC - 2 * t:NSLC - 2 * t + 64].unsqueeze(1).broadcast_to([128, 2, 64]), ALU.add,
                 [impa.name, "G"], [impa.name])
            c.ts("vector", impa[:, :, 0:1], impa[:, :, 0:1], 1e4, None, ALU.add, None, [impa.name], [impa.name])
            for g in range(2):
                if NSLC > 16:
                    kb.op("vector", lambda e, g=g: e.max(out=m8[:, 0, :], in_=impa[:, g, :]), reads=[impa.name], writes=[m8.name])
                    kb.op("vector", lambda e, g=g: e.match_replace(out=impr[:], in_to_replace=m8[:, 0, :], in_values=impa[:, g, :],
                                                                     imm_value=-2e9), reads=[impa.name, m8.name], writes=[impr.name])
                    kb.op("vector", lambda e: e.max(out=m8[:, 1, :], in_=impr[:]), reads=[impr.name], writes=[m8.name])
                    c.ts("vector", thr[:, g:g + 1], m8[:, 1, 7:8], -1e8, None, ALU.max, None, [m8.name], [thr.name])
                else:
                    kb.op("vector", lambda e, g=g: e.memset(thr[:, g:g + 1], -1e8), writes=[thr.name])
                c.ts("vector", mb[:, g, :], impa[:, g, :], thr[:, g:g + 1], NEG, ALU.is_lt, ALU.mult, [impa.name, thr.name], [mb.name])
                c.tr(pm[0:64, g * 128:(g + 1) * 128], mb[:, g, :], idb[:], [mb.name, "identb"], [pm.name])
            c.copy("vector", mbT[:], pm[0:64, 0:256].rearrange("p (g q) -> p g q", g=2).unsqueeze(2).broadcast_to([64, 2, 4, 128]),
                   [pm.name], [mbT.name])
            for (branch, ktsrc, vsrc, dstO) in (("slc", "kts", "vs", "Oslc"), ("win", "ktw", "vw", "Owin")):
                k_lo = 0 if branch == "slc" else max(0, t - 4)
                for bnk in range(2):
                    self.zero_bank(A, pO[:, bnk * 512:(bnk + 1) * 512], pO.name)
                items = []
                for kt in range(k_lo, t + 1):
                    ri = kvr[0] % NR
                    kvr[0] += 1
                    for g in range(2):
                        ps_ = pS[ring[0] % 2]
                        ring[0] += 1
                        et = ET[eti[0] % 2]
                        eti[0] += 1

                        def stA(kt=kt, g=g, ri=ri, ps_=ps_, et=et, branch=branch, ktsrc=ktsrc, vsrc=vsrc):
                            if g == 0:
                                c.ld(ksT[ri][:], d[ktsrc][:, kt * 128:(kt + 1) * 128], "a_k%d" % ri, [ktsrc], [ksT[ri].name])
                                c.ld(vsb[ri][:, :, 0:64], d[vsrc][kt * 128:(kt + 1) * 128, :].rearrange("p (g d) -> p g d", g=2),
                                     "a_v%d" % ri, [vsrc], [vsb[ri].name])
                            extra = []
                            if branch == "slc":
                                extra.append((A["ewide"][:, kt * 128:(kt + 1) * 128], mbT[:, g, :, :].rearrange("p r q -> p (r q)"),
                                              ["ewide", mbT.name]))
                            if kt == t:
                                extra.append((idf[:], self.diag[:, 4 * g:4 * g + 4, :].rearrange("p r q -> p (r q)"), ["identf", "diag"]))
                            elif kt == t - 1:
                                extra.append((idf[:], self.off[:, 4 * g:4 * g + 4, :].rearrange("p r q -> p (r q)"), ["identf", "off"]))
                            if branch == "win" and kt == t - 4:
                                extra.append((idb[:], A["anti"][:].rearrange("p r q -> p (r q)"), ["identb", "anti"]))
                            c.mm(ps_[:, :], ksT[ri][64 * g:64 * g + 64, :], qT[64 * g:64 * g + 64, 0:4, :].rearrange("p r q -> p (r q)"),
                                 True, len(extra) == 0, [ksT[ri].name, qT.name], [ps_.name])
                            for ei, (lt, rh, rd) in enumerate(extra):
                                c.mm(ps_[:, :], lt, rh, False, ei == len(extra) - 1, rd, [ps_.name])
                            c.act(et[:, 0:512], ps_[:, :], AF.Exp, [ps_.name], [et.name])

                        def stB(g=g, ri=ri, et=et):
                            for r in range(4):
                                h = 4 * g + r
                                c.mm(oslot(h, 4, 65)[:, :], et[:, r * 128:(r + 1) * 128], vsb[ri][:, g, :], False, False,
                                     [et.name, vsb[ri].name], [pO.name])
                        items.append((stA, stB))
                self.run_pipe(items)
                evac_O(B[dstO], 4, 65)
            nb = t // 2
            if t >= 2 and t % 2 == 0:
                n_new = nb - 1
                c.ld(kblk[:], d["ktm"][:, :, n_new * 256:(n_new + 1) * 256], "a_kblk", ["ktm"], [kblk.name])
                kb.op("vector", lambda e: e.tensor_reduce(out=kmsum[:], in_=kblk[:], axis=AX.X, op=ALU.add), reads=[kblk.name], writes=[kmsum.name])
                c.ts("vector", kmeanT[:, :, n_new], kmsum[:], 1.0 / 256.0, None, ALU.mult, None, [kmsum.name], [kmeanT.name])
            if nb > 0:
                pg_ = pS[ring[0] % 2]
                ring[0] += 1
                for h in range(8):
                    hp, hj = 64 * (h % 2), h // 2
                    c.mm(pg_[:, h * 16:h * 16 + nb], qT[hp:hp + 64, 4 + hj, :], kmeanT[hp:hp + 64, hj, 0:nb], True, True,
                         [qT.name, kmeanT.name], [pg_.name])
                kb.op("vector", lambda e: e.memset(gsb[:], -1e9), writes=[gsb.name])
                c.copy("vector", gsb[:, :, 0:nb], pg_[:, 0:128].rearrange("p (h n) -> p h n", h=8)[:, :, 0:nb], [pg_.name], [gsb.name])
                for h in range(8):
                    kb.op("vector", lambda e, h=h: e.max(out=m8[:, h, :], in_=gsb[:, h, :]), reads=[gsb.name], writes=[m8.name])
                c.ts("vector", thr[:, :], m8[:, :, 2], -1e8, None, ALU.max, None, [m8.name], [thr.name])
                c.tt("vector", gsb[:], gsb[:], thr[:, :].unsqueeze(2).broadcast_to([128, 8, 16]), ALU.is_lt, [gsb.name, thr.name], [gsb.name])
                c.ts("vector", mbm[:], gsb[:], NEG, None, ALU.mult, None, [gsb.name], [mbm.name])
                kb.op("vector", lambda e: e.memset(mbm[:, :, nb:16], 0.0), reads=[mbm.name], writes=[mbm.name])
                for h in range(8):
                    c.tr(pm[0:16, h * 128:(h + 1) * 128], mbm[:, h, :], idb[:], [mbm.name, "identb"], [pm.name])
                c.copy("vector", mbmT[:], pm[0:16, :].rearrange("p (h q) -> p h q", h=8), [pm.name], [mbmT.name])
            else:
                kb.op("vector", lambda e: e.memset(mbmT[:], 0.0), writes=[mbmT.name])
            for bnk in range(2):
                self.zero_bank(A, pO[:, bnk * 512:(bnk + 1) * 512], pO.name)
            items = []
            for kt in range(0, t + 1):
                ri = kvr[0] % NR
                kvr[0] += 1
                et = ET[eti[0] % 2]
                eti[0] += 1

                def stA(kt=kt, ri=ri, et=et):
                    c.ld(kmT[ri][:], d["ktm"][:, :, kt * 128:(kt + 1) * 128], "a_km%d" % ri, ["ktm"], [kmT[ri].name])
                    c.ld(vmb[ri][:, :, 0:64], d["vm"][kt * 128:(kt + 1) * 128, :].rearrange("p (h d) -> p h d", h=8), "a_vm%d" % ri,
                         ["vm"], [vmb[ri].name])
                    for bnk in range(2):
                        ps_ = pS[bnk]
                        c.mm(ps_[:, :], A["rsm"][:, kt // 2, :], mbmT[:, 4 * bnk:4 * bnk + 4, :].rearrange("p h q -> p (h q)"), True, False,
                             ["rsm", mbmT.name], [ps_.name])
                        for hh in range(4):
                            h = 4 * bnk + hh
                            hp, hj = 64 * (h % 2), h // 2
                            last = (hh == 3) and not (kt >= t - 1)
                            c.mm(ps_[:, hh * 128:(hh + 1) * 128], kmT[ri][hp:hp + 64, hj, :], qT[hp:hp + 64, 4 + hj, :], False, last,
                                 [kmT[ri].name, qT.name], [ps_.name])
                        if kt == t:
                            c.mm(ps_[:, :], idf[:], self.diag[:, 8 + 4 * bnk:12 + 4 * bnk, :].rearrange("p r q -> p (r q)"), False, True,
                                 ["identf", "diag"], [ps_.name])
                        elif kt == t - 1:
                            c.mm(ps_[:, :], idf[:], self.off[:, 8 + 4 * bnk:12 + 4 * bnk, :].rearrange("p r q -> p (r q)"), False, True,
                                 ["identf", "off"], [ps_.name])
                        c.act(et[:, bnk * 512:(bnk + 1) * 512], ps_[:, :], AF.Exp, [ps_.name], [et.name])

                def stB(ri=ri, et=et):
                    for h in range(8):
                        c.mm(oslot(h, 4, 65)[:, :], et[:, h * 128:(h + 1) * 128], vmb[ri][:, h, :], False, False,
                             [et.name, vmb[ri].name], [pO.name])
                items.append((stA, stB))
            self.run_pipe(items)
            evac_O(B["Omob"], 4, 65)
            if self.cfg.get("DBG_T") == t and l == 0:
                self.dump(0, B["Ocmp"][:].rearrange("p h w -> p (h w)"), B["Ocmp"].name, 8 * 129)
                self.dump(1, B["Oslc"][:].rearrange("p h w -> p (h w)"), B["Oslc"].name, 8 * 65)
                self.dump(2, B["Owin"][:].rearrange("p h w -> p (h w)"), B["Owin"].name, 8 * 65)
                self.dump(3, B["Omob"][:].rearrange("p h w -> p (h w)"), B["Omob"].name, 8 * 65)
                self.dump(4, impa[:].rearrange("p g j -> p (g j)"), impa.name, 128)
                self.dump(5, gsb[:].rearrange("p h n -> p (h n)"), gsb.name, 128)
            self.epilogue(A, S, l, 128, B, ydst[r0:r0 + 128, :])


    def sample_indices(self, les, l):
        c, kb, d, NS, NPG = self, self.kb, self.d, self.NS, self.NPG
        idxs = [[c.sb(les, "idx", [128, NPG], I32) for _ in range(3)] for _ in range(NS)]
        with ExitStack() as ph:
            pti = c.sb(ph, "pti", [128, NPG], I32)
            ptf = c.sb(ph, "ptf", [128, NPG], F32)
            io = c.sb(ph, "iop", [128, 1], F32)
            kb.op("gpsimd", lambda e: e.iota(io[:], pattern=[[0, 1]], base=l * 128, channel_multiplier=1,
                                             allow_small_or_imprecise_dtypes=True), writes=[io.name])
            for s_ in range(NS):
                idx = idxs[s_]
                c.ld(pti[:], d["pt"][s_:s_ + 1, :].partition_broadcast(128), "si_pt", [], [pti.name])
                c.copy("vector", ptf[:], pti[:], [pti.name], [ptf.name])
                c.ts("vector", ptf[:], ptf[:], float(self.DEPTH * 128), io[:, 0:1], ALU.mult, ALU.add, [ptf.name, io.name], [ptf.name])
                c.copy("vector", idx[0][:], ptf[:], [ptf.name], [idx[0].name])
                c.ts("vector", ptf[:], ptf[:], 2.0, None, ALU.mult, None, [ptf.name], [ptf.name])
                c.copy("vector", idx[1][:], ptf[:], [ptf.name], [idx[1].name])
                c.ts("vector", ptf[:], ptf[:], 1.0, None, ALU.add, None, [ptf.name], [ptf.name])
                c.copy("vector", idx[2][:], ptf[:], [ptf.name], [idx[2].name])
            kb.barrier()
        return idxs

    def gather(self, out, pool, idx, page, slot, wkey):
        self.kb.dma("gpsimd", lambda e: e.indirect_dma_start(out=out, out_offset=None, in_=pool[:, :],
                                                              in_offset=bass.IndirectOffsetOnAxis(ap=idx[:, page:page + 1], axis=0)),
                    slot, reads=[idx.name], writes=[wkey])

    def phase_C_sample(self, les, ph, W, l, idxs):
        c, kb, d, NS, NPG, PAST = self, self.kb, self.d, self.NS, self.NPG, self.PAST
        n = self.NCS
        NTS = (n + 127) // 128
        res = []
        ktc = c.sb(ph, "ktcs", [128, PAST], BF16)
        vtc = c.sb(ph, "vtcs", [128, PAST], BF16)
        pg = [c.sb(ph, "pgc", [128, 256], F32) for _ in range(3)]
        ptr = c.ps(ph, "ptrc", [128, 4, 128], F32)
        for s_ in range(NS):
            ckT, cvU = les[s_]
            kb.op("gpsimd", lambda e, ckT=ckT: e.memset(ckT[:], 0.0), writes=[ckT.name])
            kb.op("gpsimd", lambda e, cvU=cvU: e.memset(cvU[:], 0.0), writes=[cvU.name])
            kb.op("gpsimd", lambda e, cvU=cvU: e.memset(cvU[:, :, :, 64:65], 1.0), writes=[cvU.name])
            for page in range(NPG):
                b = pg[page % 3]
                self.gather(b[:], d["pcmp"], idxs[s_][0], page, "cs_pg%d" % (page % 3), b.name)
                if self.cfg.get("SC", 9) < 2:
                    continue
                for kv in range(2):
                    c.tr(ptr[:, kv, :], b[:, kv * 128:(kv + 1) * 128], self.ident_f[:], [b.name, "identf"], [ptr.name])
                c.copy("vector", ktc[:, page * 128:(page + 1) * 128], ptr[:, 0, :], [ptr.name], [ktc.name])
                c.copy("scalar", vtc[:, page * 128:(page + 1) * 128], ptr[:, 1, :], [ptr.name], [vtc.name])

            def cv_write(nt_, rows, p2, cvU=cvU):
                c.tt("vector", cvU[0:rows, nt_, :, 0:64], p2[0:rows, 0:128].rearrange("p (g d) -> p g d", g=2),
                     W["b2bc"][0:rows, :].rearrange("p (g d) -> p g d", g=2), ALU.add, [p2.name, W["b2bc"].name], [cvU.name])
            if self.cfg.get("SC", 9) >= 3:
                self.compress(ph, W, ktc, vtc, n, ckT, cv_write, "s%d" % s_)
            res.append((ckT, cvU))
        return res

    def phaseA_sample_setup(self, ph):
        c, kb, d, PAST, NPG = self, self.kb, self.d, self.PAST, self.NPG
        n = self.NCS
        NTS = (n + 127) // 128
        A = {}
        ov = c.sb(ph, "ovls", [128, NTS, 132], BF16)
        kb.op("gpsimd", lambda e: e.memset(ov[:], 0.0), writes=[ov.name])
        for nt_ in range(NTS):
            rows = min(128, n + 1 - nt_ * 128)
            c.ld(ov[0:rows, nt_, :], d["ovl_s"][nt_ * 128:nt_ * 128 + rows, :], "ss_ov", [], [ov.name], eng="gpsimd")
        A["ovl"] = ov
        cb = c.sb(ph, "cbias", [128, NTS, 8], F32)
        kb.op("gpsimd", lambda e: e.memset(cb[:], 0.0), writes=[cb.name])
        c0 = PAST - 31 - 2048 * (NTS - 1)
        p16 = VL + 16
        src = AP(tensor=d["sk16"].tensor, offset=VOFF + c0 - 16 * 96, ap=[[VL, 32], [128 * p16, 8]])
        c.ld(cb[96:128, NTS - 1, :], src, "ss_cb", [], [cb.name])
        A["cbias"] = cb
        e2 = c.sb(ph, "e2", [1, 3, 128], BF16)
        kb.op("gpsimd", lambda e: e.memset(e2[:], 0.0), writes=[e2.name])
        kb.op("gpsimd", lambda e: e.memset(e2[0:1, 0, 0:64], 1.0), writes=[e2.name])
        kb.op("gpsimd", lambda e: e.memset(e2[0:1, 1, 64:128], 1.0), writes=[e2.name])
        kb.op("gpsimd", lambda e: e.memset(e2[0:1, 2, 0:1], 1.0), writes=[e2.name])
        A["e2"] = e2
        negrow = c.sb(ph, "negrow", [1, 512], BF16)
        kb.op("gpsimd", lambda e: e.memset(negrow[:], 0.0), writes=[negrow.name])
        kb.op("gpsimd", lambda e: e.memset(negrow[0:1, 0:8], NEG), writes=[negrow.name])
        A["negrow"] = negrow
        off0 = c.sb(ph, "off0", [128, 16], F32)
        d0 = c.sb(ph, "d0", [1, 16], F32)
        c.copy("vector", off0[:], self.off[:, :, 0], ["off"], [off0.name])
        c.copy("vector", d0[:], self.diag[0:1, :, 0], ["diag"], [d0.name])
        A["off0"], A["d0"] = off0, d0
        return A

    def phase_A_sample(self, ph, A, AS, S, l, s_, idx, ckT, cvU, ydst, xsrc):
        c, kb, d, SEQ, NS, NPG, PAST = self, self.kb, self.d, self.SEQ, self.NS, self.NPG, self.PAST
        n = self.NCS
        NTS = (n + 127) // 128
        NSL = PAST // 64 + 1
        NBM = PAST // 256
        row = SEQ + s_
        idb, idf = self.ident_b, self.ident_f
        B = {}
        B["qsb"] = c.sb(ph, "qsbs", [1, 4120], F32)
        B["xres"] = c.sb(ph, "xress", [1, D], F32)
        B["gate"] = c.sb(ph, "gates", [1, D], F32)
        B["w1"] = c.sb(ph, "wk1s", [1, 1024], F32)
        B["w2"] = c.sb(ph, "wk2s", [1, 1024], F32)
        B["w3"] = c.sb(ph, "wk3s", [1, 1024], F32)
        B["small"] = c.sb(ph, "smalls", [1, 64], F32)
        B["oz"] = c.sb(ph, "ozs", [1, 1024], BF16)
        B["ozT"] = c.sb(ph, "ozTs", [128, 8, 128], BF16)
        B["stats"] = c.sb(ph, "statss", [1, 2, 6], F32)
        for nm in ("Ocmp", "Oslc", "Owin", "Omob"):
            B[nm] = c.sb(ph, nm + "s", [1, 8, 65], F32)
        pT = c.ps(ph, "pTs", [128, 8, 128], BF16)
        pY = c.ps(ph, "pYs", [128, 2048], F32)
        pS = c.ps(ph, "pSs", [128, 512], F32)
        pO = c.ps(ph, "pOs", [128, 512], F32)
        pD = c.ps(ph, "pDs", [128, 512], F32)
        B["pq"], B["pY"] = pT, pY
        kq = B["qsb"].name
        c.ld(B["qsb"][:, 0:512], d["qs"][row:row + 1, 0:512], "s_qs", ["qs"], [kq])
        c.ld(B["qsb"][:, 512:1560], d["qs"][row:row + 1, C_AG:C_BQ + 512], "s_qs", ["qs"], [kq])
        c.ld(B["qsb"][:, 1560:4120], d["qs"][row:row + 1, C_BZ:NIN], "s_qs", ["qs"], [kq])
        c.ld(B["xres"][:, :], xsrc[s_:s_ + 1, :], "s_x", ["x1s"], [B["xres"].name])
        c.ld(B["gate"][:, :], d["gsr"][s_:s_ + 1, :], "s_g", ["gsr"], [B["gate"].name])
        qf = c.sb(ph, "qbdf", [128, 40], F32)
        qbd = c.sb(ph, "qbd", [128, 40], BF16)
        kb.op("gpsimd", lambda e: e.memset(qf[:], 0.0), writes=[qf.name])
        for g in range(2):
            src = AP(tensor=d["qs"].tensor, offset=row * NIN + g * 256, ap=[[1, 64], [64, 4]])
            c.ld(qf[64 * g:64 * g + 64, 4 * g:4 * g + 4], src, "s_qb", ["qs"], [qf.name])
        for h in range(8):
            e_, j = h % 2, h // 2
            src = AP(tensor=d["qs"].tensor, offset=row * NIN + C_BQ + h * 64, ap=[[1, 64], [1, 1]])
            c.ld(qf[64 * e_:64 * e_ + 64, 8 + 8 * j + h:9 + 8 * j + h], src, "s_qb", ["qs"], [qf.name])
        c.ts("vector", qbd[:], qf[:], 0.125, None, ALU.mult, None, [qf.name], [qbd.name])
        kn = c.sb(ph, "kn", [128, 8], BF16)
        vn = c.sb(ph, "vn", [1, 768], BF16)
        c.ld(kn[:, 0:1], d["kts"][:, row:row + 1], "s_kn", ["kts"], [kn.name])
        c.ld(kn[:, 1:2], d["ktw"][:, row:row + 1], "s_kn", ["ktw"], [kn.name])
        c.ld(kn[:, 2:6], d["ktm"][:, :, row], "s_kn", ["ktm"], [kn.name])
        c.ld(vn[:, 0:128], d["vs"][row:row + 1, :], "s_vn", ["vs"], [vn.name])
        c.ld(vn[:, 128:256], d["vw"][row:row + 1, :], "s_vn", ["vw"], [vn.name])
        c.ld(vn[:, 256:768], d["vm"][row:row + 1, :], "s_vn", ["vm"], [vn.name])
        sc = c.sb(ph, "scr", [128, 512], F32)
        pt_ = [c.sb(ph, "ptile", [128, 16], BF16) for _ in range(3)]
        pown = c.sb(ph, "pown", [1, 16], BF16)
        so = c.sb(ph, "sown", [1, 16], F32)
        orow = c.sb(ph, "orow", [1, 528], F32)
        ring = [0]

        def own_scores(kcols, q0, qstep, bias_ap):
            for i, kc in enumerate(kcols):
                c.mm(pD[0:1, 0:8], kn[:, kc:kc + 1], qbd[:, q0 + i * qstep:q0 + i * qstep + 8], i == 0, i == len(kcols) - 1,
                     [kn.name, qbd.name], [pD.name])
            c.tt("vector", so[:, 0:8], pD[0:1, 0:8], bias_ap, ALU.add, [pD.name, AS["d0"].name], [so.name])
            c.act(pown[:, 0:8], so[:, 0:8], AF.Exp, [so.name], [pown.name])

        def finish_branch(dst, vcol0, per_head_v):
            for h in range(8):
                vo = vcol0 + (h * 64 if per_head_v else (h // 4) * 64)
                c.mm(pO[0:1, h * 64:(h + 1) * 64], pown[0:1, h:h + 1], vn[0:1, vo:vo + 64], False, False, [pown.name, vn.name], [pO.name])
            c.mm(pD[0:1, 16:24], self.ones_b[0:1, 0:1], pown[0:1, 0:8], False, False, ["onesb", pown.name], [pD.name])
            c.copy("vector", dst[0:1, :, 0:64], pO[0:1, 0:512].rearrange("p (h d) -> p h d", h=8), [pO.name], [dst.name])
            c.copy("vector", dst[0:1, :, 64], pD[0:1, 16:24], [pD.name], [dst.name])

        def open_banks():
            self.zero_bank(A, pO[:, :], pO.name)
            self.zero_bank(A, pD[:, :], pD.name)

        def pv_tile(p_ap, pkey, v_ap_of_head, vkey):
            for h in range(8):
                c.mm(pO[0:1, h * 64:(h + 1) * 64], p_ap[:, h:h + 1], v_ap_of_head(h), False, False, [pkey, vkey], [pO.name])
            c.mm(pD[0:1, 16:24], self.ones_b[:, 0:1], p_ap[:, 0:8], False, False, ["onesb", pkey], [pD.name])

        for nt_ in range(NTS):
            c.mm(pS[:, nt_ * 8:(nt_ + 1) * 8], ckT[:, nt_ * 128:(nt_ + 1) * 128], qbd[:, 0:8], True, True, [ckT.name, qbd.name], [pS.name])
        c.tt("vector", sc[:, 0:NTS * 8], pS[:, 0:NTS * 8], AS["cbias"][:].rearrange("p t h -> p (t h)"), ALU.add,
             [pS.name, AS["cbias"].name], [sc.name])
        ec = c.sb(ph, "ecT", [128, NTS * 8], BF16)
        c.act(ec[:], sc[:, 0:NTS * 8], AF.Exp, [sc.name], [ec.name])
        for h in range(8):
            g = h // 4
            for nt_ in range(NTS):
                c.mm(pO[0:1, h * 64:h * 64 + 64], ec[:, nt_ * 8 + h:nt_ * 8 + h + 1], cvU[:, nt_, g, 0:64], nt_ == 0, nt_ == NTS - 1,
                     [ec.name, cvU.name], [pO.name])
        for h in range(8):
            for nt_ in range(NTS):
                c.mm(pD[0:1, 16 + h:17 + h], ec[:, nt_ * 8 + h:nt_ * 8 + h + 1], cvU[:, nt_, 0, 64:65], nt_ == 0, nt_ == NTS - 1,
                     [ec.name, cvU.name], [pD.name])
        for h in range(8):
            o = (h // 3) * 512 + (h % 3) * 132
            for nt_ in range(NTS):
                c.mm(pY[0:1, o:o + 132], ec[:, nt_ * 8 + h:nt_ * 8 + h + 1], AS["ovl"][:, nt_, :], nt_ == 0, nt_ == NTS - 1,
                     [ec.name, AS["ovl"].name], [pY.name])
        c.copy("vector", B["Ocmp"][0:1, :, 0:64], pO[0:1, 0:512].rearrange("p (h d) -> p h d", h=8), [pO.name], [B["Ocmp"].name])
        c.copy("vector", B["Ocmp"][0:1, :, 64], pD[0:1, 16:24], [pD.name], [B["Ocmp"].name])
        rd = c.sb(ph, "rdc", [1, 8], F32)
        un = c.sb(ph, "un", [1, 8, 132], F32)
        impa = c.sb(ph, "impas", [1, 2, 136], F32)
        impr = c.sb(ph, "imprs", [1, 136], F32)
        m8 = c.sb(ph, "m8s", [1, 8, 8], F32)
        thr = c.sb(ph, "thrs", [1, 8], F32)
        mbs = c.sb(ph, "mbs", [1, 2, 136], F32)
        c.ts("vector", rd[:], B["Ocmp"][0:1, :, 64], 1e-30, None, ALU.max, None, [B["Ocmp"].name], [rd.name])
        kb.op("vector", lambda e: e.reciprocal(out=rd[:], in_=rd[:]), reads=[rd.name], writes=[rd.name])
        for bnk in range(3):
            h0 = bnk * 3
            nh = min(3, 8 - h0)
            c.tt("vector", un[0:1, h0:h0 + nh, :], pY[0:1, bnk * 512:bnk * 512 + nh * 132].rearrange("p (h j) -> p h j", j=132),
                 rd[0:1, h0:h0 + nh].unsqueeze(2).broadcast_to([1, nh, 132]), ALU.mult, [pY.name, rd.name], [un.name])
        kb.op("vector", lambda e: e.memset(impa[:], -1e9), writes=[impa.name])
        for g in range(2):
            kb.op("vector", lambda e, g=g: e.tensor_reduce(out=impa[0:1, g, 0:132], in_=un[0:1, 4 * g:4 * g + 4, :].rearrange("p r j -> p j r"),
                                                               axis=AX.X, op=ALU.add), reads=[un.name], writes=[impa.name])
        kb.op("vector", lambda e: e.memset(impa[0:1, :, NSL:136], -1e9), reads=[impa.name], writes=[impa.name])
        for j in (0, NSL - 2, NSL - 1):
            c.ts("vector", impa[0:1, :, j:j + 1], impa[0:1, :, j:j + 1], 1e4, None, ALU.add, None, [impa.name], [impa.name])
        for g in range(2):
            if NSL > 16:
                kb.op("vector", lambda e, g=g: e.max(out=m8[0:1, 0, :], in_=impa[0:1, g, :]), reads=[impa.name], writes=[m8.name])
                kb.op("vector", lambda e, g=g: e.match_replace(out=impr[:], in_to_replace=m8[0:1, 0, :], in_values=impa[0:1, g, :],
                                                                 imm_value=-2e9), reads=[impa.name, m8.name], writes=[impr.name])
                kb.op("vector", lambda e: e.max(out=m8[0:1, 1, :], in_=impr[:]), reads=[impr.name], writes=[m8.name])
                c.ts("vector", thr[0:1, g:g + 1], m8[0:1, 1, 7:8], -1e8, None, ALU.max, None, [m8.name], [thr.name])
            else:
                kb.op("vector", lambda e, g=g: e.memset(thr[0:1, g:g + 1], -1e8), writes=[thr.name])
            c.ts("vector", mbs[0:1, g, :], impa[0:1, g, :], thr[0:1, g:g + 1], NEG, ALU.is_lt, ALU.mult, [impa.name, thr.name], [mbs.name])
        if self.cfg.get("SDBG", 9) == 3:
            return
        mrow = c.sb(ph, "mrow", [1, 2, 512], BF16)
        for jj in range(2):
            c.copy("vector", mrow[0:1, jj, 0:NPG * 8].rearrange("p (k g r) -> p k g r", g=2, r=4),
                   mbs[0:1, :, jj:jj + 2 * NPG - 1:2].rearrange("p g k -> p k g").unsqueeze(3).broadcast_to([1, NPG, 2, 4]),
                   [mbs.name], [mrow.name])
        own_scores([0], 0, 0, AS["d0"][0:1, 0:8])
        open_banks()
        W_ = NPG * 8
        c.mm(pS[:, 0:W_], AS["e2"][0:1, 0, :], mrow[0:1, 0, 0:W_], True, False, [AS["e2"].name, mrow.name], [pS.name])
        c.mm(pS[:, 0:W_], AS["e2"][0:1, 1, :], mrow[0:1, 1, 0:W_], False, False, [AS["e2"].name, mrow.name], [pS.name])
        pgs = [c.sb(ph, "pgs", [128, 256], F32) for _ in range(3)]
        pgw = [c.sb(ph, "pgw", [128, 256], BF16) for _ in range(3)]
        vsl = [c.sb(ph, "vsl", [128, 128], BF16) for _ in range(3)]
        kst = [c.sb(ph, "kst", [128, 128], BF16) for _ in range(2)]
        pTf = pY[:, 1536:2048].rearrange("p (j k) -> p j k", j=4)
        items = []
        for page in range(NPG):
            b = pgs[page % 3]
            k_ = kst[page % 2]
            v_ = vsl[page % 3]
            p_ = pt_[page % 3]

            def s0(page=page, b=b, k_=k_, v_=v_):
                self.gather(b[:], d["pslc"], idx[0], page, "s_pg%d" % (page % 3), b.name)
                c.tr(pTf[:, page % 4, :], b[:, 0:128], idf[:], [b.name, "identf"], [pY.name])
                c.copy("vector", k_[:], pTf[:, page % 4, :], [pY.name], [k_.name])
                c.copy("scalar", v_[:], b[:, 128:256], [b.name], [v_.name])

            def s1(page=page, k_=k_, p_=p_):
                c.mm(pS[:, page * 8:(page + 1) * 8], k_[:], qbd[:, 0:8], False, False, [k_.name, qbd.name], [pS.name])
                if page == NPG - 1:
                    c.tt("vector", sc[:, 0:8], pS[:, page * 8:(page + 1) * 8], AS["off0"][:, 0:8], ALU.add, [pS.name, AS["off0"].name], [sc.name])
                    c.act(p_[:, 0:8], sc[:, 0:8], AF.Exp, [sc.name], [p_.name])
                else:
                    c.act(p_[:, 0:8], pS[:, page * 8:(page + 1) * 8], AF.Exp, [pS.name], [p_.name])

            def s2(p_=p_, v_=v_):
                pv_tile(p_, p_.name, lambda h, v_=v_: v_[:, (h // 4) * 64:64 + (h // 4) * 64], v_.name)
            items.append((s0, s1, s2))
        self.run_stages(items)
        finish_branch(B["Oslc"], 0, False)
        if self.cfg.get("SDBG", 9) == 4:
            return
        own_scores([1], 0, 0, AS["d0"][0:1, 0:8])
        open_banks()
        c.mm(pS[:, 0:32], AS["e2"][0:1, 2, :], AS["negrow"][0:1, 0:32], True, False, [AS["e2"].name, AS["negrow"].name], [pS.name])
        for wt in range(4):
            b = pgw[wt % 3]
            k_ = kst[wt % 2]
            c.ld(b[:], d["wst"][l, s_, wt * 128:(wt + 1) * 128, :], "s_pw%d" % (wt % 3), [], [b.name], eng="gpsimd")
            c.tr(pT[:, 0, :], b[:, 0:128], idb[:], [b.name, "identb"], [pT.name])
            c.copy("vector", k_[:], pT[:, 0, :], [pT.name], [k_.name])
            c.mm(pS[:, wt * 8:(wt + 1) * 8], k_[:], qbd[:, 0:8], False, False, [k_.name, qbd.name], [pS.name])
            p_ = pt_[ring[0] % 3]
            ring[0] += 1
            if wt == 3:
                c.tt("vector", sc[:, 0:8], pS[:, wt * 8:(wt + 1) * 8], AS["off0"][:, 0:8], ALU.add, [pS.name, AS["off0"].name], [sc.name])
                c.act(p_[:, 0:8], sc[:, 0:8], AF.Exp, [sc.name], [p_.name])
            else:
                c.act(p_[:, 0:8], pS[:, wt * 8:(wt + 1) * 8], AF.Exp, [pS.name], [p_.name])
            pv_tile(p_, p_.name, lambda h, b=b: b[:, 128 + (h // 4) * 64:192 + (h // 4) * 64], b.name)
        finish_branch(B["Owin"], 128, False)
        if self.cfg.get("SDBG", 9) == 5:
            return
        pgm = [c.sb(ph, "pgm", [128, 512], F32) for _ in range(3)]
        vmo = [c.sb(ph, "vmo", [128, 512], BF16) for _ in range(3)]
        kmt = [c.sb(ph, "kmts", [128, 4, 128], BF16) for _ in range(2)]
        items = []
        for page in range(NPG):
            b = pgm[page % 3]
            k_ = kmt[page % 2]

            def s0(page=page, b=b, k_=k_):
                self.gather(b[:], d["pmoba"], idx[1], page, "s_pm%d" % (page % 3), b.name)
                for j in range(4):
                    c.tr(pTf[:, j, :], b[:, j * 128:(j + 1) * 128], idf[:], [b.name, "identf"], [pY.name])
                c.copy("vector" if page % 2 == 0 else "scalar", k_[:], pTf[:, 0:4, :], [pY.name], [k_.name])

            def s1(page=page, k_=k_):
                for j in range(4):
                    c.mm(pS[:, page * 8:(page + 1) * 8], k_[:, j, :], qbd[:, 8 + 8 * j:16 + 8 * j], j == 0, j == 3, [k_.name, qbd.name], [pS.name])
            items.append((s0, s1))
        self.run_stages(items)
        c.copy("vector", sc[:, 0:W_], pS[:, 0:W_], [pS.name], [sc.name])
        c.mm(pD[0:1, 0:W_], self.ones_f[:, 0:1], sc[:, 0:W_], True, True, ["onesf", sc.name], [pD.name])
        NBP = max(NBM, 8)
        gsm = c.sb(ph, "gsm", [1, 8, NBP], F32)
        kb.op("vector", lambda e: e.memset(gsm[:], -1e9), writes=[gsm.name])
        c.copy("vector", orow[0:1, 0:W_], pD[0:1, 0:W_], [pD.name], [orow.name])
        pdv = orow[0:1, 0:W_].rearrange("p (n two h) -> p h n two", two=2, h=8)
        c.tt("vector", gsm[0:1, :, 0:NBM], pdv[:, :, :, 0], pdv[:, :, :, 1], ALU.add, [orow.name], [gsm.name])
        for h in range(8):
            kb.op("vector", lambda e, h=h: e.max(out=m8[0:1, h, :], in_=gsm[0:1, h, :]), reads=[gsm.name], writes=[m8.name])
        c.ts("vector", thr[0:1, :], m8[0:1, :, 2], -1e8, None, ALU.max, None, [m8.name], [thr.name])
        c.tt("vector", gsm[:], gsm[:], thr[0:1, :].unsqueeze(2).broadcast_to([1, 8, NBP]), ALU.is_lt, [gsm.name, thr.name], [gsm.name])
        c.ts("vector", mrow[0:1, 0, 0:W_].rearrange("p (n two h) -> p n two h", two=2, h=8),
             gsm[0:1, :, 0:NBM].rearrange("p h n -> p n h").unsqueeze(2).broadcast_to([1, NBM, 2, 8]), NEG, None, ALU.mult, None,
             [gsm.name], [mrow.name])
        c.mm(pY[:, 0:W_], self.ones_b[0:1, :], mrow[0:1, 0, 0:W_], True, True, ["onesb", mrow.name], [pY.name])
        c.tt("vector", sc[:, 0:W_], sc[:, 0:W_], pY[:, 0:W_], ALU.add, [sc.name, pY.name], [sc.name])
        c.tt("vector", sc[:, W_ - 8:W_], sc[:, W_ - 8:W_], AS["off0"][:, 8:16], ALU.add, [sc.name, AS["off0"].name], [sc.name])
        pm_ = c.sb(ph, "pmT", [128, 512], BF16)
        c.act(pm_[:, 0:W_], sc[:, 0:W_], AF.Exp, [sc.name], [pm_.name])
        own_scores([2, 3, 4, 5], 8, 8, AS["d0"][0:1, 8:16])
        open_banks()
        items = []
        for page in range(NPG):
            b = pgm[page % 3]
            v_ = vmo[page % 3]

            def s0(page=page, b=b, v_=v_):
                self.gather(b[:], d["pmoba"], idx[2], page, "s_pm%d" % (page % 3), b.name)
                c.copy(("vector", "scalar")[page % 2], v_[:], b[:], [b.name], [v_.name])

            def s1(page=page, v_=v_):
                pv_tile(pm_[:, page * 8:(page + 1) * 8], pm_.name, lambda h, v_=v_: v_[:, h * 64:(h + 1) * 64], v_.name)
            items.append((s0, s1))
        self.run_stages(items)
        finish_branch(B["Omob"], 256, True)
        if self.cfg.get("SDBG", 9) == 6:
            return
        self.epilogue(A, S, l, 1, B, ydst[s_:s_ + 1, :])

    def build(self):
        c = self
        self.declare()
        with ExitStack() as es:
            self.kb = KB(self.nc, es)
            kb = self.kb
            self.constants(es)
            self.epsc = c.sb(es, "epsc", [128, 1], F32)
            kb.op("gpsimd", lambda e: e.memset(self.epsc[:], LN_EPS), writes=["epsc"])
            stop = self.cfg.get("STOP")
            for l in range(self.DEPTH):
                last = (l == self.DEPTH - 1) or stop == "A%d" % l
                with ExitStack() as les:
                    S = self.phase_S(les, l)
                    kb.barrier()
                    self.phase_P(l, S)
                    kb.barrier()
                    if stop == "P%d" % l:
                        break
                    NTL = (self.NCP + 127) // 128
                    NTS = (self.NCS + 127) // 128
                    pre_p = (c.sb(les, "ckT", [128, NTL * 128], BF16), c.sb(les, "cvU", [128, NTL, 2, 129], BF16))
                    pre_s = [(c.sb(les, "ckTs", [128, NTS * 128], BF16), c.sb(les, "cvUs", [128, NTS, 2, 65], BF16))
                             for _ in range(self.NS)]
                    idxs = self.sample_indices(les, l)
                    with ExitStack() as ph:
                        W = self.load_phi(ph, l)
                        with ExitStack() as ph2:
                            ckT, cvU = self.phase_C_prompt(pre_p, ph2, W, l)
                            kb.barrier()
                        with ExitStack() as ph2:
                            scomp = pre_s
                            if self.cfg.get("SDBG", 9) >= 1:
                                scomp = self.phase_C_sample(pre_s, ph2, W, l, idxs)
                            kb.barrier()
                    kb.barrier()
                    with ExitStack() as ph:
                        A = self.phaseA_setup(ph, l)
                        with ExitStack() as ph2:
                            if not self.cfg.get("PSKIP"):
                                self.phase_A_prompt(ph2, A, S, l, ckT, cvU, self.d["yp"] if last else self.d["x1p"])
                            kb.barrier()
                        with ExitStack() as ph2:
                            AS = self.phaseA_sample_setup(ph2)
                            for s_ in range(self.NS if self.cfg.get("SDBG", 9) >= 2 else 0):
                                with ExitStack() as ph3:
                                    self.phase_A_sample(ph3, A, AS, S, l, s_, idxs[s_], scomp[s_][0], scomp[s_][1],
                                                        self.d["ys"] if last else self.d["x1s"], self.d["xs"] if l == 0 else self.d["x1s"])
                                    kb.barrier()
                    kb.barrier()
                    if stop == "A%d" % l:
                        break
                kb.barrier()
            kb.barrier()
            if self.cfg.get("SLOW"):
                with ExitStack() as ph:
                    big = c.sb(ph, "slowbig", [128, 4096], F32)
                    kb.op("gpsimd", lambda e: e.memset(big[:], 0.0), writes=[big.name])
                    for _ in range(int(self.cfg["SLOW"])):
                        c.act(big[:], big[:], AF.Exp, [big.name], [big.name], scale=-1.0)
                    kb.barrier()
            print("instructions:", kb.ninstr, flush=True)
        return self.nc


def _host_constants(cfg):
    dist = np.arange(VL) - VOFF
    n = np.maximum(dist, 0)
    nf = np.maximum(n, 1).astype(np.float32)
    large = 16 + (np.log(nf / np.float32(16)) / np.float32(math.log(128 / 16)) * np.float32(16)).astype(np.int32)
    b = np.where(n < 16, n, np.minimum(large, 31))
    bkt = np.zeros((33, VL), np.float32)
    bkt[b, np.arange(VL)] = 1.0
    bkt[:, dist < 0] = 0.0
    bkt[32, dist < 0] = 1.0

    def ovl(n_cmp, n_slc):
        i = np.arange(n_cmp)[:, None]
        j = np.arange(n_slc)[None, :]
        return sum(((i + u) // 4 == j).astype(np.float32) for u in range(2))

    SEQ, PAST = cfg["SEQ"], cfg["PAST"]
    ncp = SEQ // 16 - 1
    ncs = PAST // 16 - 1
    ovp = np.zeros((256, 64), np.float32)
    o = ovl(ncp, SEQ // 64)
    ovp[:min(ncp, 256), :o.shape[1]] = o[:256, :64]
    ovs = np.zeros((ncs + 1, 132), np.float32)
    o = ovl(ncs, PAST // 64 + 1)
    ovs[:ncs, :o.shape[1]] = o[:, :132]
    return dict(bkt=bkt, ovl_p=ovp, ovl_s=ovs)


_CACHE = {}


def run(inputs, cfg):
    key = tuple(sorted((k, str(v)) for k, v in cfg.items()))
    if key not in _CACHE:
        _CACHE[key] = Builder(cfg).build()
    nc = _CACHE[key]
    SEQ, PAST, NS, NPHYS, DEPTH, BATCH = (cfg[k] for k in ("SEQ", "PAST", "NS", "NPHYS", "DEPTH", "BATCH"))
    f = lambda a: np.ascontiguousarray(np.asarray(a))
    hc = _host_constants(cfg)
    pcmp = f(inputs["cache_nsa_cmp"]).reshape(NPHYS * DEPTH * 128, 256)
    pslc = f(inputs["cache_nsa_slc"]).reshape(NPHYS * DEPTH * 128, 256)
    pmoba = f(inputs["cache_moba"]).reshape(NPHYS * DEPTH * 128 * 2, 512)
    shared = {k: f(inputs[k]) for k in ("w_ada", "b_ada", "w_in", "phi_pos", "phi_w1", "phi_b1", "phi_w2", "phi_b2",
                                        "w_up_a", "w_up_b", "w_out", "ln_g", "ln_b")}
    shared["relb"] = f(inputs["rel_bias"])
    shared.update(hc)
    shared.update(pcmp=pcmp, pslc=pslc, pmoba=pmoba)
    in_maps = []
    for c in range(8):
        b = c % BATCH
        ss = slice(c * NS, (c + 1) * NS)
        m = dict(shared)
        m["xp"] = f(inputs["x_prompt"][b])
        m["xs"] = f(inputs["x_sample"][ss, 0])
        m["wst"] = f(inputs["state_nsa_win"][:, ss]).reshape(DEPTH, NS, 512, 256)
        m["pt"] = f(inputs["page_table"][ss]).astype(np.int32)
        m["c5"] = f(np.concatenate([inputs["c_prompt"][b:b + 1], inputs["c_sample"][ss]], 0))
        in_maps.append(m)
    res = run_bass_kernel_spmd(nc, in_maps, core_ids=list(range(8))).results
    DB = 8 * NS
    yp = np.stack([res[b]["yp"] for b in range(BATCH)], 0)
    ys = np.concatenate([res[c]["ys"] for c in range(8)], 0).reshape(DB, 1, D)
    def pr(nm, h):
        return np.stack([res[b][nm] for b in range(BATCH)], 0).reshape(BATCH, DEPTH, SEQ, 2, h, 64)
    def sm(nm, h):
        return np.concatenate([res[c][nm] for c in range(8)], 0).reshape(DB, DEPTH, 1, 2, h, 64)
    nwp = np.stack([res[b]["nwp"] for b in range(BATCH)], 1).reshape(DEPTH, BATCH, 512, 2, 2, 64)
    nws = np.concatenate([res[c]["nws"] for c in range(8)], 1).reshape(DEPTH, DB, 1, 2, 2, 64)
    global LAST_DBG
    LAST_DBG = res[0].get("dbg")
    return (yp, ys, pr("ncp", 2), sm("ncs", 2), pr("nsp", 2), sm("nss", 2), pr("nmp", 8), sm("nms", 8), nwp, nws)


def kernel(**inputs):
    return run(inputs, dict(DEFAULT_CFG))
```

```python
import math
from contextlib import ExitStack

import numpy as np
import ml_dtypes

import concourse.bass as bass
import concourse.mybir as mybir
from concourse.bass_types import AP
from concourse.bass_utils import run_bass_kernel_spmd

F32 = mybir.dt.float32
BF16 = mybir.dt.bfloat16
I32 = mybir.dt.int32
AF = mybir.ActivationFunctionType
ALU = mybir.AluOpType
AX = mybir.AxisListType

D = 1024
NIN = 5912
C_AQ, C_KC, C_VC, C_KS, C_VS, C_KW, C_VW, C_AG, C_AZ, C_BQ, C_BK, C_BV, C_BZ, C_MA, C_MB = (
    0, 512, 640, 768, 896, 1024, 1152, 1280, 1304, 1816, 2328, 2840, 3352, 3864, 4888)
NEG = -30000.0
LN_EPS = 1e-5
VL = 1024
VOFF = 400

import os
NO_PE_SKIP = not bool(os.environ.get("PE_SKIP"))
DEFAULT_CFG = dict(SEQ=4096, PAST=8192, NS=4, NPHYS=2560, DEPTH=2, BATCH=4, ALPHA=(2 * 2) ** 0.25, STOP=None)


class KB:
    ENGS = ["tensor", "vector", "scalar", "gpsimd", "sync"]

    def __init__(self, nc, es):
        self.nc = nc
        self.es = es
        self.sem = {e: es.enter_context(nc.semaphore("s_" + e)) for e in self.ENGS}
        self.cnt = {e: 0 for e in self.ENGS}
        self.waited = {}
        self.lastw = {}
        self.reads = {}
        self.dsem = {}
        self.ninstr = 0
        self.lastfull = {}
        self.excl = set()

    def _eng(self, eng):
        return getattr(self.nc, eng)

    def _wait(self, eng, s, c):
        if self.waited.get((eng, s), 0) >= c:
            return
        self.waited[(eng, s)] = c
        semh = s[1] if isinstance(s, tuple) else self.sem[s]
        self._eng(eng).wait_ge(semh, c)

    def _deps(self, eng, reads, writes, full=False):
        need = {}
        for k in reads:
            if k in self.lastw:
                s, c = self.lastw[k]
                need[s] = max(need.get(s, 0), c)
        for k in writes:
            if k in self.lastw:
                s, c = self.lastw[k]
                if not (full and eng == "tensor" and s == "tensor" and self.lastfull.get(k)):
                    need[s] = max(need.get(s, 0), c)
            for (s, c) in self.reads.get(k, ()):
                need[s] = max(need.get(s, 0), c)
        for s, c in need.items():
            if eng == "tensor" and s == "tensor" and not NO_PE_SKIP:
                continue
            self._wait(eng, s, c)

    def _mark(self, tok, reads, writes):
        for k in writes:
            self.lastw[k] = tok
            self.reads[k] = []
        for k in reads:
            lst = self.reads.setdefault(k, [])
            lst[:] = [x for x in lst if x[0] != tok[0]]
            lst.append(tok)

    def _split(self, reads, writes):
        if not self.excl:
            return reads, writes
        r2 = [k for k in reads if k not in self.excl]
        w2 = list(writes) + [k for k in reads if k in self.excl and k not in writes]
        return r2, w2

    def op(self, eng, fn, reads=(), writes=(), full=False):
        reads, writes = self._split(reads, writes)
        self._deps(eng, reads, writes, full)
        self.cnt[eng] += 1
        fn(self._eng(eng)).then_inc(self.sem[eng], 1)
        self._mark((eng, self.cnt[eng]), reads, writes)
        for k in writes:
            self.lastfull[k] = bool(full and eng == "tensor")
        self.ninstr += 1

    def dma(self, eng, fn, slot, reads=(), writes=()):
        if slot not in self.dsem:
            self.dsem[slot] = [self.es.enter_context(self.nc.semaphore("d%d" % len(self.dsem))), 0]
        reads, writes = self._split(reads, writes)
        self._deps(eng, reads, writes)
        d = self.dsem[slot]
        d[1] += 16
        fn(self._eng(eng)).then_inc(d[0], 16)
        self._mark((("d", d[0]), d[1]), reads, writes)
        for k in writes:
            self.lastfull[k] = False
        self.ninstr += 1

    def barrier(self):
        for e in self.ENGS:
            for slot, (semh, c) in self.dsem.items():
                if c:
                    self._wait(e, ("d", semh), c)
            for e2 in self.ENGS:
                if e2 != e and self.cnt[e2]:
                    self._wait(e, e2, self.cnt[e2])
        self.lastw = {}
        self.reads = {}


class Builder:
    def __init__(self, cfg):
        self.cfg = cfg
        self.SEQ = cfg["SEQ"]
        self.PAST = cfg["PAST"]
        self.NS = cfg["NS"]
        self.NPHYS = cfg["NPHYS"]
        self.DEPTH = cfg["DEPTH"]
        self.NT = self.SEQ // 128
        self.NPG = self.PAST // 128
        self.NTOK = self.SEQ + self.NS
        self.NCP = self.SEQ // 16 - 1
        self.NCS = self.PAST // 16 - 1
        self.nc = bass.Bass("TRN2", target_bir_lowering=False)
        self.uid = 0

    def dram(self, name, shape, dt, kind):
        return self.nc.dram_tensor(name, list(shape), dt, kind=kind).ap()

    def sb(self, es, name, shape, dt):
        self.uid += 1
        return es.enter_context(self.nc.sbuf_tensor("%s_%d" % (name, self.uid), list(shape), dt))

    def ps(self, es, name, shape, dt=F32):
        self.uid += 1
        nbytes = int(np.prod(shape[1:])) * (4 if dt == F32 else 2)
        assert nbytes % 2048 == 0, (name, shape)
        t = es.enter_context(self.nc.psum_tensor("%s_%d" % (name, self.uid), list(shape), dt))
        self.kb.excl.add(t.name)
        return t

    def mm(self, out, lhsT, rhs, start, stop, reads, writes):
        full = (lhsT.shape[0] == 128) and (lhsT.dtype != F32)
        self.kb.op("tensor", lambda e: e.matmul(out, lhsT=lhsT, rhs=rhs, start=start, stop=stop),
                   reads=reads, writes=writes, full=full)

    def tr(self, out, in_, ident, reads, writes):
        self.kb.op("tensor", lambda e: e.transpose(out=out, in_=in_, identity=ident), reads=reads, writes=writes)

    def act(self, out, in_, func, reads, writes, bias=None, scale=None):
        kw = {}
        if bias is not None:
            kw["bias"] = bias
        if scale is not None:
            kw["scale"] = scale
        self.kb.op("scalar", lambda e: e.activation(out=out, in_=in_, func=func, **kw), reads=reads, writes=writes)

    def copy(self, eng, out, in_, reads, writes):
        if eng == "scalar":
            self.kb.op("scalar", lambda e: e.copy(out=out, in_=in_), reads=reads, writes=writes)
        else:
            self.kb.op(eng, lambda e: e.tensor_copy(out=out, in_=in_), reads=reads, writes=writes)

    def ts(self, eng, out, in0, s1, s2, op0, op1, reads, writes):
        if op1 is None:
            self.kb.op(eng, lambda e: e.tensor_scalar(out=out, in0=in0, scalar1=s1, scalar2=None, op0=op0),
                       reads=reads, writes=writes)
        else:
            self.kb.op(eng, lambda e: e.tensor_scalar(out=out, in0=in0, scalar1=s1, scalar2=s2, op0=op0, op1=op1),
                       reads=reads, writes=writes)

    def tt(self, eng, out, in0, in1, op, reads, writes):
        self.kb.op(eng, lambda e: e.tensor_tensor(out=out, in0=in0, in1=in1, op=op), reads=reads, writes=writes)

    def ld(self, out, in_, slot, reads, writes, eng="sync", slow=False):
        self.kb.dma(eng, lambda e: e.dma_start(out=out, in_=in_, allow_slow_non_contiguous=True), slot,
                    reads=reads, writes=writes)

    def dump(self, idx, ap, key, w):
        self.ld(self.d["dbg"][idx, 0:ap.shape[0], 0:w], ap, "dbg", [key], ["dbg"])

    def declare(self):
        c = self
        I, O, N = "ExternalInput", "ExternalOutput", "Internal"
        SEQ, NS, NPHYS, DEPTH, NPG, NTOK = c.SEQ, c.NS, c.NPHYS, c.DEPTH, c.NPG, c.NTOK
        d = {}
        d["xp"] = c.dram("xp", [SEQ, D], F32, I)
        d["xs"] = c.dram("xs", [NS, D], F32, I)
        d["pcmp"] = c.dram("pcmp", [NPHYS * DEPTH * 128, 256], F32, I)
        d["pslc"] = c.dram("pslc", [NPHYS * DEPTH * 128, 256], F32, I)
        d["pmoba"] = c.dram("pmoba", [NPHYS * DEPTH * 128 * 2, 512], F32, I)
        d["wst"] = c.dram("wst", [DEPTH, NS, 512, 256], F32, I)
        d["pt"] = c.dram("pt", [NS, NPG], I32, I)
        d["c5"] = c.dram("c5", [1 + NS, D], F32, I)
        d["relb"] = c.dram("relb", [32, 16], F32, I)
        d["w_ada"] = c.dram("w_ada", [DEPTH, D, 3 * D], F32, I)
        d["b_ada"] = c.dram("b_ada", [DEPTH, 3 * D], F32, I)
        d["w_in"] = c.dram("w_in", [DEPTH, D, NIN], F32, I)
        d["phi_pos"] = c.dram("phi_pos", [DEPTH, 2, 32, 64], F32, I)
        d["phi_w1"] = c.dram("phi_w1", [DEPTH, 2, 2048, 64], F32, I)
        d["phi_b1"] = c.dram("phi_b1", [DEPTH, 2, 64], F32, I)
        d["phi_w2"] = c.dram("phi_w2", [DEPTH, 2, 64, 64], F32, I)
        d["phi_b2"] = c.dram("phi_b2", [DEPTH, 2, 64], F32, I)
        d["w_up_a"] = c.dram("w_up_a", [DEPTH, 512, D], F32, I)
        d["w_up_b"] = c.dram("w_up_b", [DEPTH, 512, D], F32, I)
        d["w_out"] = c.dram("w_out", [DEPTH, D, D], F32, I)
        d["ln_g"] = c.dram("ln_g", [DEPTH, D], F32, I)
        d["ln_b"] = c.dram("ln_b", [DEPTH, D], F32, I)
        d["bkt"] = c.dram("bkt", [33, VL], F32, I)
        d["ovl_p"] = c.dram("ovl_p", [256, 64], F32, I)
        d["ovl_s"] = c.dram("ovl_s", [c.NCS + 1, 132], F32, I)
        d["yp"] = c.dram("yp", [SEQ, D], F32, O)
        d["ys"] = c.dram("ys", [NS, D], F32, O)
        d["ncp"] = c.dram("ncp", [DEPTH, SEQ, 256], F32, O)
        d["ncs"] = c.dram("ncs", [NS, DEPTH, 256], F32, O)
        d["nsp"] = c.dram("nsp", [DEPTH, SEQ, 256], F32, O)
        d["nss"] = c.dram("nss", [NS, DEPTH, 256], F32, O)
        d["nmp"] = c.dram("nmp", [DEPTH, SEQ, 1024], F32, O)
        d["nms"] = c.dram("nms", [NS, DEPTH, 1024], F32, O)
        d["nwp"] = c.dram("nwp", [DEPTH, 512, 256], F32, O)
        d["nws"] = c.dram("nws", [DEPTH, NS, 256], F32, O)
        if c.cfg.get("DBG_T") is not None:
            d["dbg"] = c.dram("dbg", [16, 128, 1040], F32, O)
        d["x1p"] = c.dram("x1p", [SEQ, D], F32, N)
        d["x1s"] = c.dram("x1s", [NS, D], F32, N)
        d["qs"] = c.dram("qs", [NTOK, NIN], F32, N)
        for nm in ("ktc", "vtc", "kts", "ktw"):
            d[nm] = c.dram(nm, [128, NTOK], BF16, N)
        d["ktm"] = c.dram("ktm", [128, 4, NTOK], BF16, N)
        d["vs"] = c.dram("vs", [NTOK, 128], BF16, N)
        d["vw"] = c.dram("vw", [NTOK, 128], BF16, N)
        d["vm"] = c.dram("vm", [NTOK, 512], BF16, N)
        d["gsr"] = c.dram("gsr", [NS, D], F32, N)
        d["vvec"] = c.dram("vvec", [16, VL], F32, N)
        d["sk1"] = c.dram("sk1", [16, 128 * (VL + 1)], F32, N)
        d["sk16"] = c.dram("sk16", [16, 128 * (VL + 16)], F32, N)
        self.d = d

    def constants(self, es):
        c, kb, d = self, self.kb, self.d
        self.ident_f = c.sb(es, "identf", [128, 128], F32)
        self.ident_b = c.sb(es, "identb", [128, 128], BF16)
        self.ones_b = c.sb(es, "onesb", [128, 128], BF16)
        self.ones_f = c.sb(es, "onesf", [128, 128], F32)
        kb.op("gpsimd", lambda e: e.memset(self.ident_f[:], 0.0), writes=["identf"])
        kb.op("gpsimd", lambda e: e.affine_select(out=self.ident_f[:], in_=self.ident_f[:], pattern=[[-1, 128]],
                                                    compare_op=ALU.not_equal, fill=1.0, base=0, channel_multiplier=1),
              reads=["identf"], writes=["identf"])
        c.copy("vector", self.ident_b[:], self.ident_f[:], ["identf"], ["identb"])
        kb.op("gpsimd", lambda e: e.memset(self.ones_f[:], 1.0), writes=["onesf"])
        kb.op("gpsimd", lambda e: e.memset(self.ones_b[:], 1.0), writes=["onesb"])
        with ExitStack() as ph:
            tab = c.sb(ph, "tab", [33, 16], F32)
            t31 = c.sb(ph, "t31", [33, 16], F32)
            bk = c.sb(ph, "bk", [33, VL], F32)
            vv = c.sb(ph, "vv", [16, VL], F32)
            vrep = c.sb(ph, "vrep", [128, VL], F32)
            pv = c.ps(ph, "pv", [128, 512], F32)
            kb.op("vector", lambda e: e.memset(tab[:], NEG), writes=["tab"])
            kb.op("vector", lambda e: e.memset(t31[:], 0.0), writes=["t31"])
            c.ld(tab[0:32, :], d["relb"][:, :], "c_tab", [], ["tab"])
            c.ld(t31[0:32, :], d["relb"][31:32, :].partition_broadcast(32), "c_t31", [], ["t31"])
            c.ld(bk[:], d["bkt"][:, :], "c_bk", [], ["bk"])
            c.tt("vector", tab[:], tab[:], t31[:], ALU.subtract, ["tab", "t31"], ["tab"])
            for j in range(VL // 512):
                c.mm(pv[0:16, :], tab[:], bk[:, j * 512:(j + 1) * 512], True, True, ["tab", "bk"], ["pv"])
                c.copy("vector", vv[:, j * 512:(j + 1) * 512], pv[0:16, :], ["pv"], ["vv"])
            c.ld(d["vvec"][:, :], vv[:], "c_vv", ["vv"], ["vvec"])
            for h in range(16):
                c.ld(vrep[:], d["vvec"][h:h + 1, :].partition_broadcast(128), "c_vrep", ["vvec"], ["vrep"])
                for nm, pitch in (("sk1", VL + 1), ("sk16", VL + 16)):
                    dst = AP(tensor=d[nm].tensor, offset=h * 128 * pitch, ap=[[pitch, 128], [1, VL]])
                    c.ld(dst, vrep[:], "c_" + nm, ["vrep"], [nm])
        self.diag = c.sb(es, "diag", [128, 16, 128], F32)
        self.off = c.sb(es, "off", [128, 16, 128], F32)
        p1 = VL + 1
        src = AP(tensor=d["sk1"].tensor, offset=VOFF, ap=[[p1 - 1, 128], [128 * p1, 16], [1, 128]])
        c.ld(self.diag[:], src, "c_diag", ["sk1"], ["diag"])
        src = AP(tensor=d["sk1"].tensor, offset=VOFF + 128, ap=[[p1 - 1, 128], [128 * p1, 16], [1, 128]])
        c.ld(self.off[:], src, "c_off", ["sk1"], ["off"])

    def phase_S(self, es, l):
        c, kb, d, NS = self, self.kb, self.d, self.NS
        NB = 1 + NS
        s1T = c.sb(es, "s1T", [128, 8, NB], F32)
        shT = c.sb(es, "shT", [128, 8, NB], F32)
        gate_bc = c.sb(es, "gatebc", [128, D], F32)
        lng = c.sb(es, "lng", [128, D], F32)
        lnb = c.sb(es, "lnb", [128, D], F32)
        c.ld(lng[:], d["ln_g"][l:l + 1, :].partition_broadcast(128), "s_lng", [], ["lng"])
        c.ld(lnb[:], d["ln_b"][l:l + 1, :].partition_broadcast(128), "s_lnb", [], ["lnb"])
        with ExitStack() as ph:
            wada = c.sb(ph, "wada", [128, 8, 3 * D], BF16)
            cT = c.sb(ph, "cT", [128, 8, NB], F32)
            tmp = c.sb(ph, "ctmp", [128, 8, NB], F32)
            scT = c.sb(ph, "scT", [128, 8, NB], BF16)
            screp = c.sb(ph, "screp", [128, 8, 128], BF16)
            badaT = c.sb(ph, "badaT", [128, 24], F32)
            adaT = c.sb(ph, "adaT", [128, 24, NB], F32)
            bg = c.sb(ph, "bg", [128, D], F32)
            grow = c.sb(ph, "grow", [1, D], F32)
            pa = c.ps(ph, "pa", [128, 512], F32)
            pg = c.ps(ph, "pg", [128, 1024], F32)
            for k in range(8):
                c.ld(wada[:, k, :], d["w_ada"][l, k * 128:(k + 1) * 128, :], "s_wada", [], ["wada"], eng="gpsimd")
            for j in range(NB):
                srcc = AP(tensor=d["c5"].tensor, offset=j * D, ap=[[1, 128], [128, 8]])
                c.ld(cT[:, :, j], srcc, "s_cT", [], ["cT"], slow=True)
            srcb = AP(tensor=d["b_ada"].tensor, offset=l * 3 * D, ap=[[1, 128], [128, 24]])
            c.ld(badaT[:], srcb, "s_bada", [], ["badaT"], slow=True)
            c.ld(bg[:], d["b_ada"][l:l + 1, 2 * D:3 * D].partition_broadcast(128), "s_bg", [], ["bg"])
            c.act(tmp[:], cT[:], AF.Exp, ["cT"], ["ctmp"], scale=-1.0)
            c.ts("vector", tmp[:], tmp[:], 1.0, None, ALU.add, None, ["ctmp"], ["ctmp"])
            kb.op("vector", lambda e: e.reciprocal(out=tmp[:], in_=tmp[:]), reads=["ctmp"], writes=["ctmp"])
            c.tt("vector", scT[:], cT[:], tmp[:], ALU.mult, ["cT", "ctmp"], ["scT"])
            for k in range(8):
                c.copy("vector", screp[:, k, :], scT[:, k, 0:1].broadcast_to([128, 128]), ["scT"], ["screp"])
            for j in range(24):
                for k in range(8):
                    c.mm(pa[:, j * NB:(j + 1) * NB], wada[:, k, j * 128:(j + 1) * 128], scT[:, k, :],
                         k == 0, k == 7, ["wada", "scT"], ["pa"])
            c.tt("vector", adaT[:], pa[:, 0:24 * NB].rearrange("p (j b) -> p j b", b=NB),
                 badaT[:].unsqueeze(2).broadcast_to([128, 24, NB]), ALU.add, ["pa", "badaT"], ["adaT"])
            c.copy("vector", shT[:], adaT[:, 0:8, :], ["adaT"], ["shT"])
            c.ts("vector", s1T[:], adaT[:, 8:16, :], 1.0, None, ALU.add, None, ["adaT"], ["s1T"])
            for hf in range(2):
                for k in range(8):
                    c.mm(pg[:, hf * 512:(hf + 1) * 512], screp[:, k, :], wada[:, k, 2 * D + hf * 512:2 * D + (hf + 1) * 512],
                         k == 0, k == 7, ["screp", "wada"], ["pg"])
            c.tt("vector", gate_bc[:], pg[:], bg[:], ALU.add, ["pg", "bg"], ["gatebc"])
            for s in range(NS):
                for hf in range(2):
                    for k in range(8):
                        c.mm(pg[0:1, hf * 512:(hf + 1) * 512], scT[:, k, 1 + s:2 + s],
                             wada[:, k, 2 * D + hf * 512:2 * D + (hf + 1) * 512], k == 0, k == 7, ["scT", "wada"], ["pg"])
                c.tt("vector", grow[:], pg[0:1, :], bg[0:1, :], ALU.add, ["pg", "bg"], ["grow"])
                c.ld(d["gsr"][s:s + 1, :], grow[:], "s_gsr", ["grow"], ["gsr"])
        return dict(s1T=s1T, shT=shT, gate_bc=gate_bc, lng=lng, lnb=lnb)

    def phase_P(self, l, S):
        c, kb, d, NS, SEQ, NT = self, self.kb, self.d, self.NS, self.SEQ, self.NT
        xin_p = d["xp"] if l == 0 else d["x1p"]
        xin_s = d["xs"] if l == 0 else d["x1s"]
        with ExitStack() as ph:
            win = c.sb(ph, "win", [128, 8, NIN], BF16)
            xt = [c.sb(ph, "xt", [128, D], F32) for _ in range(2)]
            hT = [c.sb(ph, "hT", [128, 8, 128], BF16) for _ in range(2)]
            p32 = [c.sb(ph, "p32", [128, NIN], F32) for _ in range(2)]
            kT = [c.sb(ph, "kT", [128, 8, 128], BF16) for _ in range(2)]
            pT = c.ps(ph, "pT", [128, 1024], F32)
            pacc = [c.ps(ph, "pacc", [128, 512], F32) for _ in range(3)]
            pk = c.ps(ph, "pk", [128, 1024], F32)
            for k in range(8):
                for hf in range(2):
                    c0 = hf * (NIN // 2)
                    c.ld(win[:, k, c0:c0 + NIN // 2], d["w_in"][l, k * 128:(k + 1) * 128, c0:c0 + NIN // 2],
                         "p_win", [], ["win"], eng="gpsimd")
            groups = [(g0, min(512, NIN - g0)) for g0 in range(0, NIN, 512)]
            gi = 0
            for t in range(NT + 1):
                i = t % 2
                samp = t == NT
                nt = NS if samp else 128
                r0 = SEQ if samp else t * 128
                kx, kh, kp, kk = "xt%d" % i, "hT%d" % i, "p32%d" % i, "kT%d" % i
                if samp:
                    c.ld(xt[i][0:nt, :], xin_s[:, :], "p_x%d" % i, ["x1s"], [kx])
                else:
                    c.ld(xt[i][:, :], xin_p[r0:r0 + 128, :], "p_x%d" % i, ["x1p"], [kx])
                for k in range(8):
                    c.tr(pT[:, k * 128:k * 128 + nt], xt[i][0:nt, k * 128:(k + 1) * 128], self.ident_f[0:nt, 0:nt],
                         [kx, "identf"], ["pT"])
                if samp:
                    pv = pT[:, :].rearrange("p (k t) -> p k t", t=128)[:, :, 0:nt]
                    c.tt("vector", pv, pv, S["s1T"][:, :, 1:1 + NS], ALU.mult, ["pT", "s1T"], ["pT"])
                    c.tt("vector", hT[i][:, :, 0:nt], pv, S["shT"][:, :, 1:1 + NS], ALU.add, ["pT", "shT"], [kh])
                else:
                    for k in range(8):
                        if k % 2 == 0:
                            c.act(hT[i][:, k, :], pT[:, k * 128:(k + 1) * 128], AF.Identity, ["pT", "s1T", "shT"], [kh],
                                  bias=S["shT"][:, k, 0:1], scale=S["s1T"][:, k, 0:1])
                        else:
                            c.ts("vector", hT[i][:, k, :], pT[:, k * 128:(k + 1) * 128], S["s1T"][:, k, 0:1],
                                 S["shT"][:, k, 0:1], ALU.mult, ALU.add, ["pT", "s1T", "shT"], [kh])
                for (g0, w) in groups:
                    pa = pacc[gi % 3]
                    pkey = "pacc%d" % (gi % 3)
                    for k in range(8):
                        c.mm(pa[0:nt, 0:w], hT[i][:, k, 0:nt], win[:, k, g0:g0 + w], k == 0, k == 7, [kh, "win"], [pkey])
                    c.copy("scalar" if gi % 2 == 0 else "vector", p32[i][0:nt, g0:g0 + w], pa[0:nt, 0:w], [pkey], [kp])
                    gi += 1
                blocks = [C_KC, C_VC, C_KS, C_KW, C_BK, C_BK + 128, C_BK + 256, C_BK + 384]
                for bi, c0 in enumerate(blocks):
                    c.tr(pk[:, bi * 128:bi * 128 + nt], p32[i][0:nt, c0:c0 + 128], self.ident_f[0:nt, 0:nt],
                         [kp, "identf"], ["pk"])
                pkv = pk[:, :].rearrange("p (k t) -> p k t", t=128)
                c.copy("vector", kT[i][:, 0:4, 0:nt], pkv[:, 0:4, 0:nt], ["pk"], [kk])
                c.copy("scalar", kT[i][:, 4:8, 0:nt], pkv[:, 4:8, 0:nt], ["pk"], [kk])
                for bi, nm in enumerate(("ktc", "vtc", "kts", "ktw")):
                    c.ld(d[nm][:, r0:r0 + nt], kT[i][:, bi, 0:nt], "p_kt%d" % i, [kk], [nm])
                c.ld(d["ktm"][:, :, r0:r0 + nt], kT[i][:, 4:8, 0:nt], "p_kt%d" % i, [kk], ["ktm"])
                c.ld(d["vs"][r0:r0 + nt, :], p32[i][0:nt, C_VS:C_VS + 128], "p_v%d" % i, [kp], ["vs"], eng="gpsimd")
                c.ld(d["vw"][r0:r0 + nt, :], p32[i][0:nt, C_VW:C_VW + 128], "p_v%d" % i, [kp], ["vw"], eng="gpsimd")
                c.ld(d["vm"][r0:r0 + nt, :], p32[i][0:nt, C_BV:C_BV + 512], "p_v%d" % i, [kp], ["vm"], eng="gpsimd")
                if samp:
                    c.ld(d["ncs"][:, l, :], p32[i][0:nt, C_KC:C_KC + 256], "p_o%d" % i, [kp], ["ncs"])
                    c.ld(d["nss"][:, l, :], p32[i][0:nt, C_KS:C_KS + 256], "p_o%d" % i, [kp], ["nss"])
                    c.ld(d["nws"][l, :, :], p32[i][0:nt, C_KW:C_KW + 256], "p_o%d" % i, [kp], ["nws"])
                    c.ld(d["nms"][:, l, :], p32[i][0:nt, C_BK:C_BK + 1024], "p_o%d" % i, [kp], ["nms"])
                else:
                    c.ld(d["ncp"][l, r0:r0 + 128, :], p32[i][:, C_KC:C_KC + 256], "p_o%d" % i, [kp], ["ncp"])
                    c.ld(d["nsp"][l, r0:r0 + 128, :], p32[i][:, C_KS:C_KS + 256], "p_o%d" % i, [kp], ["nsp"])
                    c.ld(d["nmp"][l, r0:r0 + 128, :], p32[i][:, C_BK:C_BK + 1024], "p_o%d" % i, [kp], ["nmp"])
                    if r0 >= SEQ - 512:
                        w0 = r0 - (SEQ - 512)
                        c.ld(d["nwp"][l, w0:w0 + 128, :], p32[i][:, C_KW:C_KW + 256], "p_o%d" % i, [kp], ["nwp"])
                c.ld(d["qs"][r0:r0 + nt, :], p32[i][0:nt, :], "p_o%d" % i, [kp], ["qs"])


    def gelu_tanh(self, u, x, n, h1):
        c, kb = self, self.kb
        kx, ku = x.name, u.name
        X, U = x[:, 0:n], u[:, 0:n]
        c.tt("vector", U, X, X, ALU.mult, [kx], [ku])
        c.ts("vector", U, U, 0.044715, 1.0, ALU.mult, ALU.add, [ku], [ku])
        c.tt("vector", U, U, X, ALU.mult, [ku, kx], [ku])
        c.act(U, U, AF.Exp, [ku], [ku], scale=-2.0 * math.sqrt(2.0 / math.pi))
        c.ts("vector", U, U, 1.0, None, ALU.add, None, [ku], [ku])
        kb.op("vector", lambda e: e.reciprocal(out=U, in_=U), reads=[ku], writes=[ku])
        c.tt("vector", h1[:, 0:n], X, U, ALU.mult, [kx, ku], [h1.name])

    def load_phi(self, ph, l):
        c, kb, d = self, self.kb, self.d
        W = {}
        pb = c.ps(ph, "pb1", [128, 512], F32)
        for kv in range(2):
            w1 = c.sb(ph, "w1bd", [128, 32, 128], BF16)
            w2 = c.sb(ph, "w2bd", [128, 128], BF16)
            pos = c.sb(ph, "posbd", [128, 32], BF16)
            b1 = c.sb(ph, "b1T", [128, 1], F32)
            b2 = c.sb(ph, "b2T", [128, 1], F32)
            b1e = c.sb(ph, "b1e", [128, 1], F32)
            kb.op("gpsimd", lambda e, w1=w1: e.memset(w1[:], 0.0), writes=[w1.name])
            kb.op("gpsimd", lambda e, w2=w2: e.memset(w2[:], 0.0), writes=[w2.name])
            for g in range(2):
                src = AP(tensor=d["phi_w1"].tensor, offset=(l * 2 + kv) * 2048 * 64, ap=[[64, 64], [4096, 32], [1, 64]])
                c.ld(w1[64 * g:64 * g + 64, :, 64 * g:64 * g + 64], src, "c_w1", [], [w1.name], eng="gpsimd")
                c.ld(w2[64 * g:64 * g + 64, 64 * g:64 * g + 64], d["phi_w2"][l, kv, :, :], "c_w2", [], [w2.name], eng="gpsimd")
                srcp = AP(tensor=d["phi_pos"].tensor, offset=(l * 2 + kv) * 2048, ap=[[1, 64], [64, 32]])
                c.ld(pos[64 * g:64 * g + 64, :], srcp, "c_pos", [], [pos.name], eng="gpsimd")
                srcb = AP(tensor=d["phi_b1"].tensor, offset=(l * 2 + kv) * 64, ap=[[1, 64], [1, 1]])
                c.ld(b1[64 * g:64 * g + 64, :], srcb, "c_b1", [], [b1.name])
                srcb = AP(tensor=d["phi_b2"].tensor, offset=(l * 2 + kv) * 64, ap=[[1, 64], [1, 1]])
                c.ld(b2[64 * g:64 * g + 64, :], srcb, "c_b2", [], [b2.name])
            for lp in range(32):
                c.mm(pb[:, kv:kv + 1], w1[:, lp, :], pos[:, lp:lp + 1], lp == 0, lp == 31, [w1.name, pos.name], ["pb1"])
            c.tt("vector", b1e[:], pb[:, kv:kv + 1], b1[:], ALU.add, ["pb1", b1.name], [b1e.name])
            W[kv] = dict(w1=w1, w2=w2, b1e=b1e, b2=b2)
        b2bc = c.sb(ph, "b2bc", [128, 128], F32)
        for g in range(2):
            c.ld(b2bc[:, 64 * g:64 * g + 64], d["phi_b2"][l, 1:2, :].partition_broadcast(128), "c_b2bc", [], [b2bc.name])
        W["b2bc"] = b2bc
        return W

    def compress(self, ph, W, ktc, vtc, n, ckT, cv_write, tag):
        c, kb = self, self.kb
        if "cp1" not in W or W.get("cp_scope") is not ph:
            W["cp1"] = c.ps(ph, "cps1", [128, 512], F32)
            W["cp2"] = c.ps(ph, "cps2", [128, 512], F32)
            W["cx"] = c.sb(ph, "cx", [128, 512], F32)
            W["ch1"] = c.sb(ph, "ch1", [128, 512], BF16)
            W["cu"] = c.sb(ph, "cu", [128, 512], F32)
            W["cp_scope"] = ph
        p1, p2 = W["cp1"], W["cp2"]
        for kv, src in ((0, ktc), (1, vtc)):
            w = W[kv]
            x = W["cx"]
            h1 = W["ch1"]
            for lp in range(32):
                c.mm(p1[:, 0:n], w["w1"][:, lp, :], src[:, lp:lp + 16 * (n - 1) + 1:16], lp == 0, lp == 31,
                     [w["w1"].name, src.name], [p1.name])
            c.act(x[:, 0:n], p1[:, 0:n], AF.Identity, [p1.name, w["b1e"].name], [x.name], bias=w["b1e"][:, 0:1])
            self.gelu_tanh(W["cu"], x, n, h1)
            if kv == 0:
                c.mm(p2[:, 0:n], w["w2"][:], h1[:, 0:n], True, True, [w["w2"].name, h1.name], [p2.name])
                c.act(ckT[:, 0:n], p2[:, 0:n], AF.Identity, [p2.name, w["b2"].name], [ckT.name], bias=w["b2"][:, 0:1])
            else:
                for nt_ in range((n + 127) // 128):
                    rows = min(128, n - nt_ * 128)
                    c.mm(p2[0:rows, 0:128], h1[:, nt_ * 128:nt_ * 128 + rows], w["w2"][:], True, True,
                         [w["w2"].name, h1.name], [p2.name])
                    cv_write(nt_, rows, p2)

    def phase_C_prompt(self, les, ph, W, l):
        c, kb, d, SEQ = self, self.kb, self.d, self.SEQ
        n = self.NCP
        NTL = (n + 127) // 128
        ckT, cvU = les
        kb.op("gpsimd", lambda e: e.memset(ckT[:], 0.0), writes=[ckT.name])
        kb.op("gpsimd", lambda e: e.memset(cvU[:], 0.0), writes=[cvU.name])
        kb.op("gpsimd", lambda e: e.memset(cvU[:, :, :, 64:65], 1.0), writes=[cvU.name])
        for g in range(2):
            c.ld(cvU[:, :, g, 65:129], d["ovl_p"][0:NTL * 128, :].rearrange("(t p) j -> p t j", p=128), "c_ovl", [], [cvU.name],
                 eng="gpsimd")
        ktc = c.sb(ph, "ktc_sb", [128, SEQ], BF16)
        vtc = c.sb(ph, "vtc_sb", [128, SEQ], BF16)
        c.ld(ktc[:], d["ktc"][:, 0:SEQ], "c_ktc", ["ktc"], [ktc.name])
        c.ld(vtc[:], d["vtc"][:, 0:SEQ], "c_vtc", ["vtc"], [vtc.name])

        def cv_write(nt_, rows, p2):
            c.tt("vector", cvU[0:rows, nt_, :, 0:64], p2[0:rows, 0:128].rearrange("p (g d) -> p g d", g=2),
                 W["b2bc"][0:rows, :].rearrange("p (g d) -> p g d", g=2), ALU.add, [p2.name, W["b2bc"].name], [cvU.name])
        self.compress(ph, W, ktc, vtc, n, ckT, cv_write, "p")
        return ckT, cvU

    def phaseA_setup(self, ph, l):
        c, kb, d, SEQ = self, self.kb, self.d, self.SEQ
        A = {}
        NSLC = SEQ // 64
        A["wupa"] = c.sb(ph, "wupa", [128, 4, D], BF16)
        A["wupb"] = c.sb(ph, "wupb", [128, 4, D], BF16)
        A["wout"] = c.sb(ph, "wout", [128, 8, D], BF16)
        for k in range(4):
            c.ld(A["wupa"][:, k, :], d["w_up_a"][l, k * 128:(k + 1) * 128, :], "a_w", [], ["wupa"], eng="gpsimd")
            c.ld(A["wupb"][:, k, :], d["w_up_b"][l, k * 128:(k + 1) * 128, :], "a_w", [], ["wupb"], eng="gpsimd")
        for k in range(8):
            c.ld(A["wout"][:, k, :], d["w_out"][l, k * 128:(k + 1) * 128, :], "a_w", [], ["wout"], eng="gpsimd")
        ew = c.sb(ph, "ewide", [64, SEQ], BF16)
        kb.op("gpsimd", lambda e: e.memset(ew[:], 1.0), writes=["ewide"])
        kb.op("gpsimd", lambda e: e.affine_select(out=ew[:], in_=ew[:], pattern=[[1, SEQ]], compare_op=ALU.is_ge, fill=0.0,
                                                    base=0, channel_multiplier=-64), reads=["ewide"], writes=["ewide"])
        kb.op("gpsimd", lambda e: e.affine_select(out=ew[:], in_=ew[:], pattern=[[-1, SEQ]], compare_op=ALU.is_ge, fill=0.0,
                                                    base=63, channel_multiplier=64), reads=["ewide"], writes=["ewide"])
        A["ewide"] = ew
        rs = c.sb(ph, "rsm", [16, 16, 128], BF16)
        kb.op("gpsimd", lambda e: e.memset(rs[:], 1.0), writes=["rsm"])
        kb.op("gpsimd", lambda e: e.affine_select(out=rs[:], in_=rs[:], pattern=[[-1, 16], [0, 128]], compare_op=ALU.is_equal,
                                                    fill=0.0, base=0, channel_multiplier=1), reads=["rsm"], writes=["rsm"])
        A["rsm"] = rs
        zc = c.sb(ph, "zc", [16, 640], F32)
        kb.op("gpsimd", lambda e: e.memset(zc[:], 0.0), writes=["zc"])
        kb.op("gpsimd", lambda e: e.affine_select(out=zc[:], in_=zc[:], pattern=[[1, 640]], compare_op=ALU.not_equal, fill=1.0,
                                                    base=-256, channel_multiplier=-1), reads=["zc"], writes=["zc"])
        A["zc"] = zc
        band = c.sb(ph, "band", [16, 16, 128], F32)
        p16 = VL + 16
        src = AP(tensor=d["sk16"].tensor, offset=VOFF + 97, ap=[[VL, 16], [128 * p16, 16], [1, 128]])
        c.ld(band[:], src, "a_band", [], ["band"])
        A["band"] = band
        anti = c.sb(ph, "anti", [128, 4, 128], BF16)
        kb.op("gpsimd", lambda e: e.memset(anti[:], 0.0), writes=["anti"])
        kb.op("gpsimd", lambda e: e.affine_select(out=anti[:], in_=anti[:], pattern=[[0, 4], [-1, 128]], compare_op=ALU.is_gt,
                                                    fill=NEG, base=0, channel_multiplier=1), reads=["anti"], writes=["anti"])
        A["anti"] = anti
        GW = NSLC + 64
        G = c.sb(ph, "G", [128, GW], F32)
        rel = c.sb(ph, "Grel", [128, GW], F32)
        t1 = c.sb(ph, "Gt1", [128, GW], F32)
        pcol = c.sb(ph, "Gp", [128, 1], F32)
        kb.op("gpsimd", lambda e: e.iota(rel[:], pattern=[[1, GW]], base=-NSLC, channel_multiplier=0,
                                         allow_small_or_imprecise_dtypes=True), writes=["Grel"])
        kb.op("gpsimd", lambda e: e.iota(pcol[:], pattern=[[0, 1]], base=0, channel_multiplier=1,
                                         allow_small_or_imprecise_dtypes=True), writes=["Gp"])
        c.ts("vector", pcol[:], pcol[:], 64.0, None, ALU.is_ge, None, ["Gp"], ["Gp"])
        c.ts("vector", rel[:], rel[:], pcol[:, 0:1], None, ALU.subtract, None, ["Grel", "Gp"], ["Grel"])
        c.ts("vector", G[:], rel[:], 0.0, 1e4, ALU.is_equal, ALU.mult, ["Grel"], ["G"])
        c.ts("vector", t1[:], rel[:], -1.0, 1e4, ALU.is_equal, ALU.mult, ["Grel"], ["Gt1"])
        c.tt("vector", G[:], G[:], t1[:], ALU.add, ["G", "Gt1"], ["G"])
        c.ts("vector", t1[:], rel[:], 0.0, -1e9, ALU.is_gt, ALU.mult, ["Grel"], ["Gt1"])
        c.tt("vector", G[:], G[:], t1[:], ALU.add, ["G", "Gt1"], ["G"])
        A["G"] = G
        zrow = c.sb(ph, "zrow", [1, 512], BF16)
        kb.op("gpsimd", lambda e: e.memset(zrow[:], 0.0), writes=["zrow"])
        A["zrow"] = zrow
        return A

    @staticmethod
    def run_stages(items):
        if not items:
            return
        k = len(items[0])
        n = len(items)
        for step in range(n + k - 1):
            for st in range(k):
                i = step - st
                if 0 <= i < n:
                    items[i][st]()

    @staticmethod
    def run_pipe(items, depth=1):
        n = len(items)
        for i in range(min(depth, n)):
            items[i][0]()
        for i in range(n):
            if i + depth < n:
                items[i + depth][0]()
            items[i][1]()

    def zero_bank(self, A, bank_ap, key):
        self.mm(bank_ap, A["zrow"][0:1, 0:128], A["zrow"][0:1, 0:512], True, False, ["zrow"], [key])

    def epilogue(self, A, S, l, nt, B, dst_rows):
        c, kb = self, self.kb
        L_AG, L_AZ, L_BZ, L_MA, L_MB = 512, 536, 1560, 2072, 3096
        qsb, xres = B["qsb"], B["xres"]
        P = slice(0, nt)
        W1, W2, W3 = B["w1"], B["w2"], B["w3"]
        k1, k2, k3 = W1.name, W2.name, W3.name
        kq = qsb.name
        sm = B["small"]
        ks = sm.name
        c.act(sm[P, 0:24], qsb[P, L_AG:L_AG + 24], AF.Exp, [kq], [ks], scale=-1.0)
        c.ts("vector", sm[P, 0:24], sm[P, 0:24], 1.0, None, ALU.add, None, [ks], [ks])
        kb.op("vector", lambda e: e.reciprocal(out=sm[P, 0:24], in_=sm[P, 0:24]), reads=[ks], writes=[ks])
        for bi, (nm, w) in enumerate((("Ocmp", 129), ("Oslc", 65), ("Owin", 65), ("Omob", 65))):
            c.ts("vector", sm[P, 24 + 8 * bi:32 + 8 * bi], B[nm][P, :, 64], 1e-30, None, ALU.max, None, [B[nm].name], [ks])
            kb.op("vector", lambda e, bi=bi: e.reciprocal(out=sm[P, 24 + 8 * bi:32 + 8 * bi], in_=sm[P, 24 + 8 * bi:32 + 8 * bi]),
                  reads=[ks], writes=[ks])
        sgv = sm[P, 0:24].rearrange("p (h b) -> p b h", b=3)
        c.tt("vector", sm[P, 24:48].rearrange("p (b h) -> p b h", b=3), sm[P, 24:48].rearrange("p (b h) -> p b h", b=3), sgv,
             ALU.mult, [ks], [ks])
        oa = W1[P, 0:512].rearrange("p (h d) -> p h d", h=8)
        tmp = W1[P, 512:1024].rearrange("p (h d) -> p h d", h=8)
        for bi, nm in enumerate(("Ocmp", "Oslc", "Owin")):
            cf = sm[P, 24 + 8 * bi:32 + 8 * bi].unsqueeze(2).broadcast_to([nt, 8, 64])
            if bi == 0:
                c.tt("vector", oa, B[nm][P, :, 0:64], cf, ALU.mult, [B[nm].name, ks], [k1])
            else:
                c.tt("vector", tmp, B[nm][P, :, 0:64], cf, ALU.mult, [B[nm].name, ks], [k1])
                c.tt("vector", oa, oa, tmp, ALU.add, [k1], [k1])
        for (lo, off) in ((L_AZ, 0), (L_BZ, 512)):
            c.act(W2[P, off:off + 512], qsb[P, lo:lo + 512], AF.Exp, [kq], [k2], scale=-1.0)
            c.ts("gpsimd", W2[P, off:off + 512], W2[P, off:off + 512], 1.0, None, ALU.add, None, [k2], [k2])
            kb.op("vector", lambda e, off=off: e.reciprocal(out=W2[P, off:off + 512], in_=W2[P, off:off + 512]), reads=[k2], writes=[k2])
            c.tt("gpsimd", W2[P, off:off + 512], W2[P, off:off + 512], qsb[P, lo:lo + 512], ALU.mult, [k2, kq], [k2])
        oz = B["oz"]
        c.tt("vector", oz[P, 0:512], W1[P, 0:512], W2[P, 0:512], ALU.mult, [k1, k2], [oz.name])
        ob = W1[P, 512:1024].rearrange("p (h d) -> p h d", h=8)
        c.tt("vector", ob, B["Omob"][P, :, 0:64], sm[P, 48:56].unsqueeze(2).broadcast_to([nt, 8, 64]), ALU.mult,
             [B["Omob"].name, ks, k1], [k1])
        c.tt("vector", oz[P, 512:1024], W1[P, 512:1024], W2[P, 512:1024], ALU.mult, [k1, k2], [oz.name])
        pq, ozT = B["pq"], B["ozT"]
        for k in range(8):
            c.tr(pq[:, k, 0:nt], oz[P, k * 128:(k + 1) * 128], self.ident_b[0:nt, 0:nt], [oz.name, "identb"], [pq.name])
        c.copy("scalar", ozT[:, :, 0:nt], pq[:, :, 0:nt], [pq.name], [ozT.name])
        pY = B["pY"]
        for br, wt in ((0, A["wupa"]), (1, A["wupb"])):
            for hf in range(2):
                o = (br * 2 + hf) * 512
                for k in range(4):
                    c.mm(pY[P, o:o + 512], ozT[:, br * 4 + k, 0:nt], wt[:, k, hf * 512:(hf + 1) * 512], k == 0, k == 3,
                         [ozT.name, wt.name.split("_")[0]], [pY.name])
        for (lo, Wt, kk) in ((L_MA, W1, k1), (L_MB, W2, k2)):
            c.act(Wt[P, :], qsb[P, lo:lo + 1024], AF.Exp, [kq], [kk], scale=-1.0)
            c.ts("gpsimd", Wt[P, :], Wt[P, :], 1.0, None, ALU.add, None, [kk], [kk])
            kb.op("vector", lambda e, Wt=Wt: e.reciprocal(out=Wt[P, :], in_=Wt[P, :]), reads=[kk], writes=[kk])
        c.tt("vector", W1[P, :], W1[P, :], pY[P, 0:1024], ALU.mult, [k1, pY.name], [k1])
        c.tt("vector", W2[P, :], W2[P, :], pY[P, 1024:2048], ALU.mult, [k2, pY.name], [k2])
        c.tt("gpsimd", oz[P, :], W1[P, :], W2[P, :], ALU.add, [k1, k2], [oz.name])
        for k in range(8):
            c.tr(pq[:, k, 0:nt], oz[P, k * 128:(k + 1) * 128], self.ident_b[0:nt, 0:nt], [oz.name, "identb"], [pq.name])
        c.copy("scalar", ozT[:, :, 0:nt], pq[:, :, 0:nt], [pq.name], [ozT.name])
        for hf in range(2):
            for k in range(8):
                c.mm(pY[P, hf * 512:(hf + 1) * 512], ozT[:, k, 0:nt], A["wout"][:, k, hf * 512:(hf + 1) * 512], k == 0, k == 7,
                     [ozT.name, "wout"], [pY.name])
        gate = B["gate"]
        c.tt("vector", W1[P, :], pY[P, 0:1024], gate[P, :], ALU.mult, [pY.name, gate.name], [k1])
        kb.op("vector", lambda e: e.scalar_tensor_tensor(out=W1[P, :], in0=xres[P, :], scalar=float(self.cfg["ALPHA"]), in1=W1[P, :],
                                                          op0=ALU.mult, op1=ALU.add), reads=[xres.name, k1], writes=[k1])
        st = B["stats"]
        for hf in range(2):
            kb.op("vector", lambda e, hf=hf: e.bn_stats(out=st[P, hf, :], in_=W1[P, hf * 512:(hf + 1) * 512]), reads=[k1], writes=[st.name])
        kb.op("vector", lambda e: e.bn_aggr(out=sm[P, 56:58], in_=st[P, :, :]), reads=[st.name], writes=[ks])
        c.act(sm[P, 58:59], sm[P, 57:58], AF.Ln, [ks, "epsc"], [ks], bias=self.epsc[P, 0:1])
        c.act(sm[P, 58:59], sm[P, 58:59], AF.Exp, [ks], [ks], scale=-0.5)
        c.ts("vector", W1[P, :], W1[P, :], sm[P, 56:57], sm[P, 58:59], ALU.subtract, ALU.mult, [k1, ks], [k1])
        c.tt("gpsimd", W1[P, :], W1[P, :], S["lng"][P, :], ALU.mult, [k1, "lng"], [k1])
        c.tt("gpsimd", W1[P, :], W1[P, :], S["lnb"][P, :], ALU.add, [k1, "lnb"], [k1])
        c.ld(dst_rows, W1[P, :], "a_y", [k1], [dst_rows.tensor.name])

    def phase_A_prompt(self, ph, A, S, l, ckT, cvU, ydst):
        c, kb, d, SEQ, NT = self, self.kb, self.d, self.SEQ, self.NT
        NSLC = SEQ // 64
        NCP = self.NCP
        B = {}
        B["qsb"] = c.sb(ph, "qsb", [128, 4120], F32)
        B["xres"] = c.sb(ph, "xres", [128, D], F32)
        B["w1"] = c.sb(ph, "wk1", [128, 1024], F32)
        B["w2"] = c.sb(ph, "wk2", [128, 1024], F32)
        B["w3"] = c.sb(ph, "wk3", [128, 1024], F32)
        B["small"] = c.sb(ph, "small", [128, 64], F32)
        B["oz"] = c.sb(ph, "oz", [128, 1024], BF16)
        B["ozT"] = c.sb(ph, "ozT", [128, 8, 128], BF16)
        B["stats"] = c.sb(ph, "stats", [128, 2, 6], F32)
        B["Ocmp"] = c.sb(ph, "Ocmp", [128, 8, 129], F32)
        B["Oslc"] = c.sb(ph, "Oslc", [128, 8, 65], F32)
        B["Owin"] = c.sb(ph, "Owin", [128, 8, 65], F32)
        B["Omob"] = c.sb(ph, "Omob", [128, 8, 65], F32)
        B["gate"] = S["gate_bc"]
        qT = c.sb(ph, "qT", [128, 8, 128], BF16)
        aqp = c.sb(ph, "aqp", [128, 4, 2, 64], BF16)
        bqb = c.sb(ph, "bqb", [128, 512], BF16)
        ET = [c.sb(ph, "ET", [128, 1024], BF16) for _ in range(2)]
        mb = c.sb(ph, "mb", [128, 2, 64], BF16)
        mbT = c.sb(ph, "mbT", [64, 2, 4, 128], BF16)
        impa = c.sb(ph, "impa", [128, 2, 64], F32)
        impr = c.sb(ph, "impr", [128, 64], F32)
        m8 = c.sb(ph, "m8", [128, 8, 8], F32)
        thr = c.sb(ph, "thr", [128, 8], F32)
        gsb = c.sb(ph, "gsb", [128, 8, 16], F32)
        mbm = c.sb(ph, "mbm", [128, 8, 16], BF16)
        mbmT = c.sb(ph, "mbmT", [16, 8, 128], BF16)
        kmeanT = c.sb(ph, "kmeanT", [128, 4, 16], BF16)
        kmsum = c.sb(ph, "kmsum", [128, 4], F32)
        kblk = c.sb(ph, "kblk", [128, 4, 256], BF16)
        NR = 3
        ksT = [c.sb(ph, "ksT", [128, 128], BF16) for _ in range(NR)]
        vsb = [c.sb(ph, "vsb", [128, 2, 65], BF16) for _ in range(NR)]
        kmT = [c.sb(ph, "kmT", [128, 4, 128], BF16) for _ in range(NR)]
        vmb = [c.sb(ph, "vmb", [128, 8, 65], BF16) for _ in range(NR)]
        for i in range(NR):
            kb.op("gpsimd", lambda e, i=i: e.memset(vsb[i][:, :, 64:65], 1.0), writes=[vsb[i].name])
            kb.op("gpsimd", lambda e, i=i: e.memset(vmb[i][:, :, 64:65], 1.0), writes=[vmb[i].name])
        pq = c.ps(ph, "pq", [128, 8, 128], BF16)
        pS = [c.ps(ph, "pS", [128, 512], F32) for _ in range(2)]
        pO = c.ps(ph, "pO", [128, 2048], F32)
        pm = c.ps(ph, "pm", [128, 1024], BF16)
        B["pq"], B["pY"] = pq, pO
        idb, idf = self.ident_b, self.ident_f
        ring = [0]
        eti = [0]
        kvr = [0]

        def evac_O(dst, nh_per_bank, w):
            for bnk in range((8 + nh_per_bank - 1) // nh_per_bank):
                h0 = bnk * nh_per_bank
                nh = min(nh_per_bank, 8 - h0)
                c.copy("vector", dst[:, h0:h0 + nh, :], pO[:, bnk * 512:bnk * 512 + nh * w].rearrange("p (h w) -> p h w", w=w),
                       [pO.name], [dst.name])

        def oslot(h, nh_per_bank, w):
            return pO[:, (h // nh_per_bank) * 512 + (h % nh_per_bank) * w:(h // nh_per_bank) * 512 + (h % nh_per_bank) * w + w]

        for t in range(NT):
            r0 = t * 128
            kq = B["qsb"].name
            c.ld(B["qsb"][:, 0:512], d["qs"][r0:r0 + 128, 0:512], "a_qs", ["qs"], [kq])
            c.ld(B["qsb"][:, 512:1560], d["qs"][r0:r0 + 128, C_AG:C_BQ + 512], "a_qs", ["qs"], [kq])
            c.ld(B["qsb"][:, 1560:4120], d["qs"][r0:r0 + 128, C_BZ:NIN], "a_qs", ["qs"], [kq])
            xsrc = d["xp"] if l == 0 else d["x1p"]
            c.ld(B["xres"][:, :], xsrc[r0:r0 + 128, :], "a_x", ["x1p"], [B["xres"].name])
            c.ts("vector", aqp[:].rearrange("p r g d -> p g r d"), B["qsb"][:, 0:512].rearrange("p (g r d) -> p g r d", g=2, r=4),
                 0.125, None, ALU.mult, None, [kq], [aqp.name])
            c.ts("vector", bqb[:], B["qsb"][:, 1048:1560], 0.125, None, ALU.mult, None, [kq], [bqb.name])
            for r in range(4):
                c.tr(pq[:, r, :], aqp[:, r, :, :].rearrange("p g d -> p (g d)"), idb[:], [aqp.name, "identb"], [pq.name])
                c.tr(pq[:, 4 + r, :], bqb[:, r * 128:(r + 1) * 128], idb[:], [bqb.name, "identb"], [pq.name])
            c.copy("scalar", qT[:], pq[:], [pq.name], [qT.name])
            Mt = min(8 * t + 7, NCP)
            ntiles = [(0, min(Mt, 128))] + ([(128, Mt - 128)] if Mt > 128 else [])
            for bnk in range(3):
                self.zero_bank(A, pO[:, bnk * 512:(bnk + 1) * 512], pO.name)
            items = []
            for g in range(2):
                for (n0, M) in ntiles:
                    ps_ = pS[ring[0] % 2]
                    ring[0] += 1
                    et = ET[eti[0] % 2]
                    eti[0] += 1

                    def stA(g=g, n0=n0, M=M, ps_=ps_, et=et):
                        lo_band = 8 * t - 8
                        has_band = (lo_band < n0 + M) and (8 * t + 6 >= n0)
                        c.mm(ps_[0:M, :], ckT[64 * g:64 * g + 64, n0:n0 + M], qT[64 * g:64 * g + 64, 0:4, :].rearrange("p r q -> p (r q)"),
                             True, not has_band, [ckT.name, qT.name], [ps_.name])
                        if has_band:
                            s0 = 256 - (lo_band - n0)
                            c.mm(ps_[0:M, :], A["zc"][:, s0:s0 + M], A["band"][:, 4 * g:4 * g + 4, :].rearrange("p r q -> p (r q)"),
                                 False, True, ["zc", "band"], [ps_.name])
                        c.act(et[0:M, 0:512], ps_[0:M, :], AF.Exp, [ps_.name], [et.name])

                    def stB(g=g, n0=n0, M=M, et=et):
                        for r in range(4):
                            h = 4 * g + r
                            c.mm(oslot(h, 3, 129)[:, :], et[0:M, r * 128:(r + 1) * 128], cvU[0:M, n0 // 128, g, :], False, False,
                                 [et.name, cvU.name], [pO.name])
                    items.append((stA, stB))
            self.run_pipe(items)
            evac_O(B["Ocmp"], 3, 129)
            ko = B["Ocmp"].name
            c.ts("vector", thr[:, :], B["Ocmp"][:, :, 64], 1e-30, None, ALU.max, None, [ko], [thr.name])
            kb.op("vector", lambda e: e.reciprocal(out=thr[:, :], in_=thr[:, :]), reads=[thr.name], writes=[thr.name])
            U = B["w3"][:, 0:512].rearrange("p (h j) -> p h j", h=8)
            c.tt("vector", U, B["Ocmp"][:, :, 65:129], thr[:, :].unsqueeze(2).broadcast_to([128, 8, 64]), ALU.mult,
                 [ko, thr.name], [B["w3"].name])
            for g in range(2):
                kb.op("vector", lambda e, g=g: e.tensor_reduce(out=impa[:, g, :], in_=B["w3"][:, g * 256:(g + 1) * 256].rearrange("p (r j) -> p j r", r=4),
                                                                   axis=AX.X, op=ALU.add), reads=[B["w3"].name], writes=[impa.name])
            c.tt("vector", impa[:], impa[:], A["G"][:, NSLC - 2 * t:NSLC - 2 * t + 64].unsqueeze(1).broadcast_to([128, 2, 64]), ALU.add,
                 [impa.name, "G"], [impa.name])
            c.ts("vector", impa[:, :, 0:1], impa[:, :, 0:1], 1e4, None, ALU.add, None, [impa.name], [impa.name])
            for g in range(2):
                if NSLC > 16:
                    kb.op("vector", lambda e, g=g: e.max(out=m8[:, 0, :], in_=impa[:, g, :]), reads=[impa.name], writes=[m8.name])
                    kb.op("vector", lambda e, g=g: e.match_replace(out=impr[:], in_to_replace=m8[:, 0, :], in_values=impa[:, g, :],
                                                                     imm_value=-2e9), reads=[impa.name, m8.name], writes=[impr.name])
                    kb.op("vector", lambda e: e.max(out=m8[:, 1, :], in_=impr[:]), reads=[impr.name], writes=[m8.name])
                    c.ts("vector", thr[:, g:g + 1], m8[:, 1, 7:8], -1e8, None, ALU.max, None, [m8.name], [thr.name])
                else:
                    kb.op("vector", lambda e, g=g: e.memset(thr[:, g:g + 1], -1e8), writes=[thr.name])
                c.ts("vector", mb[:, g, :], impa[:, g, :], thr[:, g:g + 1], NEG, ALU.is_lt, ALU.mult, [impa.name, thr.name], [mb.name])
                c.tr(pm[0:64, g * 128:(g + 1) * 128], mb[:, g, :], idb[:], [mb.name, "identb"], [pm.name])
            c.copy("vector", mbT[:], pm[0:64, 0:256].rearrange("p (g q) -> p g q", g=2).unsqueeze(2).broadcast_to([64, 2, 4, 128]),
                   [pm.name], [mbT.name])
            for (branch, ktsrc, vsrc, dstO) in (("slc", "kts", "vs", "Oslc"), ("win", "ktw", "vw", "Owin")):
                k_lo = 0 if branch == "slc" else max(0, t - 4)
                for bnk in range(2):
                    self.zero_bank(A, pO[:, bnk * 512:(bnk + 1) * 512], pO.name)
                items = []
                for kt in range(k_lo, t + 1):
                    ri = kvr[0] % NR
                    kvr[0] += 1
                    for g in range(2):
                        ps_ = pS[ring[0] % 2]
                        ring[0] += 1
                        et = ET[eti[0] % 2]
                        eti[0] += 1

                        def stA(kt=kt, g=g, ri=ri, ps_=ps_, et=et, branch=branch, ktsrc=ktsrc, vsrc=vsrc):
                            if g == 0:
                                c.ld(ksT[ri][:], d[ktsrc][:, kt * 128:(kt + 1) * 128], "a_k%d" % ri, [ktsrc], [ksT[ri].name])
                                c.ld(vsb[ri][:, :, 0:64], d[vsrc][kt * 128:(kt + 1) * 128, :].rearrange("p (g d) -> p g d", g=2),
                                     "a_v%d" % ri, [vsrc], [vsb[ri].name])
                            extra = []
                            if branch == "slc":
                                extra.append((A["ewide"][:, kt * 128:(kt + 1) * 128], mbT[:, g, :, :].rearrange("p r q -> p (r q)"),
                                              ["ewide", mbT.name]))
                            if kt == t:
                                extra.append((idf[:], self.diag[:, 4 * g:4 * g + 4, :].rearrange("p r q -> p (r q)"), ["identf", "diag"]))
                            elif kt == t - 1:
                                extra.append((idf[:], self.off[:, 4 * g:4 * g + 4, :].rearrange("p r q -> p (r q)"), ["identf", "off"]))
                            if branch == "win" and kt == t - 4:
                                extra.append((idb[:], A["anti"][:].rearrange("p r q -> p (r q)"), ["identb", "anti"]))
                            c.mm(ps_[:, :], ksT[ri][64 * g:64 * g + 64, :], qT[64 * g:64 * g + 64, 0:4, :].rearrange("p r q -> p (r q)"),
                                 True, len(extra) == 0, [ksT[ri].name, qT.name], [ps_.name])
                            for ei, (lt, rh, rd) in enumerate(extra):
                                c.mm(ps_[:, :], lt, rh, False, ei == len(extra) - 1, rd, [ps_.name])
                            c.act(et[:, 0:512], ps_[:, :], AF.Exp, [ps_.name], [et.name])

                        def stB(g=g, ri=ri, et=et):
                            for r in range(4):
                                h = 4 * g + r
                                c.mm(oslot(h, 4, 65)[:, :], et[:, r * 128:(r + 1) * 128], vsb[ri][:, g, :], False, False,
                                     [et.name, vsb[ri].name], [pO.name])
                        items.append((stA, stB))
                self.run_pipe(items)
                evac_O(B[dstO], 4, 65)
            nb = t // 2
            if t >= 2 and t % 2 == 0:
                n_new = nb - 1
                c.ld(kblk[:], d["ktm"][:, :, n_new * 256:(n_new + 1) * 256], "a_kblk", ["ktm"], [kblk.name])
                kb.op("vector", lambda e: e.tensor_reduce(out=kmsum[:], in_=kblk[:], axis=AX.X, op=ALU.add), reads=[kblk.name], writes=[kmsum.name])
                c.ts("vector", kmeanT[:, :, n_new], kmsum[:], 1.0 / 256.0, None, ALU.mult, None, [kmsum.name], [kmeanT.name])
            if nb > 0:
                pg_ = pS[ring[0] % 2]
                ring[0] += 1
                for h in range(8):
                    hp, hj = 64 * (h % 2), h // 2
                    c.mm(pg_[:, h * 16:h * 16 + nb], qT[hp:hp + 64, 4 + hj, :], kmeanT[hp:hp + 64, hj, 0:nb], True, True,
                         [qT.name, kmeanT.name], [pg_.name])
                kb.op("vector", lambda e: e.memset(gsb[:], -1e9), writes=[gsb.name])
                c.copy("vector", gsb[:, :, 0:nb], pg_[:, 0:128].rearrange("p (h n) -> p h n", h=8)[:, :, 0:nb], [pg_.name], [gsb.name])
                for h in range(8):
                    kb.op("vector", lambda e, h=h: e.max(out=m8[:, h, :], in_=gsb[:, h, :]), reads=[gsb.name], writes=[m8.name])
                c.ts("vector", thr[:, :], m8[:, :, 2], -1e8, None, ALU.max, None, [m8.name], [thr.name])
                c.tt("vector", gsb[:], gsb[:], thr[:, :].unsqueeze(2).broadcast_to([128, 8, 16]), ALU.is_lt, [gsb.name, thr.name], [gsb.name])
                c.ts("vector", mbm[:], gsb[:], NEG, None, ALU.mult, None, [gsb.name], [mbm.name])
                kb.op("vector", lambda e: e.memset(mbm[:, :, nb:16], 0.0), reads=[mbm.name], writes=[mbm.name])
                for h in range(8):
                    c.tr(pm[0:16, h * 128:(h + 1) * 128], mbm[:, h, :], idb[:], [mbm.name, "identb"], [pm.name])
                c.copy("vector", mbmT[:], pm[0:16, :].rearrange("p (h q) -> p h q", h=8), [pm.name], [mbmT.name])
            else:
                kb.op("vector", lambda e: e.memset(mbmT[:], 0.0), writes=[mbmT.name])
            for bnk in range(2):
                self.zero_bank(A, pO[:, bnk * 512:(bnk + 1) * 512], pO.name)
            items = []
            for kt in range(0, t + 1):
                ri = kvr[0] % NR
                kvr[0] += 1
                et = ET[eti[0] % 2]
                eti[0] += 1

                def stA(kt=kt, ri=ri, et=et):
                    c.ld(kmT[ri][:], d["ktm"][:, :, kt * 128:(kt + 1) * 128], "a_km%d" % ri, ["ktm"], [kmT[ri].name])
                    c.ld(vmb[ri][:, :, 0:64], d["vm"][kt * 128:(kt + 1) * 128, :].rearrange("p (h d) -> p h d", h=8), "a_vm%d" % ri,
                         ["vm"], [vmb[ri].name])
                    for bnk in range(2):
                        ps_ = pS[bnk]
                        c.mm(ps_[:, :], A["rsm"][:, kt // 2, :], mbmT[:, 4 * bnk:4 * bnk + 4, :].rearrange("p h q -> p (h q)"), True, False,
                             ["rsm", mbmT.name], [ps_.name])
                        for hh in range(4):
                            h = 4 * bnk + hh
                            hp, hj = 64 * (h % 2), h // 2
                            last = (hh == 3) and not (kt >= t - 1)
                            c.mm(ps_[:, hh * 128:(hh + 1) * 128], kmT[ri][hp:hp + 64, hj, :], qT[hp:hp + 64, 4 + hj, :], False, last,
                                 [kmT[ri].name, qT.name], [ps_.name])
                        if kt == t:
                            c.mm(ps_[:, :], idf[:], self.diag[:, 8 + 4 * bnk:12 + 4 * bnk, :].rearrange("p r q -> p (r q)"), False, True,
                                 ["identf", "diag"], [ps_.name])
                        elif kt == t - 1:
                            c.mm(ps_[:, :], idf[:], self.off[:, 8 + 4 * bnk:12 + 4 * bnk, :].rearrange("p r q -> p (r q)"), False, True,
                                 ["identf", "off"], [ps_.name])
                        c.act(et[:, bnk * 512:(bnk + 1) * 512], ps_[:, :], AF.Exp, [ps_.name], [et.name])

                def stB(ri=ri, et=et):
                    for h in range(8):
                        c.mm(oslot(h, 4, 65)[:, :], et[:, h * 128:(h + 1) * 128], vmb[ri][:, h, :], False, False,
                             [et.name, vmb[ri].name], [pO.name])
                items.append((stA, stB))
            self.run_pipe(items)
            evac_O(B["Omob"], 4, 65)
            if self.cfg.get("DBG_T") == t and l == 0:
                self.dump(0, B["Ocmp"][:].rearrange("p h w -> p (h w)"), B["Ocmp"].name, 8 * 129)
                self.dump(1, B["Oslc"][:].rearrange("p h w -> p (h w)"), B["Oslc"].name, 8 * 65)
                self.dump(2, B["Owin"][:].rearrange("p h w -> p (h w)"), B["Owin"].name, 8 * 65)
                self.dump(3, B["Omob"][:].rearrange("p h w -> p (h w)"), B["Omob"].name, 8 * 65)
                self.dump(4, impa[:].rearrange("p g j -> p (g j)"), impa.name, 128)
                self.dump(5, gsb[:].rearrange("p h n -> p (h n)"), gsb.name, 128)
            self.epilogue(A, S, l, 128, B, ydst[r0:r0 + 128, :])


    def sample_indices(self, les, l):
        c, kb, d, NS, NPG = self, self.kb, self.d, self.NS, self.NPG
        idxs = [[c.sb(les, "idx", [128, NPG], I32) for _ in range(3)] for _ in range(NS)]
        with ExitStack() as ph:
            pti = c.sb(ph, "pti", [128, NPG], I32)
            ptf = c.sb(ph, "ptf", [128, NPG], F32)
            io = c.sb(ph, "iop", [128, 1], F32)
            kb.op("gpsimd", lambda e: e.iota(io[:], pattern=[[0, 1]], base=l * 128, channel_multiplier=1,
                                             allow_small_or_imprecise_dtypes=True), writes=[io.name])
            for s_ in range(NS):
                idx = idxs[s_]
                c.ld(pti[:], d["pt"][s_:s_ + 1, :].partition_broadcast(128), "si_pt", [], [pti.name])
                c.copy("vector", ptf[:], pti[:], [pti.name], [ptf.name])
                c.ts("vector", ptf[:], ptf[:], float(self.DEPTH * 128), io[:, 0:1], ALU.mult, ALU.add, [ptf.name, io.name], [ptf.name])
                c.copy("vector", idx[0][:], ptf[:], [ptf.name], [idx[0].name])
                c.ts("vector", ptf[:], ptf[:], 2.0, None, ALU.mult, None, [ptf.name], [ptf.name])
                c.copy("vector", idx[1][:], ptf[:], [ptf.name], [idx[1].name])
                c.ts("vector", ptf[:], ptf[:], 1.0, None, ALU.add, None, [ptf.name], [ptf.name])
                c.copy("vector", idx[2][:], ptf[:], [ptf.name], [idx[2].name])
            kb.barrier()
        return idxs

    def gather(self, out, pool, idx, page, slot, wkey):
        self.kb.dma("gpsimd", lambda e: e.indirect_dma_start(out=out, out_offset=None, in_=pool[:, :],
                                                              in_offset=bass.IndirectOffsetOnAxis(ap=idx[:, page:page + 1], axis=0)),
                    slot, reads=[idx.name], writes=[wkey])

    def phase_C_sample(self, les, ph, W, l, idxs):
        c, kb, d, NS, NPG, PAST = self, self.kb, self.d, self.NS, self.NPG, self.PAST
        n = self.NCS
        NTS = (n + 127) // 128
        res = []
        ktc = c.sb(ph, "ktcs", [128, PAST], BF16)
        vtc = c.sb(ph, "vtcs", [128, PAST], BF16)
        pg = [c.sb(ph, "pgc", [128, 256], F32) for _ in range(3)]
        ptrs = [c.ps(ph, "ptrc", [128, 4, 128], F32) for _ in range(2)]
        for s_ in range(NS):
            ckT, cvU = les[s_]
            kb.op("gpsimd", lambda e, ckT=ckT: e.memset(ckT[:], 0.0), writes=[ckT.name])
            kb.op("gpsimd", lambda e, cvU=cvU: e.memset(cvU[:], 0.0), writes=[cvU.name])
            kb.op("gpsimd", lambda e, cvU=cvU: e.memset(cvU[:, :, :, 64:65], 1.0), writes=[cvU.name])
            for page in range(NPG):
                b = pg[page % 3]
                ptr = ptrs[page % 2]
                self.gather(b[:], d["pcmp"], idxs[s_][0], page, "cs_pg%d" % (page % 3), b.name)
                if self.cfg.get("SC", 9) < 2:
                    continue
                for kv in range(2):
                    c.tr(ptr[:, kv, :], b[:, kv * 128:(kv + 1) * 128], self.ident_f[:], [b.name, "identf"], [ptr.name])
                c.copy("vector", ktc[:, page * 128:(page + 1) * 128], ptr[:, 0, :], [ptr.name], [ktc.name])
                c.copy("scalar", vtc[:, page * 128:(page + 1) * 128], ptr[:, 1, :], [ptr.name], [vtc.name])

            def cv_write(nt_, rows, p2, cvU=cvU):
                c.tt("vector", cvU[0:rows, nt_, :, 0:64], p2[0:rows, 0:128].rearrange("p (g d) -> p g d", g=2),
                     W["b2bc"][0:rows, :].rearrange("p (g d) -> p g d", g=2), ALU.add, [p2.name, W["b2bc"].name], [cvU.name])
            if self.cfg.get("SC", 9) >= 3:
                self.compress(ph, W, ktc, vtc, n, ckT, cv_write, "s%d" % s_)
            res.append((ckT, cvU))
        return res

    def phaseA_sample_setup(self, ph):
        c, kb, d, PAST, NPG = self, self.kb, self.d, self.PAST, self.NPG
        n = self.NCS
        NTS = (n + 127) // 128
        A = {}
        ov = c.sb(ph, "ovls", [128, NTS, 132], BF16)
        kb.op("gpsimd", lambda e: e.memset(ov[:], 0.0), writes=[ov.name])
        for nt_ in range(NTS):
            rows = min(128, n + 1 - nt_ * 128)
            c.ld(ov[0:rows, nt_, :], d["ovl_s"][nt_ * 128:nt_ * 128 + rows, :], "ss_ov", [], [ov.name], eng="gpsimd")
        A["ovl"] = ov
        cb = c.sb(ph, "cbias", [128, NTS, 8], F32)
        kb.op("gpsimd", lambda e: e.memset(cb[:], 0.0), writes=[cb.name])
        c0 = PAST - 31 - 2048 * (NTS - 1)
        p16 = VL + 16
        src = AP(tensor=d["sk16"].tensor, offset=VOFF + c0 - 16 * 96, ap=[[VL, 32], [128 * p16, 8]])
        c.ld(cb[96:128, NTS - 1, :], src, "ss_cb", [], [cb.name])
        A["cbias"] = cb
        e2 = c.sb(ph, "e2", [1, 3, 128], BF16)
        kb.op("gpsimd", lambda e: e.memset(e2[:], 0.0), writes=[e2.name])
        kb.op("gpsimd", lambda e: e.memset(e2[0:1, 0, 0:64], 1.0), writes=[e2.name])
        kb.op("gpsimd", lambda e: e.memset(e2[0:1, 1, 64:128], 1.0), writes=[e2.name])
        kb.op("gpsimd", lambda e: e.memset(e2[0:1, 2, 0:1], 1.0), writes=[e2.name])
        A["e2"] = e2
        negrow = c.sb(ph, "negrow", [1, 512], BF16)
        kb.op("gpsimd", lambda e: e.memset(negrow[:], 0.0), writes=[negrow.name])
        kb.op("gpsimd", lambda e: e.memset(negrow[0:1, 0:8], NEG), writes=[negrow.name])
        A["negrow"] = negrow
        off0 = c.sb(ph, "off0", [128, 16], F32)
        d0 = c.sb(ph, "d0", [1, 16], F32)
        c.copy("vector", off0[:], self.off[:, :, 0], ["off"], [off0.name])
        c.copy("vector", d0[:], self.diag[0:1, :, 0], ["diag"], [d0.name])
        A["off0"], A["d0"] = off0, d0
        return A

    def phase_A_sample(self, ph, A, AS, S, l, s_, idx, ckT, cvU, ydst, xsrc):
        c, kb, d, SEQ, NS, NPG, PAST = self, self.kb, self.d, self.SEQ, self.NS, self.NPG, self.PAST
        n = self.NCS
        NTS = (n + 127) // 128
        NSL = PAST // 64 + 1
        NBM = PAST // 256
        row = SEQ + s_
        idb, idf = self.ident_b, self.ident_f
        B = {}
        B["qsb"] = c.sb(ph, "qsbs", [1, 4120], F32)
        B["xres"] = c.sb(ph, "xress", [1, D], F32)
        B["gate"] = c.sb(ph, "gates", [1, D], F32)
        B["w1"] = c.sb(ph, "wk1s", [1, 1024], F32)
        B["w2"] = c.sb(ph, "wk2s", [1, 1024], F32)
        B["w3"] = c.sb(ph, "wk3s", [1, 1024], F32)
        B["small"] = c.sb(ph, "smalls", [1, 64], F32)
        B["oz"] = c.sb(ph, "ozs", [1, 1024], BF16)
        B["ozT"] = c.sb(ph, "ozTs", [128, 8, 128], BF16)
        B["stats"] = c.sb(ph, "statss", [1, 2, 6], F32)
        for nm in ("Ocmp", "Oslc", "Owin", "Omob"):
            B[nm] = c.sb(ph, nm + "s", [1, 8, 65], F32)
        pT = c.ps(ph, "pTs", [128, 8, 128], BF16)
        pY = c.ps(ph, "pYs", [128, 2048], F32)
        pS = c.ps(ph, "pSs", [128, 512], F32)
        pO = c.ps(ph, "pOs", [128, 512], F32)
        pD = c.ps(ph, "pDs", [128, 512], F32)
        B["pq"], B["pY"] = pT, pY
        kq = B["qsb"].name
        c.ld(B["qsb"][:, 0:512], d["qs"][row:row + 1, 0:512], "s_qs", ["qs"], [kq])
        c.ld(B["qsb"][:, 512:1560], d["qs"][row:row + 1, C_AG:C_BQ + 512], "s_qs", ["qs"], [kq])
        c.ld(B["qsb"][:, 1560:4120], d["qs"][row:row + 1, C_BZ:NIN], "s_qs", ["qs"], [kq])
        c.ld(B["xres"][:, :], xsrc[s_:s_ + 1, :], "s_x", ["x1s"], [B["xres"].name])
        c.ld(B["gate"][:, :], d["gsr"][s_:s_ + 1, :], "s_g", ["gsr"], [B["gate"].name])
        qf = c.sb(ph, "qbdf", [128, 40], F32)
        qbd = c.sb(ph, "qbd", [128, 40], BF16)
        kb.op("gpsimd", lambda e: e.memset(qf[:], 0.0), writes=[qf.name])
        for g in range(2):
            src = AP(tensor=d["qs"].tensor, offset=row * NIN + g * 256, ap=[[1, 64], [64, 4]])
            c.ld(qf[64 * g:64 * g + 64, 4 * g:4 * g + 4], src, "s_qb", ["qs"], [qf.name])
        for h in range(8):
            e_, j = h % 2, h // 2
            src = AP(tensor=d["qs"].tensor, offset=row * NIN + C_BQ + h * 64, ap=[[1, 64], [1, 1]])
            c.ld(qf[64 * e_:64 * e_ + 64, 8 + 8 * j + h:9 + 8 * j + h], src, "s_qb", ["qs"], [qf.name])
        c.ts("vector", qbd[:], qf[:], 0.125, None, ALU.mult, None, [qf.name], [qbd.name])
        kn = c.sb(ph, "kn", [128, 8], BF16)
        vn = c.sb(ph, "vn", [1, 768], BF16)
        c.ld(kn[:, 0:1], d["kts"][:, row:row + 1], "s_kn", ["kts"], [kn.name])
        c.ld(kn[:, 1:2], d["ktw"][:, row:row + 1], "s_kn", ["ktw"], [kn.name])
        c.ld(kn[:, 2:6], d["ktm"][:, :, row], "s_kn", ["ktm"], [kn.name])
        c.ld(vn[:, 0:128], d["vs"][row:row + 1, :], "s_vn", ["vs"], [vn.name])
        c.ld(vn[:, 128:256], d["vw"][row:row + 1, :], "s_vn", ["vw"], [vn.name])
        c.ld(vn[:, 256:768], d["vm"][row:row + 1, :], "s_vn", ["vm"], [vn.name])
        sc = c.sb(ph, "scr", [128, 512], F32)
        pt_ = [c.sb(ph, "ptile", [128, 16], BF16) for _ in range(3)]
        pown = c.sb(ph, "pown", [1, 16], BF16)
        so = c.sb(ph, "sown", [1, 16], F32)
        orow = c.sb(ph, "orow", [1, 528], F32)
        ring = [0]

        def own_scores(kcols, q0, qstep, bias_ap):
            for i, kc in enumerate(kcols):
                c.mm(pD[0:1, 0:8], kn[:, kc:kc + 1], qbd[:, q0 + i * qstep:q0 + i * qstep + 8], i == 0, i == len(kcols) - 1,
                     [kn.name, qbd.name], [pD.name])
            c.tt("vector", so[:, 0:8], pD[0:1, 0:8], bias_ap, ALU.add, [pD.name, AS["d0"].name], [so.name])
            c.act(pown[:, 0:8], so[:, 0:8], AF.Exp, [so.name], [pown.name])

        def finish_branch(dst, vcol0, per_head_v):
            for h in range(8):
                vo = vcol0 + (h * 64 if per_head_v else (h // 4) * 64)
                c.mm(pO[0:1, h * 64:(h + 1) * 64], pown[0:1, h:h + 1], vn[0:1, vo:vo + 64], False, False, [pown.name, vn.name], [pO.name])
            c.mm(pD[0:1, 16:24], self.ones_b[0:1, 0:1], pown[0:1, 0:8], False, False, ["onesb", pown.name], [pD.name])
            c.copy("vector", dst[0:1, :, 0:64], pO[0:1, 0:512].rearrange("p (h d) -> p h d", h=8), [pO.name], [dst.name])
            c.copy("vector", dst[0:1, :, 64], pD[0:1, 16:24], [pD.name], [dst.name])

        def open_banks():
            self.zero_bank(A, pO[:, :], pO.name)
            self.zero_bank(A, pD[:, :], pD.name)

        def pv_tile(p_ap, pkey, v_ap_of_head, vkey):
            for h in range(8):
                c.mm(pO[0:1, h * 64:(h + 1) * 64], p_ap[:, h:h + 1], v_ap_of_head(h), False, False, [pkey, vkey], [pO.name])
            c.mm(pD[0:1, 16:24], self.ones_b[:, 0:1], p_ap[:, 0:8], False, False, ["onesb", pkey], [pD.name])

        for nt_ in range(NTS):
            c.mm(pS[:, nt_ * 8:(nt_ + 1) * 8], ckT[:, nt_ * 128:(nt_ + 1) * 128], qbd[:, 0:8], True, True, [ckT.name, qbd.name], [pS.name])
        c.tt("vector", sc[:, 0:NTS * 8], pS[:, 0:NTS * 8], AS["cbias"][:].rearrange("p t h -> p (t h)"), ALU.add,
             [pS.name, AS["cbias"].name], [sc.name])
        ec = c.sb(ph, "ecT", [128, NTS * 8], BF16)
        c.act(ec[:], sc[:, 0:NTS * 8], AF.Exp, [sc.name], [ec.name])
        for h in range(8):
            g = h // 4
            for nt_ in range(NTS):
                c.mm(pO[0:1, h * 64:h * 64 + 64], ec[:, nt_ * 8 + h:nt_ * 8 + h + 1], cvU[:, nt_, g, 0:64], nt_ == 0, nt_ == NTS - 1,
                     [ec.name, cvU.name], [pO.name])
        for h in range(8):
            for nt_ in range(NTS):
                c.mm(pD[0:1, 16 + h:17 + h], ec[:, nt_ * 8 + h:nt_ * 8 + h + 1], cvU[:, nt_, 0, 64:65], nt_ == 0, nt_ == NTS - 1,
                     [ec.name, cvU.name], [pD.name])
        for h in range(8):
            o = (h // 3) * 512 + (h % 3) * 132
            for nt_ in range(NTS):
                c.mm(pY[0:1, o:o + 132], ec[:, nt_ * 8 + h:nt_ * 8 + h + 1], AS["ovl"][:, nt_, :], nt_ == 0, nt_ == NTS - 1,
                     [ec.name, AS["ovl"].name], [pY.name])
        c.copy("vector", B["Ocmp"][0:1, :, 0:64], pO[0:1, 0:512].rearrange("p (h d) -> p h d", h=8), [pO.name], [B["Ocmp"].name])
        c.copy("vector", B["Ocmp"][0:1, :, 64], pD[0:1, 16:24], [pD.name], [B["Ocmp"].name])
        rd = c.sb(ph, "rdc", [1, 8], F32)
        un = c.sb(ph, "un", [1, 8, 132], F32)
        impa = c.sb(ph, "impas", [1, 2, 136], F32)
        impr = c.sb(ph, "imprs", [1, 136], F32)
        m8 = c.sb(ph, "m8s", [1, 8, 8], F32)
        thr = c.sb(ph, "thrs", [1, 8], F32)
        mbs = c.sb(ph, "mbs", [1, 2, 136], F32)
        c.ts("vector", rd[:], B["Ocmp"][0:1, :, 64], 1e-30, None, ALU.max, None, [B["Ocmp"].name], [rd.name])
        kb.op("vector", lambda e: e.reciprocal(out=rd[:], in_=rd[:]), reads=[rd.name], writes=[rd.name])
        for bnk in range(3):
            h0 = bnk * 3
            nh = min(3, 8 - h0)
            c.tt("vector", un[0:1, h0:h0 + nh, :], pY[0:1, bnk * 512:bnk * 512 + nh * 132].rearrange("p (h j) -> p h j", j=132),
                 rd[0:1, h0:h0 + nh].unsqueeze(2).broadcast_to([1, nh, 132]), ALU.mult, [pY.name, rd.name], [un.name])
        kb.op("vector", lambda e: e.memset(impa[:], -1e9), writes=[impa.name])
        for g in range(2):
            kb.op("vector", lambda e, g=g: e.tensor_reduce(out=impa[0:1, g, 0:132], in_=un[0:1, 4 * g:4 * g + 4, :].rearrange("p r j -> p j r"),
                                                               axis=AX.X, op=ALU.add), reads=[un.name], writes=[impa.name])
        kb.op("vector", lambda e: e.memset(impa[0:1, :, NSL:136], -1e9), reads=[impa.name], writes=[impa.name])
        for j in (0, NSL - 2, NSL - 1):
            c.ts("vector", impa[0:1, :, j:j + 1], impa[0:1, :, j:j + 1], 1e4, None, ALU.add, None, [impa.name], [impa.name])
        for g in range(2):
            if NSL > 16:
                kb.op("vector", lambda e, g=g: e.max(out=m8[0:1, 0, :], in_=impa[0:1, g, :]), reads=[impa.name], writes=[m8.name])
                kb.op("vector", lambda e, g=g: e.match_replace(out=impr[:], in_to_replace=m8[0:1, 0, :], in_values=impa[0:1, g, :],
                                                                 imm_value=-2e9), reads=[impa.name, m8.name], writes=[impr.name])
                kb.op("vector", lambda e: e.max(out=m8[0:1, 1, :], in_=impr[:]), reads=[impr.name], writes=[m8.name])
                c.ts("vector", thr[0:1, g:g + 1], m8[0:1, 1, 7:8], -1e8, None, ALU.max, None, [m8.name], [thr.name])
            else:
                kb.op("vector", lambda e, g=g: e.memset(thr[0:1, g:g + 1], -1e8), writes=[thr.name])
            c.ts("vector", mbs[0:1, g, :], impa[0:1, g, :], thr[0:1, g:g + 1], NEG, ALU.is_lt, ALU.mult, [impa.name, thr.name], [mbs.name])
        if self.cfg.get("SDBG", 9) == 3:
            return
        mrow = c.sb(ph, "mrow", [1, 2, 512], BF16)
        for jj in range(2):
            c.copy("vector", mrow[0:1, jj, 0:NPG * 8].rearrange("p (k g r) -> p k g r", g=2, r=4),
                   mbs[0:1, :, jj:jj + 2 * NPG - 1:2].rearrange("p g k -> p k g").unsqueeze(3).broadcast_to([1, NPG, 2, 4]),
                   [mbs.name], [mrow.name])
        own_scores([0], 0, 0, AS["d0"][0:1, 0:8])
        open_banks()
        W_ = NPG * 8
        c.mm(pS[:, 0:W_], AS["e2"][0:1, 0, :], mrow[0:1, 0, 0:W_], True, False, [AS["e2"].name, mrow.name], [pS.name])
        c.mm(pS[:, 0:W_], AS["e2"][0:1, 1, :], mrow[0:1, 1, 0:W_], False, False, [AS["e2"].name, mrow.name], [pS.name])
        pgs = [c.sb(ph, "pgs", [128, 256], F32) for _ in range(3)]
        pgw = [c.sb(ph, "pgw", [128, 256], BF16) for _ in range(3)]
        vsl = [c.sb(ph, "vsl", [128, 128], BF16) for _ in range(3)]
        kst = [c.sb(ph, "kst", [128, 128], BF16) for _ in range(2)]
        pTf = pY[:, 1536:2048].rearrange("p (j k) -> p j k", j=4)
        items = []
        for page in range(NPG):
            b = pgs[page % 3]
            k_ = kst[page % 2]
            v_ = vsl[page % 3]
            p_ = pt_[page % 3]

            def s0(page=page, b=b, k_=k_, v_=v_):
                self.gather(b[:], d["pslc"], idx[0], page, "s_pg%d" % (page % 3), b.name)
                c.tr(pTf[:, page % 4, :], b[:, 0:128], idf[:], [b.name, "identf"], [pY.name])
                c.copy("vector", k_[:], pTf[:, page % 4, :], [pY.name], [k_.name])
                c.copy("scalar", v_[:], b[:, 128:256], [b.name], [v_.name])

            def s1(page=page, k_=k_, p_=p_):
                c.mm(pS[:, page * 8:(page + 1) * 8], k_[:], qbd[:, 0:8], False, False, [k_.name, qbd.name], [pS.name])
                if page == NPG - 1:
                    c.tt("vector", sc[:, 0:8], pS[:, page * 8:(page + 1) * 8], AS["off0"][:, 0:8], ALU.add, [pS.name, AS["off0"].name], [sc.name])
                    c.act(p_[:, 0:8], sc[:, 0:8], AF.Exp, [sc.name], [p_.name])
                else:
                    c.act(p_[:, 0:8], pS[:, page * 8:(page + 1) * 8], AF.Exp, [pS.name], [p_.name])

            def s2(p_=p_, v_=v_):
                pv_tile(p_, p_.name, lambda h, v_=v_: v_[:, (h // 4) * 64:64 + (h // 4) * 64], v_.name)
            items.append((s0, s1, s2))
        self.run_stages(items)
        finish_branch(B["Oslc"], 0, False)
        if self.cfg.get("SDBG", 9) == 4:
            return
        own_scores([1], 0, 0, AS["d0"][0:1, 0:8])
        open_banks()
        c.mm(pS[:, 0:32], AS["e2"][0:1, 2, :], AS["negrow"][0:1, 0:32], True, False, [AS["e2"].name, AS["negrow"].name], [pS.name])
        for wt in range(4):
            b = pgw[wt % 3]
            k_ = kst[wt % 2]
            c.ld(b[:], d["wst"][l, s_, wt * 128:(wt + 1) * 128, :], "s_pw%d" % (wt % 3), [], [b.name], eng="gpsimd")
            c.tr(pT[:, 0, :], b[:, 0:128], idb[:], [b.name, "identb"], [pT.name])
            c.copy("vector", k_[:], pT[:, 0, :], [pT.name], [k_.name])
            c.mm(pS[:, wt * 8:(wt + 1) * 8], k_[:], qbd[:, 0:8], False, False, [k_.name, qbd.name], [pS.name])
            p_ = pt_[ring[0] % 3]
            ring[0] += 1
            if wt == 3:
                c.tt("vector", sc[:, 0:8], pS[:, wt * 8:(wt + 1) * 8], AS["off0"][:, 0:8], ALU.add, [pS.name, AS["off0"].name], [sc.name])
                c.act(p_[:, 0:8], sc[:, 0:8], AF.Exp, [sc.name], [p_.name])
            else:
                c.act(p_[:, 0:8], pS[:, wt * 8:(wt + 1) * 8], AF.Exp, [pS.name], [p_.name])
            pv_tile(p_, p_.name, lambda h, b=b: b[:, 128 + (h // 4) * 64:192 + (h // 4) * 64], b.name)
        finish_branch(B["Owin"], 128, False)
        if self.cfg.get("SDBG", 9) == 5:
            return
        pgm = [c.sb(ph, "pgm", [128, 512], F32) for _ in range(3)]
        vmo = [c.sb(ph, "vmo", [128, 512], BF16) for _ in range(3)]
        kmt = [c.sb(ph, "kmts", [128, 4, 128], BF16) for _ in range(2)]
        items = []
        for page in range(NPG):
            b = pgm[page % 3]
            k_ = kmt[page % 2]

            def s0(page=page, b=b, k_=k_):
                self.gather(b[:], d["pmoba"], idx[1], page, "s_pm%d" % (page % 3), b.name)
                for j in range(4):
                    c.tr(pTf[:, j, :], b[:, j * 128:(j + 1) * 128], idf[:], [b.name, "identf"], [pY.name])
                c.copy("vector" if page % 2 == 0 else "scalar", k_[:], pTf[:, 0:4, :], [pY.name], [k_.name])

            def s1(page=page, k_=k_):
                for j in range(4):
                    c.mm(pS[:, page * 8:(page + 1) * 8], k_[:, j, :], qbd[:, 8 + 8 * j:16 + 8 * j], j == 0, j == 3, [k_.name, qbd.name], [pS.name])
            items.append((s0, s1))
        self.run_stages(items)
        c.copy("vector", sc[:, 0:W_], pS[:, 0:W_], [pS.name], [sc.name])
        c.mm(pD[0:1, 0:W_], self.ones_f[:, 0:1], sc[:, 0:W_], True, True, ["onesf", sc.name], [pD.name])
        NBP = max(NBM, 8)
        gsm = c.sb(ph, "gsm", [1, 8, NBP], F32)
        kb.op("vector", lambda e: e.memset(gsm[:], -1e9), writes=[gsm.name])
        c.copy("vector", orow[0:1, 0:W_], pD[0:1, 0:W_], [pD.name], [orow.name])
        pdv = orow[0:1, 0:W_].rearrange("p (n two h) -> p h n two", two=2, h=8)
        c.tt("vector", gsm[0:1, :, 0:NBM], pdv[:, :, :, 0], pdv[:, :, :, 1], ALU.add, [orow.name], [gsm.name])
        for h in range(8):
            kb.op("vector", lambda e, h=h: e.max(out=m8[0:1, h, :], in_=gsm[0:1, h, :]), reads=[gsm.name], writes=[m8.name])
        c.ts("vector", thr[0:1, :], m8[0:1, :, 2], -1e8, None, ALU.max, None, [m8.name], [thr.name])
        c.tt("vector", gsm[:], gsm[:], thr[0:1, :].unsqueeze(2).broadcast_to([1, 8, NBP]), ALU.is_lt, [gsm.name, thr.name], [gsm.name])
        c.ts("vector", mrow[0:1, 0, 0:W_].rearrange("p (n two h) -> p n two h", two=2, h=8),
             gsm[0:1, :, 0:NBM].rearrange("p h n -> p n h").unsqueeze(2).broadcast_to([1, NBM, 2, 8]), NEG, None, ALU.mult, None,
             [gsm.name], [mrow.name])
        c.mm(pY[:, 0:W_], self.ones_b[0:1, :], mrow[0:1, 0, 0:W_], True, True, ["onesb", mrow.name], [pY.name])
        c.tt("vector", sc[:, 0:W_], sc[:, 0:W_], pY[:, 0:W_], ALU.add, [sc.name, pY.name], [sc.name])
        c.tt("vector", sc[:, W_ - 8:W_], sc[:, W_ - 8:W_], AS["off0"][:, 8:16], ALU.add, [sc.name, AS["off0"].name], [sc.name])
        pm_ = c.sb(ph, "pmT", [128, 512], BF16)
        c.act(pm_[:, 0:W_], sc[:, 0:W_], AF.Exp, [sc.name], [pm_.name])
        own_scores([2, 3, 4, 5], 8, 8, AS["d0"][0:1, 8:16])
        open_banks()
        items = []
        for page in range(NPG):
            b = pgm[page % 3]
            v_ = vmo[page % 3]

            def s0(page=page, b=b, v_=v_):
                self.gather(b[:], d["pmoba"], idx[2], page, "s_pm%d" % (page % 3), b.name)
                c.copy(("vector", "scalar")[page % 2], v_[:], b[:], [b.name], [v_.name])

            def s1(page=page, v_=v_):
                pv_tile(pm_[:, page * 8:(page + 1) * 8], pm_.name, lambda h, v_=v_: v_[:, h * 64:(h + 1) * 64], v_.name)
            items.append((s0, s1))
        self.run_stages(items)
        finish_branch(B["Omob"], 256, True)
        if self.cfg.get("SDBG", 9) == 6:
            return
        self.epilogue(A, S, l, 1, B, ydst[s_:s_ + 1, :])

    def build(self):
        c = self
        self.declare()
        with ExitStack() as es:
            self.kb = KB(self.nc, es)
            kb = self.kb
            self.constants(es)
            self.epsc = c.sb(es, "epsc", [128, 1], F32)
            kb.op("gpsimd", lambda e: e.memset(self.epsc[:], LN_EPS), writes=["epsc"])
            stop = self.cfg.get("STOP")
            for l in range(self.DEPTH):
                last = (l == self.DEPTH - 1) or stop == "A%d" % l
                with ExitStack() as les:
                    S = self.phase_S(les, l)
                    kb.barrier()
                    self.phase_P(l, S)
                    kb.barrier()
                    if stop == "P%d" % l:
                        break
                    NTL = (self.NCP + 127) // 128
                    NTS = (self.NCS + 127) // 128
                    pre_p = (c.sb(les, "ckT", [128, NTL * 128], BF16), c.sb(les, "cvU", [128, NTL, 2, 129], BF16))
                    pre_s = [(c.sb(les, "ckTs", [128, NTS * 128], BF16), c.sb(les, "cvUs", [128, NTS, 2, 65], BF16))
                             for _ in range(self.NS)]
                    idxs = self.sample_indices(les, l)
                    with ExitStack() as ph:
                        W = self.load_phi(ph, l)
                        with ExitStack() as ph2:
                            ckT, cvU = self.phase_C_prompt(pre_p, ph2, W, l)
                            kb.barrier()
                        with ExitStack() as ph2:
                            scomp = pre_s
                            if self.cfg.get("SDBG", 9) >= 1:
                                scomp = self.phase_C_sample(pre_s, ph2, W, l, idxs)
                            kb.barrier()
                    kb.barrier()
                    with ExitStack() as ph:
                        A = self.phaseA_setup(ph, l)
                        with ExitStack() as ph2:
                            if not self.cfg.get("PSKIP"):
                                self.phase_A_prompt(ph2, A, S, l, ckT, cvU, self.d["yp"] if last else self.d["x1p"])
                            kb.barrier()
                        with ExitStack() as ph2:
                            AS = self.phaseA_sample_setup(ph2)
                            for s_ in range(self.NS if self.cfg.get("SDBG", 9) >= 2 else 0):
                                with ExitStack() as ph3:
                                    self.phase_A_sample(ph3, A, AS, S, l, s_, idxs[s_], scomp[s_][0], scomp[s_][1],
                                                        self.d["ys"] if last else self.d["x1s"], self.d["xs"] if l == 0 else self.d["x1s"])
                                    kb.barrier()
                    kb.barrier()
                    if stop == "A%d" % l:
                        break
                kb.barrier()
            kb.barrier()
            if self.cfg.get("SLOW"):
                with ExitStack() as ph:
                    big = c.sb(ph, "slowbig", [128, 4096], F32)
                    kb.op("gpsimd", lambda e: e.memset(big[:], 0.0), writes=[big.name])
                    for _ in range(int(self.cfg["SLOW"])):
                        c.act(big[:], big[:], AF.Exp, [big.name], [big.name], scale=-1.0)
                    kb.barrier()
            print("instructions:", kb.ninstr, flush=True)
        return self.nc


def _host_constants(cfg):
    dist = np.arange(VL) - VOFF
    n = np.maximum(dist, 0)
    nf = np.maximum(n, 1).astype(np.float32)
    large = 16 + (np.log(nf / np.float32(16)) / np.float32(math.log(128 / 16)) * np.float32(16)).astype(np.int32)
    b = np.where(n < 16, n, np.minimum(large, 31))
    bkt = np.zeros((33, VL), np.float32)
    bkt[b, np.arange(VL)] = 1.0
    bkt[:, dist < 0] = 0.0
    bkt[32, dist < 0] = 1.0

    def ovl(n_cmp, n_slc):
        i = np.arange(n_cmp)[:, None]
        j = np.arange(n_slc)[None, :]
        return sum(((i + u) // 4 == j).astype(np.float32) for u in range(2))

    SEQ, PAST = cfg["SEQ"], cfg["PAST"]
    ncp = SEQ // 16 - 1
    ncs = PAST // 16 - 1
    ovp = np.zeros((256, 64), np.float32)
    o = ovl(ncp, SEQ // 64)
    ovp[:min(ncp, 256), :o.shape[1]] = o[:256, :64]
    ovs = np.zeros((ncs + 1, 132), np.float32)
    o = ovl(ncs, PAST // 64 + 1)
    ovs[:ncs, :o.shape[1]] = o[:, :132]
    return dict(bkt=bkt, ovl_p=ovp, ovl_s=ovs)


_CACHE = {}


def run(inputs, cfg):
    key = tuple(sorted((k, str(v)) for k, v in cfg.items()))
    if key not in _CACHE:
        _CACHE[key] = Builder(cfg).build()
    nc = _CACHE[key]
    SEQ, PAST, NS, NPHYS, DEPTH, BATCH = (cfg[k] for k in ("SEQ", "PAST", "NS", "NPHYS", "DEPTH", "BATCH"))
    f = lambda a: np.ascontiguousarray(np.asarray(a))
    hc = _host_constants(cfg)
    pcmp = f(inputs["cache_nsa_cmp"]).reshape(NPHYS * DEPTH * 128, 256)
    pslc = f(inputs["cache_nsa_slc"]).reshape(NPHYS * DEPTH * 128, 256)
    pmoba = f(inputs["cache_moba"]).reshape(NPHYS * DEPTH * 128 * 2, 512)
    shared = {k: f(inputs[k]) for k in ("w_ada", "b_ada", "w_in", "phi_pos", "phi_w1", "phi_b1", "phi_w2", "phi_b2",
                                        "w_up_a", "w_up_b", "w_out", "ln_g", "ln_b")}
    shared["relb"] = f(inputs["rel_bias"])
    shared.update(hc)
    shared.update(pcmp=pcmp, pslc=pslc, pmoba=pmoba)
    in_maps = []
    for c in range(8):
        b = c % BATCH
        ss = slice(c * NS, (c + 1) * NS)
        m = dict(shared)
        m["xp"] = f(inputs["x_prompt"][b])
        m["xs"] = f(inputs["x_sample"][ss, 0])
        m["wst"] = f(inputs["state_nsa_win"][:, ss]).reshape(DEPTH, NS, 512, 256)
        m["pt"] = f(inputs["page_table"][ss]).astype(np.int32)
        m["c5"] = f(np.concatenate([inputs["c_prompt"][b:b + 1], inputs["c_sample"][ss]], 0))
        in_maps.append(m)
    res = run_bass_kernel_spmd(nc, in_maps, core_ids=list(range(8))).results
    DB = 8 * NS
    yp = np.stack([res[b]["yp"] for b in range(BATCH)], 0)
    ys = np.concatenate([res[c]["ys"] for c in range(8)], 0).reshape(DB, 1, D)
    def pr(nm, h):
        return np.stack([res[b][nm] for b in range(BATCH)], 0).reshape(BATCH, DEPTH, SEQ, 2, h, 64)
    def sm(nm, h):
        return np.concatenate([res[c][nm] for c in range(8)], 0).reshape(DB, DEPTH, 1, 2, h, 64)
    nwp = np.stack([res[b]["nwp"] for b in range(BATCH)], 1).reshape(DEPTH, BATCH, 512, 2, 2, 64)
    nws = np.concatenate([res[c]["nws"] for c in range(8)], 1).reshape(DEPTH, DB, 1, 2, 2, 64)
    global LAST_DBG
    LAST_DBG = res[0].get("dbg")
    return (yp, ys, pr("ncp", 2), sm("ncs", 2), pr("nsp", 2), sm("nss", 2), pr("nmp", 8), sm("nms", 8), nwp, nws)


def kernel(**inputs):
    return run(inputs, dict(DEFAULT_CFG))
```
